# Optimizing a Trainium2 kernel written in Bass

```python
import jax
import jax.numpy as jnp
from jax import lax
import numpy as np

D_MODEL = 1024
BATCH = 4
SEQ = 4096
DEPTH = 2

HGRN_WIDTH = D_MODEL // 2
HGRN_HEAD_DIM = 128
HGRN_HEADS = HGRN_WIDTH // HGRN_HEAD_DIM
HGRN_CHUNK = 64
HGRN_PROJ = 4 * HGRN_WIDTH
RWKV_WIDTH = D_MODEL // 2
RWKV_HEAD_DIM = 64
RWKV_HEADS = RWKV_WIDTH // RWKV_HEAD_DIM
RWKV_DECAY_LORA = 64
RWKV_AAA_LORA = 64
RWKV_GATE_LORA = 128
RWKV_LN_EPS = 64e-5
RWKV_PROJ = 3 * RWKV_WIDTH + RWKV_DECAY_LORA + RWKV_AAA_LORA + RWKV_GATE_LORA
AR_PROJ = HGRN_PROJ + RWKV_PROJ
ATTN_HEADS = 8
ATTN_HEAD_DIM = D_MODEL // ATTN_HEADS
MOBA_BLOCK = 256
MOBA_TOPK = 3
MOBA_QCHUNK = 64
ROPE_THETA = 10000.0
D_FF = 4 * D_MODEL
PLE_DIM = 256
N_EVEN = (DEPTH + 1) // 2
N_ODD = DEPTH // 2
NORM_EPS = 1e-6

kernel_name = 'hybrid_hgrn2_rwkv7_moba_trunk'


def rms_norm(x, g, eps=NORM_EPS):
    xf = x.astype(jnp.float32)
    y = xf * lax.rsqrt(jnp.mean(xf * xf, axis=-1, keepdims=True) + eps)
    return (y * g.astype(jnp.float32)).astype(x.dtype)


def head_layer_norm(y, w, b, eps=RWKV_LN_EPS):
    yf = y.astype(jnp.float32)
    mu = jnp.mean(yf, axis=-1, keepdims=True)
    var = jnp.mean(jnp.square(yf - mu), axis=-1, keepdims=True)
    return (yf - mu) * lax.rsqrt(var + eps) * w + b


def rotary(x, pos):
    half = x.shape[-1] // 2
    inv_freq = jnp.power(ROPE_THETA, -jnp.arange(half, dtype=jnp.float32) / half)
    ang = pos.astype(jnp.float32)[:, None] * inv_freq[None, :]
    cos = jnp.cos(ang)[None, :, None, :]
    sin = jnp.sin(ang)[None, :, None, :]
    xf = x.astype(jnp.float32)
    x1, x2 = xf[..., :half], xf[..., half:]
    return jnp.concatenate([x1 * cos - x2 * sin, x2 * cos + x1 * sin], axis=-1).astype(x.dtype)


def token_shift(u):
    return jnp.pad(u[:, :-1], ((0, 0), (1, 0), (0, 0)))


def hgrn2_chunked(q, k, v, log_f):
    B, H, S, dk = q.shape
    dv = v.shape[-1]
    C = HGRN_CHUNK
    n = S // C

    def to_chunks(t):
        return t.reshape(B, H, n, C, t.shape[-1]).transpose(2, 0, 1, 3, 4)

    causal = jnp.tril(jnp.ones((C, C), dtype=bool))

    def step(state, inp):
        qc, kc, vc, gc = inp
        b = jnp.cumsum(gc, axis=2)
        o_inter = jnp.einsum('bhtk,bhkv->bhtv', qc * jnp.exp(b), state)
        diff = jnp.where(causal[None, None, :, :, None],
                         b[:, :, :, None, :] - b[:, :, None, :, :], -jnp.inf)
        scores = jnp.einsum('bhtk,bhsk,bhtsk->bhts', qc, kc, jnp.exp(diff))
        o_intra = jnp.einsum('bhts,bhsv->bhtv', scores, vc)
        b_last = b[:, :, -1:, :]
        state = (state * jnp.exp(b_last)[:, :, 0, :, None]
                 + jnp.einsum('bhsk,bhsv->bhkv', kc * jnp.exp(b_last - b), vc))
        return state, o_inter + o_intra

    state0 = jnp.zeros((B, H, dk, dv), jnp.float32)
    _, o = lax.scan(step, state0, (to_chunks(q), to_chunks(k), to_chunks(v), to_chunks(log_f)))
    return o.transpose(1, 2, 0, 3, 4).reshape(B, H, S, dv)


def rwkv7_scan(r, decay, k, v, kk, a):
    B, S, H, N = r.shape

    def step(state, inp):
        r_t, w_t, k_t, v_t, kk_t, a_t = inp
        sa = jnp.einsum('bhvk,bhk->bhv', state, -kk_t)
        state = (state * w_t[:, :, None, :]
                 + sa[..., None] * (kk_t * a_t)[:, :, None, :]
                 + v_t[..., None] * k_t[:, :, None, :])
        return state, jnp.einsum('bhvk,bhk->bhv', state, r_t)

    xs = (r.transpose(1, 0, 2, 3), decay.transpose(1, 0, 2, 3), k.transpose(1, 0, 2, 3),
          v.transpose(1, 0, 2, 3), kk.transpose(1, 0, 2, 3), a.transpose(1, 0, 2, 3))
    state0 = jnp.zeros((B, H, N, N), jnp.float32)
    _, y = lax.scan(step, state0, xs)
    return y.transpose(1, 0, 2, 3)


def hgrn_rwkv_mixer(h, w_in, w_out, lb, onorm, mu, w0, w2, a0, a2, g2, k_k, k_a, r_k, ln_w, ln_b):
    B, S, _ = h.shape
    u = (h @ w_in).astype(jnp.float32)
    hq, hf, hi, hg, ur = jnp.split(u, [HGRN_WIDTH, 2 * HGRN_WIDTH, 3 * HGRN_WIDTH, HGRN_PROJ], axis=-1)

    def heads(t, n_heads):
        return t.reshape(B, S, n_heads, -1).transpose(0, 2, 1, 3)

    f = lb + (1.0 - lb) * jax.nn.sigmoid(hf)
    o_a = hgrn2_chunked(heads(jax.nn.silu(hq), HGRN_HEADS), heads(1.0 - f, HGRN_HEADS),
                        heads(hi, HGRN_HEADS), heads(jnp.log(f), HGRN_HEADS))
    o_a = rms_norm(o_a.transpose(0, 2, 1, 3), onorm.reshape(HGRN_HEADS, HGRN_HEAD_DIM))
    o_a = o_a.reshape(B, S, HGRN_WIDTH) * jax.nn.silu(hg)

    ur = ur + (token_shift(ur) - ur) * mu
    r, k, v, wd, ad, gd = jnp.split(
        ur, [RWKV_WIDTH, 2 * RWKV_WIDTH, 3 * RWKV_WIDTH, 3 * RWKV_WIDTH + RWKV_DECAY_LORA,
             3 * RWKV_WIDTH + RWKV_DECAY_LORA + RWKV_AAA_LORA], axis=-1)
    w = -jax.nn.softplus(-(w0 + jnp.tanh(wd) @ w2)) - 0.5
    decay = jnp.exp(-jnp.exp(w))
    a = jax.nn.sigmoid(a0 + ad @ a2)
    g = jax.nn.sigmoid(gd) @ g2
    kk = (k * k_k).reshape(B, S, RWKV_HEADS, RWKV_HEAD_DIM)
    kk = kk / jnp.maximum(jnp.sqrt(jnp.sum(kk * kk, axis=-1, keepdims=True)), 1e-12)
    k = k * (1.0 + (a - 1.0) * k_a)

    def rh(t):
        return t.reshape(B, S, RWKV_HEADS, RWKV_HEAD_DIM)

    r4, k4, v4 = rh(r), rh(k), rh(v)
    y = rwkv7_scan(r4, rh(decay), k4, v4, kk, rh(a))
    y = head_layer_norm(y, ln_w.reshape(RWKV_HEADS, RWKV_HEAD_DIM),
                        ln_b.reshape(RWKV_HEADS, RWKV_HEAD_DIM))
    bonus = jnp.sum(r4 * k4 * r_k.reshape(RWKV_HEADS, RWKV_HEAD_DIM), axis=-1, keepdims=True) * v4
    o_b = (y + bonus).reshape(B, S, RWKV_WIDTH) * g

    o = jnp.concatenate([o_a, o_b], axis=-1).astype(h.dtype)
    return o @ w_out


def moba_attention(q, k, v):
    B, H, S, Dh = q.shape
    nb = -(-S // MOBA_BLOCK)
    pad = nb * MOBA_BLOCK - S
    kb = jnp.pad(k, ((0, 0), (0, 0), (0, pad), (0, 0))).reshape(B, H, nb, MOBA_BLOCK, Dh)
    vb = jnp.pad(v, ((0, 0), (0, 0), (0, pad), (0, 0))).reshape(B, H, nb, MOBA_BLOCK, Dh)
    scale = Dh ** -0.5
    q_blk = jnp.arange(S) // MOBA_BLOCK
    n_sel = min(MOBA_TOPK, nb - 1)
    if n_sel > 0:
        k_mean = jnp.mean(kb.astype(jnp.float32), axis=3)
        gate = jnp.einsum('bhsd,bhnd->bhsn', q.astype(jnp.float32), k_mean)
        past = jnp.arange(nb)[None, :] < q_blk[:, None]
        gate = jnp.where(past, gate, -jnp.inf)
        _, sel = lax.top_k(gate, n_sel)
    else:
        sel = jnp.zeros((B, H, S, 0), jnp.int32)
    valid = jnp.arange(n_sel)[None, :] < q_blk[:, None]

    qc_size = MOBA_QCHUNK
    nc = S // qc_size
    q_chunks = q.reshape(B, H, nc, qc_size, Dh).transpose(2, 0, 1, 3, 4)
    sel_chunks = sel.reshape(B, H, nc, qc_size, n_sel).transpose(2, 0, 1, 3, 4)
    valid_chunks = valid.reshape(nc, qc_size, n_sel)
    bi = jnp.arange(B)[:, None, None]
    hi = jnp.arange(H)[None, :, None]

    def attend_chunk(inp):
        c, qc, sel_c, valid_c = inp
        q_pos = c * qc_size + jnp.arange(qc_size)
        own = (c * qc_size) // MOBA_BLOCK
        k_own = lax.dynamic_index_in_dim(kb, own, axis=2, keepdims=False)
        v_own = lax.dynamic_index_in_dim(vb, own, axis=2, keepdims=False)
        k_pos = own * MOBA_BLOCK + jnp.arange(MOBA_BLOCK)
        s_own = jnp.einsum('bhqd,bhkd->bhqk', qc, k_own).astype(jnp.float32) * scale
        scores = [jnp.where(k_pos[None, :] <= q_pos[:, None], s_own, -jnp.inf)]
        for j in range(n_sel):
            k_sel = kb[bi, hi, sel_c[..., j]]
            s = jnp.einsum('bhqd,bhqkd->bhqk', qc, k_sel).astype(jnp.float32) * scale
            scores.append(jnp.where(valid_c[:, j][:, None], s, -jnp.inf))
        probs = jax.nn.softmax(jnp.concatenate(scores, axis=-1), axis=-1).astype(vb.dtype)
        out = jnp.einsum('bhqk,bhkd->bhqd', probs[..., :MOBA_BLOCK], v_own)
        for j in range(n_sel):
            v_sel = vb[bi, hi, sel_c[..., j]]
            p_j = probs[..., (j + 1) * MOBA_BLOCK:(j + 2) * MOBA_BLOCK]
            out = out + jnp.einsum('bhqk,bhqkd->bhqd', p_j, v_sel)
        return out

    o = lax.map(attend_chunk, (jnp.arange(nc), q_chunks, sel_chunks, valid_chunks))
    return o.transpose(1, 2, 0, 3, 4).reshape(B, H, S, Dh)


def moba_mixer(h, w_qkv, w_o, q_gain, k_gain, pos):
    B, S, D = h.shape
    q, k, v = jnp.split(h @ w_qkv, 3, axis=-1)
    q = q.reshape(B, S, ATTN_HEADS, ATTN_HEAD_DIM)
    k = k.reshape(B, S, ATTN_HEADS, ATTN_HEAD_DIM)
    v = v.reshape(B, S, ATTN_HEADS, ATTN_HEAD_DIM)
    q = rotary(rms_norm(q, q_gain), pos)
    k = rotary(rms_norm(k, k_gain), pos)
    o = moba_attention(q.transpose(0, 2, 1, 3), k.transpose(0, 2, 1, 3), v.transpose(0, 2, 1, 3))
    return o.transpose(0, 2, 1, 3).reshape(B, S, D) @ w_o


def setup_inputs(seed: int = 0) -> dict:
    key = jax.random.key(seed)
    ks = iter(jax.random.split(key, 40))

    def normal(shape, scale):
        return jax.random.normal(next(ks), shape, jnp.float32) * scale

    def gain(shape):
        return 1.0 + normal(shape, 0.05)

    D = D_MODEL
    mix_w = HGRN_WIDTH + RWKV_WIDTH
    return {
        'x': normal((BATCH, SEQ, D), 1.0),
        'p': normal((DEPTH, BATCH, SEQ, PLE_DIM), 1.0),
        'attn_norm': gain((DEPTH, D)),
        'mlp_norm': gain((DEPTH, D)),
        'w_in_ar': normal((N_EVEN, D, AR_PROJ), D ** -0.5),
        'w_out_ar': normal((N_EVEN, mix_w, D), mix_w ** -0.5),
        'hgrn_lb': normal((DEPTH + 1, HGRN_WIDTH), 0.5),
        'hgrn_onorm': gain((N_EVEN, HGRN_WIDTH)),
        'rwkv_mu': jax.random.uniform(next(ks), (N_EVEN, RWKV_PROJ), jnp.float32),
        'rwkv_w0': jax.random.uniform(next(ks), (N_EVEN, RWKV_WIDTH), jnp.float32, -3.0, 1.0),
        'rwkv_w2': normal((N_EVEN, RWKV_DECAY_LORA, RWKV_WIDTH), 0.1 * RWKV_DECAY_LORA ** -0.5),
        'rwkv_a0': normal((N_EVEN, RWKV_WIDTH), 0.5),
        'rwkv_a2': normal((N_EVEN, RWKV_AAA_LORA, RWKV_WIDTH), 0.1 * RWKV_AAA_LORA ** -0.5),
        'rwkv_g2': normal((N_EVEN, RWKV_GATE_LORA, RWKV_WIDTH), RWKV_GATE_LORA ** -0.5),
        'rwkv_kk': 0.85 + normal((N_EVEN, RWKV_WIDTH), 0.05),
        'rwkv_ka': 1.0 + normal((N_EVEN, RWKV_WIDTH), 0.05),
        'rwkv_rk': normal((N_EVEN, RWKV_WIDTH), 0.1),
        'rwkv_ln_w': gain((N_EVEN, RWKV_WIDTH)),
        'rwkv_ln_b': normal((N_EVEN, RWKV_WIDTH), 0.02),
        'w_qkv': normal((N_ODD, D, 3 * D), D ** -0.5),
        'w_o_attn': normal((N_ODD, D, D), D ** -0.5),
        'q_norm': gain((N_ODD, ATTN_HEAD_DIM)),
        'k_norm': gain((N_ODD, ATTN_HEAD_DIM)),
        'w_up': normal((DEPTH, D, D_FF), D ** -0.5),
        'w_down': normal((DEPTH, D_FF, D), D_FF ** -0.5),
        'ple_proj': normal((DEPTH, PLE_DIM, D), PLE_DIM ** -0.5),
        'ple_norm': gain((DEPTH, D)),
        'ple_gate': normal((DEPTH, D, D), D ** -0.5),
    }


def reference(x, p, attn_norm, mlp_norm, w_in_ar, w_out_ar, hgrn_lb, hgrn_onorm,
              rwkv_mu, rwkv_w0, rwkv_w2, rwkv_a0, rwkv_a2, rwkv_g2, rwkv_kk, rwkv_ka, rwkv_rk,
              rwkv_ln_w, rwkv_ln_b, w_qkv, w_o_attn, q_norm, k_norm, w_up, w_down,
              ple_proj, ple_norm, ple_gate):
    S = x.shape[1]
    pos = jnp.arange(S)
    lb_all = jnp.cumsum(jax.nn.softmax(hgrn_lb.astype(jnp.float32), axis=0), axis=0)
    for l in range(DEPTH):
        h = rms_norm(x, attn_norm[l])
        if l % 2 == 0:
            e = l // 2
            mix = hgrn_rwkv_mixer(h, w_in_ar[e], w_out_ar[e], lb_all[l], hgrn_onorm[e],
                                  rwkv_mu[e], rwkv_w0[e], rwkv_w2[e], rwkv_a0[e], rwkv_a2[e],
                                  rwkv_g2[e], rwkv_kk[e], rwkv_ka[e], rwkv_rk[e],
                                  rwkv_ln_w[e], rwkv_ln_b[e])
        else:
            o = l // 2
            mix = moba_mixer(h, w_qkv[o], w_o_attn[o], q_norm[o], k_norm[o], pos)
        x = x + mix.astype(x.dtype)
        h = rms_norm(x, mlp_norm[l])
        x = x + jnp.square(jax.nn.relu(h @ w_up[l])) @ w_down[l]
        ple = rms_norm(p[l] @ ple_proj[l], ple_norm[l])
        x = x + ple * jax.nn.sigmoid(x @ ple_gate[l])
    return x
```

```python
from contextlib import ExitStack

import numpy as np
import ml_dtypes
import concourse.bass as bass
import concourse.mybir as mybir
from concourse.bass_utils import run_bass_kernel_spmd

F32 = mybir.dt.float32
BF16 = mybir.dt.bfloat16
AF = mybir.ActivationFunctionType
ALU = mybir.AluOpType
AX = mybir.AxisListType

ENGS = ["tensor", "vector", "scalar", "gpsimd", "sync"]
SAME_ENGINE_SYNC = True
N_DMA_SEMS = 6
CC_INC = 1


class Tile:
    def __init__(self, h, name):
        self.h = h
        self.name = name
        self.psum = False
        self.w = None
        self.r = {}

    def __getitem__(self, idx):
        return V(self, self.h[idx])

    def ap(self):
        return V(self, self.h.ap() if hasattr(self.h, "ap") and callable(self.h.ap) else self.h[:])


class V:
    def __init__(self, tile, ap):
        self.tile = tile
        self.ap = ap

    def __getitem__(self, idx):
        return V(self.tile, self.ap[idx])

    def re(self, pat, **kw):
        return V(self.tile, self.ap.rearrange(pat, **kw))

    def bc(self, shape):
        return V(self.tile, self.ap.broadcast_to(shape))


def _ap(x):
    return x.ap if isinstance(x, V) else x


class Prog:
    def __init__(self):
        self.nc = bass.Bass("TRN2", target_bir_lowering=False)
        self.es = ExitStack()
        self.ops = {e: [] for e in ENGS}
        self.cnt = {}
        self.seen = {e: {} for e in ENGS}
        self.sems = {}
        self.dma_rr = {"sync": 0, "gpsimd": 0, "scalar": 0}
        self.dma_last = {}
        self.n_ops = 0

    def sem(self, key):
        if key not in self.sems:
            self.sems[key] = self.es.enter_context(self.nc.semaphore(key))
            self.cnt[key] = 0
        return self.sems[key]

    def _uniq(self, name):
        self.uid = getattr(self, "uid", 0) + 1
        return f"{name}_u{self.uid}"

    def sb(self, name, shape, dt, stack=None):
        name = self._uniq(name)
        h = (stack or self.es).enter_context(self.nc.sbuf_tensor(name, list(shape), dt))
        return Tile(h, name)

    def ps(self, name, shape, dt=F32, stack=None):
        name = self._uniq(name)
        h = (stack or self.es).enter_context(self.nc.psum_tensor(name, list(shape), dt))
        t = Tile(h, name)
        t.psum = True
        return t

    def dram(self, name, shape, dt, kind):
        h = self.nc.dram_tensor(name, list(shape), dt, kind=kind)
        return Tile(h.ap(), name)

    def _need(self, eng, waits, tk):
        if tk is None:
            return
        key, val = tk
        if key == eng and not (SAME_ENGINE_SYNC and eng != "tensor"):
            return
        if self.seen[eng].get(key, 0) >= val:
            return
        waits[key] = max(waits.get(key, 0), val)

    def op(self, eng, fn, reads=(), writes=(), dma=False, inc=True, cc=False):
        waits = {}
        rt = []
        wt = []
        for r in reads:
            t = r.tile if isinstance(r, V) else r
            if t is not None and t not in rt:
                rt.append(t)
        for w in writes:
            t = w.tile if isinstance(w, V) else w
            if t is not None and t not in wt:
                wt.append(t)
        for t in rt:
            self._need(eng, waits, t.w)
            if t.psum:
                for k, v in t.r.items():
                    if k != eng:
                        self._need(eng, waits, (k, v))
        for t in wt:
            self._need(eng, waits, t.w)
            for k, v in t.r.items():
                self._need(eng, waits, (k, v))
        if cc:
            key = "cc"
            self.sem(key)
            incv = CC_INC
        elif dma:
            i = self.dma_rr[eng]
            self.dma_rr[eng] = (i + 1) % N_DMA_SEMS
            key = f"d_{eng}_{i}"
            self.sem(key)
            self._need(eng, waits, (key, self.cnt[key]))
            incv = 16
        else:
            key = eng
            self.sem(key)
            incv = 1
        if inc:
            self.cnt[key] += incv
            tk = (key, self.cnt[key])
        else:
            tk = (key, self.cnt[key] + incv)
        for k, v in waits.items():
            self.seen[eng][k] = v
        if not dma and inc:
            self.seen[eng][key] = max(self.seen[eng].get(key, 0), 0)
        wl = [(self.sems[k], v) for k, v in waits.items()]
        semh = self.sems[key]
        self.ops[eng].append((wl, fn, semh if inc else None, incv))
        for t in rt:
            if t not in wt:
                t.r[key] = max(t.r.get(key, 0), tk[1])
        for t in wt:
            t.w = tk
            t.r = {}
        self.n_ops += 1
        return tk

    def wait_ticket(self, eng, tk):
        waits = {}
        self._need(eng, waits, tk)
        if waits:
            for k, v in waits.items():
                self.seen[eng][k] = v
            wl = [(self.sems[k], v) for k, v in waits.items()]
            self.ops[eng].append((wl, None, None, 0))

    def barrier(self):
        snap = dict(self.cnt)
        for e in ENGS:
            for k, v in snap.items():
                if v > 0:
                    self.wait_ticket(e, (k, v)) if k != e else None

    def emit(self):
        nc = self.nc
        with nc.Block() as block:
            def mk(eng_name):
                lst = self.ops[eng_name]

                def body(e):
                    for wl, fn, semh, incv in lst:
                        for s, v in wl:
                            e.wait_ge(s, v)
                        if fn is not None:
                            ins = fn(e)
                            if semh is not None:
                                ins.then_inc(semh, incv)
                return body
            block.tensor(mk("tensor"))
            block.vector(mk("vector"))
            block.scalar(mk("scalar"))
            block.gpsimd(mk("gpsimd"))
            block.sync(mk("sync"))
        self.es.close()
        return nc

    def dma(self, out, in_, eng="sync", **kw):
        o, i = _ap(out), _ap(in_)
        return self.op(eng, lambda e: e.dma_start(out=o, in_=i, **kw),
                       reads=[in_], writes=[out], dma=True)

    def mm(self, out, lhsT, rhs, start=True, stop=True, extra_reads=(), **kw):
        o, l, r = _ap(out), _ap(lhsT), _ap(rhs)
        return self.op("tensor", lambda e: e.matmul(o, l, r, start=start, stop=stop, **kw),
                       reads=[lhsT, rhs] + list(extra_reads), writes=[out], inc=stop)

    def transpose(self, out, in_, ident, **kw):
        o, i, d = _ap(out), _ap(in_), _ap(ident)
        return self.op("tensor", lambda e: e.transpose(o, i, d, **kw),
                       reads=[in_, ident], writes=[out])

    def act(self, out, in_, func, bias=None, scale=None, accum_out=None, eng="scalar"):
        o, i = _ap(out), _ap(in_)
        kw = {}
        reads = [in_]
        if bias is not None:
            kw["bias"] = _ap(bias)
            if isinstance(bias, V):
                reads.append(bias)
        if scale is not None:
            kw["scale"] = _ap(scale)
            if isinstance(scale, V):
                reads.append(scale)
        writes = [out]
        if accum_out is not None:
            kw["accum_out"] = _ap(accum_out)
            writes.append(accum_out)
        return self.op(eng, lambda e: e.activation(o, i, func, **kw), reads=reads, writes=writes)

    def tt(self, out, in0, in1, op, eng="vector"):
        o, a, b = _ap(out), _ap(in0), _ap(in1)
        return self.op(eng, lambda e: e.tensor_tensor(o, a, b, op), reads=[in0, in1], writes=[out])

    def ts(self, out, in0, s1, op0, s2=None, op1=None, eng="vector", accum_out=None):
        o, a = _ap(out), _ap(in0)
        reads = [in0]
        for s in (s1, s2):
            if isinstance(s, V):
                reads.append(s)
        x1, x2 = _ap(s1), _ap(s2)
        kw = {}
        writes = [out]
        if op1 is not None:
            kw["op1"] = op1
        if accum_out is not None:
            kw["accum_out"] = _ap(accum_out)
            writes.append(accum_out)
        return self.op(eng, lambda e: e.tensor_scalar(o, a, x1, x2, op0, **kw), reads=reads, writes=writes)

    def stt(self, out, in0, scalar, in1, op0, op1, eng="vector"):
        o, a, b = _ap(out), _ap(in0), _ap(in1)
        reads = [in0, in1]
        if isinstance(scalar, V):
            reads.append(scalar)
        s = _ap(scalar)
        return self.op(eng, lambda e: e.scalar_tensor_tensor(o, a, s, b, op0, op1), reads=reads, writes=[out])

    def copy(self, out, in_, eng="vector"):
        o, i = _ap(out), _ap(in_)
        if eng == "scalar":
            return self.op(eng, lambda e: e.copy(o, i), reads=[in_], writes=[out])
        return self.op(eng, lambda e: e.tensor_copy(o, i), reads=[in_], writes=[out])

    def memset(self, out, val, eng="vector"):
        o = _ap(out)
        return self.op(eng, lambda e: e.memset(o, val), reads=[], writes=[out])

    def reduce(self, out, in_, op, axis, eng="vector"):
        o, i = _ap(out), _ap(in_)
        return self.op(eng, lambda e: e.tensor_reduce(o, i, axis, op), reads=[in_], writes=[out])


EPS = 1e-6


def wview(w, r0, kc, c0, ncols):
    return V(w, w.h[r0:r0 + 128 * kc, c0:c0 + ncols].rearrange("(c p) n -> p c n", p=128))


def tview(a, kc, t0, nt, r0=0):
    return V(a, a.h[r0:r0 + 128 * kc, t0:t0 + nt].rearrange("(c p) t -> p c t", p=128))


def rms_rstd(P, pn, rstd, n_feat, eps=EPS):
    P.act(rstd, pn, AF.Ln, scale=1.0 / n_feat, bias=eps)
    P.act(rstd, rstd, AF.Exp, scale=-0.5)


def dense_phase(P, T, xT, oT, pT, wout, wup, wdown, wple, wgate, gmT, gpT, yT,
                o_gather=None, sel_d=None, h_out=None, ga_next_d=None):
    NT = T // 512
    outer = ExitStack()
    X = [P.sb(f"X{t}", [128, 8, 512], F32, outer) for t in range(NT)]
    HT = [P.sb(f"HT{t}", [128, 8, 512], BF16, outer) for t in range(NT)]
    ones = P.sb("ones", [128, 128], BF16, outer)
    gm = P.sb("gm", [128, 8], F32, outer)
    gp = P.sb("gp", [128, 8], F32, outer)
    gn = P.sb("gn", [128, 8], F32, outer)
    sq = [P.sb(f"sq{i}", [128, 8, 512], BF16, outer) for i in range(1)]
    rstd = [P.sb(f"rstd{i}", [128, 512], F32, outer) for i in range(2)]
    tmp = [P.sb(f"tmp{i}", [128, 512], F32, outer) for i in range(2)]
    wo = P.sb("wo", [128, 8, 1024], BF16, outer)
    pa = [P.ps(f"pa{i}", [128, 512], F32, outer) for i in range(4)]
    pn = [P.ps(f"pn{i}", [128, 512], F32, outer) for i in range(2)]
    P.memset(ones[:], 1.0)
    P.dma(gm[:], gmT[:])
    P.dma(gp[:], gpT[:])
    P.dma(wo[:], wview(wout, 0, 8, 0, 1024), eng="gpsimd")
    for t in range(NT):
        P.dma(X[t][:], tview(xT, 8, t * 512, 512))
        if o_gather is None:
            P.dma(HT[t][:], tview(oT, 8, t * 512, 512))
    if o_gather is not None:
        o_all, rowmap = o_gather
        with ExitStack() as s0:
            sel = P.sb("sel", [128, 2], F32, s0)
            Ab = [[P.sb(f"Ab{s_}{i}", [128, 8, 512], BF16, s0) for i in range(2)] for s_ in range(2)]
            P.dma(sel[:], sel_d[:])
            for t in range(NT):
                for s_ in range(2):
                    a = Ab[s_][t % 2]
                    for j in range(4):
                        r0 = rowmap[2 * j]
                        c0 = t * 512
                        oa = o_all[s_]
                        P.dma(a[:, 2 * j:2 * j + 2, :],
                              V(oa, oa.h[r0:r0 + 256, c0:c0 + 512].rearrange("(c p) t -> p c t", p=128)))
                a0, a1 = Ab[0][t % 2], Ab[1][t % 2]
                P.ts(a0[:], a0[:], sel[:, 0:1], ALU.mult)
                P.stt(HT[t][:], a1[:], sel[:, 1:2], a0[:], ALU.mult, ALU.add)
            P.barrier()
    pi = [0]

    def nextpa():
        pi[0] = (pi[0] + 1) % len(pa)
        return pa[pi[0]]

    for m in range(8):
        for t in range(NT):
            acc = nextpa()
            for kc in range(8):
                P.mm(acc[:], wo[:, kc, m * 128:(m + 1) * 128], HT[t][:, kc, :], start=kc == 0, stop=kc == 7)
            P.tt(X[t][:, m, :], X[t][:, m, :], acc[:], ALU.add)

    def rmsnorm_to(dst, src, g, t):
        s = sq[0]
        P.act(s[:], src[:], AF.Square)
        n = pn[t % 2]
        for c in range(8):
            P.mm(n[:], ones[:], s[:, c, :], start=c == 0, stop=c == 7)
        r = rstd[t % 2]
        rms_rstd(P, r[:], n[:], 1024.0) if False else rms_rstd(P, n[:], r[:], 1024.0)
        return r

    for t in range(NT):
        r = rmsnorm_to(None, X[t], gm, t)
        for c in range(8):
            P.stt(HT[t][:, c, :], X[t][:, c, :], gm[:, c:c + 1], r[:], ALU.mult, ALU.mult)

    with ExitStack() as s1:
        A = [P.sb(f"A{t}", [128, 4, 512], BF16, s1) for t in range(NT)]
        wu = [P.sb(f"wu{i}", [128, 8, 512], BF16, s1) for i in range(2)]
        wd = [P.sb(f"wd{i}", [128, 4, 1024], BF16, s1) for i in range(2)]
        for e in range(8):
            P.dma(wu[e % 2][:], wview(wup, 0, 8, e * 512, 512), eng="gpsimd")
            P.dma(wd[e % 2][:], wview(wdown, e * 512, 4, 0, 1024), eng="gpsimd")
            for f in range(4):
                for t in range(NT):
                    acc = nextpa()
                    for kc in range(8):
                        P.mm(acc[:], wu[e % 2][:, kc, f * 128:(f + 1) * 128], HT[t][:, kc, :],
                             start=kc == 0, stop=kc == 7)
                    tm = tmp[(f * NT + t) % 2]
                    P.act(tm[:], acc[:], AF.Square)
                    P.stt(A[t][:, f, :], acc[:], 0.0, tm[:], ALU.is_gt, ALU.mult)
            for m in range(8):
                for t in range(NT):
                    acc = nextpa()
                    for f in range(4):
                        P.mm(acc[:], wd[e % 2][:, f, m * 128:(m + 1) * 128], A[t][:, f, :],
                             start=f == 0, stop=f == 3)
                    P.tt(X[t][:, m, :], X[t][:, m, :], acc[:], ALU.add)
        P.barrier()

    with ExitStack() as s2:
        PT = P.sb("PT", [128, 2, T], BF16, s2)
        wp = P.sb("wp", [128, 2, 1024], BF16, s2)
        PP = P.sb("PP", [128, 8, 512], F32, s2)
        P.dma(PT[:], tview(pT, 2, 0, T), eng="gpsimd")
        P.dma(wp[:], wview(wple, 0, 2, 0, 1024), eng="gpsimd")
        P.dma(wo[:], wview(wgate, 0, 8, 0, 1024), eng="gpsimd")
        for t in range(NT):
            P.copy(HT[t][:], X[t][:], eng="scalar")
        for t in range(NT):
            for m in range(8):
                acc = nextpa()
                for kc in range(2):
                    P.mm(acc[:], wp[:, kc, m * 128:(m + 1) * 128], PT[:, kc, t * 512:(t + 1) * 512],
                         start=kc == 0, stop=kc == 1)
                P.copy(PP[:, m, :], acc[:], eng="scalar")
            r = rmsnorm_to(None, PP, gp, t)
            for m in range(8):
                acc = nextpa()
                for kc in range(8):
                    P.mm(acc[:], wo[:, kc, m * 128:(m + 1) * 128], HT[t][:, kc, :], start=kc == 0, stop=kc == 7)
                tm = tmp[m % 2]
                P.act(tm[:], acc[:], AF.Sigmoid)
                P.stt(PP[:, m, :], PP[:, m, :], gp[:, m:m + 1], r[:], ALU.mult, ALU.mult)
                P.tt(tm[:], tm[:], PP[:, m, :], ALU.mult)
                P.tt(X[t][:, m, :], X[t][:, m, :], tm[:], ALU.add)
            P.dma(tview(yT, 8, t * 512, 512), X[t][:])
            if h_out is not None:
                if t == 0:
                    P.dma(gn[:], ga_next_d[:])
                r = rmsnorm_to(None, X[t], gn, t)
                for c in range(8):
                    P.stt(HT[t][:, c, :], X[t][:, c, :], gn[:, c:c + 1], r[:], ALU.mult, ALU.mult)
                for f_ in range(2):
                    P.dma(tview(h_out[f_], 4, t * 512, 512), HT[t][:, 4 * f_:4 * f_ + 4, :])
        P.barrier()
    outer.close()


def build_dense(T):
    P = Prog()
    xT = P.dram("xT", [1024, T], F32, "ExternalInput")
    oT = P.dram("oT", [1024, T], BF16, "ExternalInput")
    pT = P.dram("pT", [256, T], F32, "ExternalInput")
    wout = P.dram("wout", [1024, 1024], F32, "ExternalInput")
    wup = P.dram("wup", [1024, 4096], F32, "ExternalInput")
    wdown = P.dram("wdown", [4096, 1024], F32, "ExternalInput")
    wple = P.dram("wple", [256, 1024], F32, "ExternalInput")
    wgate = P.dram("wgate", [1024, 1024], F32, "ExternalInput")
    gmT = P.dram("gmT", [128, 8], F32, "ExternalInput")
    gpT = P.dram("gpT", [128, 8], F32, "ExternalInput")
    yT = P.dram("yT", [1024, T], F32, "ExternalOutput")
    dense_phase(P, T, xT, oT, pT, wout, wup, wdown, wple, wgate, gmT, gpT, yT)
    P.wait_ticket("sync", yT.w)
    P.barrier()
    return P.emit()


def vecT(v):
    return np.ascontiguousarray(v.reshape(-1, 128).T)


S_LEN = 4096
NEG = -100.0
DBG = {}


def moba_consts():
    half = 64
    inv = np.power(10000.0, -np.arange(half, dtype=np.float32) / half).astype(np.float32)
    ang = np.arange(S_LEN, dtype=np.float32)[:, None] * inv[None, :]
    cos = np.cos(ang).astype(np.float32).T
    sin = np.sin(ang).astype(np.float32).T
    cosT = np.concatenate([cos, cos], 0)
    sinT = np.concatenate([-sin, sin], 0)
    perm = np.zeros((128, 128), np.float32)
    for dd in range(128):
        perm[(dd + 64) % 128, dd] = 1.0
    ident = np.eye(128, dtype=np.float32)
    cb = np.zeros((128, 2, 256), np.float32)
    for kt in range(2):
        k = kt * 128 + np.arange(128)[:, None]
        q = np.arange(256)[None, :]
        cb[:, kt, :] = np.where(k > q, NEG, 0.0)
    esel = np.zeros((16, 16, 128), np.float32)
    for n in range(16):
        esel[n, n, :] = 1.0
    pb = np.zeros((128, 16, 16), np.float32)
    for i in range(16):
        pb[:, i, i:] = -1e30
    bf = ml_dtypes.bfloat16
    return dict(cosT=np.ascontiguousarray(cosT), sinT=np.ascontiguousarray(sinT), perm=perm.astype(bf),
                ident=ident.astype(bf), cb=cb.reshape(128, 512).astype(bf),
                esel=esel.reshape(16, 2048).astype(bf), pb=pb.reshape(128, 256))


def moba_phase(P, xT, wq, wk, wv, gaT, qn2, kn2, cosT, sinT, perm_d, ident_d, cb_d, esel_d, pb_d, oT, h_all=None):
    S = S_LEN
    NT = S // 512
    outer = ExitStack()
    QR = [P.sb(f"QR{h}", [128, S], BF16, outer) for h in range(4)]
    KR = [P.sb(f"KR{h}", [128, S], BF16, outer) for h in range(4)]
    VP = P.sb("VP", [128, 32, 4, 130], BF16, outer)
    ones = P.sb("ones", [128, 128], BF16, outer)
    ident = P.sb("ident", [128, 128], BF16, outer)
    kmT = P.sb("kmT", [128, 4, 16], BF16, outer)
    P.memset(ones[:], 1.0)
    P.memset(VP[:], 1.0)
    P.dma(ident[:], ident_d[:])
    with ExitStack() as s1:
        ga = P.sb("ga", [128, 8], F32, s1)
        qk = P.sb("qk", [128, 4], F32, s1)
        perm = P.sb("perm", [128, 128], BF16, s1)
        Xt = [P.sb(f"Xt{i}", [128, 8, 512], F32, s1) for i in range(2)]
        Ht = [P.sb(f"Ht{i}", [128, 8, 512], BF16, s1) for i in range(2)]
        cs = [P.sb(f"cs{i}", [128, 2, 512], F32, s1) for i in range(2)]
        sq = P.sb("sq", [128, 8, 512], BF16, s1)
        rstd = P.sb("rstd", [128, 512], F32, s1)
        w3 = [P.sb(f"w3{i}", [128, 8, 512], BF16, s1) for i in range(3)]
        kb = [P.sb(f"kb{i}", [128, 512], BF16, s1) for i in range(2)]
        sk = [P.sb(f"sk{i}", [128, 512], BF16, s1) for i in range(2)]
        r2 = [P.sb(f"r2{i}", [128, 512], F32, s1) for i in range(2)]
        t1 = [P.sb(f"t1{i}", [128, 512], F32, s1) for i in range(2)]
        t2 = [P.sb(f"t2{i}", [128, 512], F32, s1) for i in range(2)]
        km32 = P.sb("km32", [128, 16], F32, s1)
        pk = [P.ps(f"pk{i}", [128, 512], F32, s1) for i in range(2)]
        pp = [P.ps(f"pp{i}", [128, 512], F32, s1) for i in range(2)]
        pn = [P.ps(f"pn{i}", [128, 512], F32, s1) for i in range(2)]
        pv = [P.ps(f"pv{i}", [128, 512], F32, s1) for i in range(2)]
        P.dma(ga[:], gaT[:])
        P.dma(qk[:, 0:2], qn2[:])
        P.dma(qk[:, 2:4], kn2[:])
        P.dma(perm[:], perm_d[:])
        P.ts(qk[:, 0:2], qk[:, 0:2], float(128 ** -0.5), ALU.mult)
        for i, w in enumerate((wq, wk, wv)):
            P.dma(w3[i][:], wview(w, 0, 8, 0, 512), eng="gpsimd")
        cnt = 0
        for t in range(NT):
            X = Xt[t % 2]
            H = Ht[t % 2]
            C = cs[t % 2]
            P.dma(C[:, 0, :], V(cosT, cosT.h[:, t * 512:(t + 1) * 512]))
            P.dma(C[:, 1, :], V(sinT, sinT.h[:, t * 512:(t + 1) * 512]))
            if h_all is not None:
                rk_, c0_ = t // 4, (t % 4) * 512
                for f_ in range(2):
                    P.dma(H[:, 4 * f_:4 * f_ + 4, :],
                          V(h_all[f_], h_all[f_].h[rk_ * 512:(rk_ + 1) * 512, c0_:c0_ + 512].rearrange("(c p) t -> p c t", p=128)))
            else:
                P.dma(X[:], tview(xT, 8, t * 512, 512))
                P.act(sq[:], X[:], AF.Square)
                n = pn[0]
                for c in range(8):
                    P.mm(n[:], ones[:], sq[:, c, :], start=c == 0, stop=c == 7)
                rms_rstd(P, n[:], rstd[:], 1024.0)
                for c in range(8):
                    P.stt(H[:, c, :], X[:, c, :], ga[:, c:c + 1], rstd[:], ALU.mult, ALU.mult)
            for h in range(4):
                for which in range(2):
                    w = w3[which]
                    dst = (QR if which == 0 else KR)[h]
                    g0 = qk[:, 2 * which:2 * which + 1]
                    g1 = qk[:, 2 * which + 1:2 * which + 2]
                    j = cnt % 2
                    cnt += 1
                    a = pk[j]
                    for kc in range(8):
                        P.mm(a[:], w[:, kc, h * 128:(h + 1) * 128], H[:, kc, :], start=kc == 0, stop=kc == 7)
                    P.copy(kb[j][:], a[:], eng="scalar")
                    P.act(sk[j][:], a[:], AF.Square)
                    P.mm(pp[j][:], perm[:], kb[j][:])
                    P.mm(pn[1][:], ones[:], sk[j][:])
                    rms_rstd(P, pn[1][:], r2[j][:], 128.0)
                    P.stt(t1[j][:], a[:], g0, C[:, 0, :], ALU.mult, ALU.mult)
                    P.stt(t2[j][:], pp[j][:], g1, C[:, 1, :], ALU.mult, ALU.mult)
                    P.tt(t1[j][:], t1[j][:], t2[j][:], ALU.add, eng="gpsimd")
                    P.tt(dst[:, t * 512:(t + 1) * 512], t1[j][:], r2[j][:], ALU.mult, eng="gpsimd")
            for sub in range(4):
                a = pv[sub % 2]
                for kc in range(8):
                    P.mm(a[:], H[:, kc, sub * 128:(sub + 1) * 128], w3[2][:, kc, :], start=kc == 0, stop=kc == 7)
                P.copy(VP[:, t * 4 + sub, :, 0:128], a[:].re("p (h d) -> p h d", h=4), eng="vector")
        for h in range(4):
            P.reduce(km32[:], KR[h][:].re("p (n j) -> p n j", j=256), ALU.add, AX.X)
            P.ts(kmT[:, h, :], km32[:], 1.0 / 256.0, ALU.mult)
        P.barrier()
    if DBG.get("moba_stop") == "A":
        outer.close()
        return
    with ExitStack() as s2:
        cb = P.sb("cb", [128, 2, 256], BF16, s2)
        esel = P.sb("esel", [16, 16, 128], BF16, s2)
        pb = P.sb("pb", [128, 16, 16], F32, s2)
        SBT = P.sb("SBT", [16, 4, S], BF16, s2)
        OT = [P.sb(f"OT{i}", [128, S], BF16, s2) for i in range(2)]
        PT = [P.sb(f"PT{i}", [128, 2, 256], BF16, s2) for i in range(3)]
        gm = P.sb("gm", [128, 32, 16], F32, s2)
        m8 = P.sb("m8", [128, 32, 8], F32, s2)
        sbq = P.sb("sbq", [128, 32, 16], BF16, s2)
        rec = [P.sb(f"rec{i}", [128, 1], F32, s2) for i in range(4)]
        on = [P.sb(f"on{i}", [128, 128], BF16, s2) for i in range(4)]
        pS = [P.ps(f"pS{i}", [128, 2, 256], F32, s2) for i in range(2)]
        pO = [P.ps(f"pO{i}", [128, 512], F32, s2) for i in range(4)]
        pg = P.ps("pg", [128, 32, 16], F32, s2)
        ptr = P.ps("ptr", [128, 1024], BF16, s2)
        P.dma(cb[:], V(cb_d, cb_d.h[:, :].rearrange("p (k q) -> p k q", k=2)))
        P.dma(esel[:], V(esel_d, esel_d.h[:, :].rearrange("p (n j) -> p n j", n=16)))
        P.dma(pb[:], V(pb_d, pb_d.h[:, :].rearrange("p (i n) -> p i n", i=16)))
        for h in range(4):
            for qt in range(32):
                P.mm(pg[:, qt, :], QR[h][:, qt * 128:(qt + 1) * 128], kmT[:, h, :])
            gm4 = gm[:].re("p (i two) n -> p i two n", two=2)
            pg4 = pg[:].re("p (i two) n -> p i two n", two=2)
            for two in range(2):
                P.tt(gm4[:, :, two, :], pg4[:, :, two, :], pb[:], ALU.add)
            for qt in range(32):
                P.op("vector", (lambda qt: lambda e: e.max(out=m8.h[:, qt, :], in_=gm.h[:, qt, :]))(qt),
                     reads=[gm], writes=[m8])
            P.tt(sbq[:], gm[:], V(m8, m8.h[:, :, 2:3].broadcast_to([128, 32, 16])), ALU.is_lt)
            P.ts(sbq[:], sbq[:], NEG, ALU.mult, eng="gpsimd")
            for rnd in range(4):
                for j in range(8):
                    qt = rnd * 8 + j
                    P.transpose(ptr[0:16, j * 128:(j + 1) * 128], sbq[:, qt, :], ident[:])
                P.copy(SBT[:, h, rnd * 1024:(rnd + 1) * 1024], ptr[0:16, :], eng="scalar")
        for h in range(4):
            ot = OT[h % 2]
            its = [(i, n) for i in range(DBG.get("moba_nblk", 16)) for n in range(i + 1)]
            pend = []

            def emit_S(idx):
                i, n = its[idx]
                q0 = i * 256
                ps_ = pS[idx % 2]
                for kt in range(2):
                    k0 = (n * 2 + kt) * 128
                    P.mm(ps_[:, kt, :], KR[h][:, k0:k0 + 128], QR[h][:, q0:q0 + 256], start=True, stop=False)
                    if n < i:
                        P.mm(ps_[:, kt, :], esel[:, n, :], SBT[:, h, q0:q0 + 256], start=False, stop=True)
                    else:
                        P.mm(ps_[:, kt, :], ident[:], cb[:, kt, :], start=False, stop=True)
                P.act(PT[idx % 3][:], ps_[:], AF.Exp)

            def emit_PV(idx):
                i, n = its[idx]
                q0 = i * 256
                pt = PT[idx % 3]
                po = [pO[(i % 2) * 2 + qs] for qs in range(2)]
                for qs in range(2):
                    for kt in range(2):
                        P.mm(po[qs][:, 0:129], pt[:, kt, qs * 128:(qs + 1) * 128], VP[:, n * 2 + kt, h, 0:129],
                             start=(n == 0 and kt == 0), stop=(n == i and kt == 1))
                if n == i:
                    for qs in range(2):
                        k_ = (i % 2) * 2 + qs
                        P.op("vector", (lambda r, p_: lambda e: e.reciprocal(r.h[:], p_.h[:, 128:129]))(rec[k_], po[qs]),
                             reads=[po[qs]], writes=[rec[k_]])
                        P.ts(on[k_][:], po[qs][:, 0:128], rec[k_][:, 0:1], ALU.mult)
                        pend.append((idx + 2, k_, q0 + qs * 128))

            def flush(idx, force=False):
                while pend and (force or pend[0][0] <= idx):
                    _, k_, c0 = pend.pop(0)
                    cc = 256 + (k_ % 2) * 128
                    P.transpose(ptr[:, cc:cc + 128], on[k_][:], ident[:])
                    P.copy(ot[:, c0:c0 + 128], ptr[:, cc:cc + 128], eng="scalar")

            for idx in range(len(its) + 1):
                if idx < len(its):
                    emit_S(idx)
                if idx >= 1:
                    emit_PV(idx - 1)
                flush(idx)
            flush(0, force=True)
            for s_ in range(2):
                P.dma(oT(h * 128, (h + 1) * 128, s_), ot[:, s_ * 2048:(s_ + 1) * 2048])
        P.barrier()
    outer.close()


def build_moba():
    P = Prog()
    S = S_LEN
    xT = P.dram("xT", [1024, S], F32, "ExternalInput")
    wq = P.dram("wq", [1024, 512], F32, "ExternalInput")
    wk = P.dram("wk", [1024, 512], F32, "ExternalInput")
    wv = P.dram("wv", [1024, 512], F32, "ExternalInput")
    gaT = P.dram("gaT", [128, 8], F32, "ExternalInput")
    qn2 = P.dram("qn2", [128, 2], F32, "ExternalInput")
    kn2 = P.dram("kn2", [128, 2], F32, "ExternalInput")
    cosT = P.dram("cosT", [128, S], F32, "ExternalInput")
    sinT = P.dram("sinT", [128, S], F32, "ExternalInput")
    perm = P.dram("perm", [128, 128], BF16, "ExternalInput")
    ident = P.dram("ident", [128, 128], BF16, "ExternalInput")
    cb = P.dram("cb", [128, 512], BF16, "ExternalInput")
    esel = P.dram("esel", [16, 2048], BF16, "ExternalInput")
    pb = P.dram("pb", [128, 256], F32, "ExternalInput")
    oT = P.dram("oT", [512, S], BF16, "ExternalOutput")
    moba_phase(P, xT, wq, wk, wv, gaT, qn2, kn2, cosT, sinT, perm, ident, cb, esel, pb,
               lambda r0, r1, s_: V(oT, oT.h[r0:r1, s_ * 2048:(s_ + 1) * 2048]))
    P.wait_ticket("sync", oT.w)
    P.barrier()
    return P.emit()


def moba_inputs(x1T_b, hh, d):
    c = moba_consts()
    wqkv = d["w_qkv"][0]
    qn = d["q_norm"][0]
    kn = d["k_norm"][0]
    pidx = (np.arange(128) + 64) % 128
    ins = dict(wq=np.ascontiguousarray(wqkv[:, hh * 512:(hh + 1) * 512]),
               wk=np.ascontiguousarray(wqkv[:, 1024 + hh * 512:1024 + (hh + 1) * 512]),
               wv=np.ascontiguousarray(wqkv[:, 2048 + hh * 512:2048 + (hh + 1) * 512]),
               gaT=vecT(d["attn_norm"][1]),
               qn2=np.ascontiguousarray(np.stack([qn, qn[pidx]], 1)),
               kn2=np.ascontiguousarray(np.stack([kn, kn[pidx]], 1)))
    if x1T_b is not None:
        ins["xT"] = x1T_b
    ins.update(c)
    return ins


CH = 64
RW_LN_EPS = 64e-5


def ar_consts():
    bf = ml_dtypes.bfloat16
    s = np.arange(64)[:, None]
    t = np.arange(64)[None, :]
    strictT = (s < t).astype(np.float32)
    inclT = (s <= t).astype(np.float32)
    strict = (t < s).astype(np.float32)
    m = np.concatenate([strictT, inclT, strictT, inclT, strict], 1)
    mask = np.concatenate([m, m], 0)
    i64 = np.concatenate([np.eye(64), np.eye(64)], 0).astype(np.float32)
    bones = np.zeros((128, 128), np.float32)
    bones[:64, :64] = 1
    bones[64:, 64:] = 1
    rm = np.ones((128, 512), np.float32)
    rm[:, ::64] = 0
    return dict(mask=mask.astype(np.float32), i64=i64, bones=bones.astype(bf),
                ident=np.eye(128, dtype=np.float32).astype(bf), rm=rm)


def ar_phase(P, xT, whg, wrw, w2a2_d, g2_d, gaT, lb3_d, hv_d, rv_d, mu8_d, mask_d, i64_d, bones_d, ident_d, rm_d, oT):
    S = S_LEN
    NT = S // 512
    NC = 512 // CH
    outer = ExitStack()
    sb = lambda n, sh, dt: P.sb(n, sh, dt, outer)
    ones = sb("ones", [128, 128], BF16)
    bones = sb("bones", [128, 128], BF16)
    ident = sb("ident", [128, 128], BF16)
    mask = sb("mask", [128, 320], F32)
    i64 = sb("i64", [128, 64], F32)
    rm = sb("rm", [128, 512], F32)
    ga = sb("ga", [128, 8], F32)
    lb3 = sb("lb3", [128, 2, 3], F32)
    lbv = sb("lbv", [128, 2, 4], F32)
    hv = sb("hv", [128, 2], F32)
    rv = sb("rv", [128, 2, 8], F32)
    mu8 = sb("mu8", [128, 2, 8], F32)
    whgs = sb("whgs", [128, 8, 1024], BF16)
    wrws = sb("wrws", [128, 8, 1024], BF16)
    w2a2 = sb("w2a2", [128, 256], BF16)
    g2 = sb("g2", [128, 256], BF16)
    Ucar = sb("Ucar", [128, 8, 516], F32)
    Hb = [sb(f"Hb{i}", [128, 2, 64], BF16) for i in range(2)]
    Hg = sb("Hg", [128, 2, 64], BF16)
    Sb = [sb(f"Sb{i}", [128, 2, 128], BF16) for i in range(2)]
    Sg = sb("Sg", [128, 2, 128], BF16)
    P.memset(ones[:], 1.0)
    P.memset(Ucar[:], 0.0)
    for t_ in Hb + Sb:
        P.memset(t_[:], 0.0)
    for dst, src in ((bones, bones_d), (ident, ident_d), (mask, mask_d), (i64, i64_d), (rm, rm_d), (ga, gaT)):
        P.dma(dst[:], src[:])
    P.dma(lb3[:], V(lb3_d, lb3_d.h[:, :].rearrange("p (h j) -> p h j", h=2)))
    P.dma(hv[:], hv_d[:])
    P.dma(rv[:, :, 0:7], V(rv_d, rv_d.h[:, :].rearrange("p (h j) -> p h j", h=2)))
    P.dma(mu8[:, 0, :], mu8_d[:])
    P.dma(whgs[:], wview(whg, 0, 8, 0, 1024), eng="gpsimd")
    P.dma(wrws[:], wview(wrw, 0, 8, 0, 1024), eng="gpsimd")
    P.dma(w2a2[:], w2a2_d[:], eng="gpsimd")
    P.dma(g2[:], g2_d[:], eng="gpsimd")
    P.act(lb3[:], lb3[:], AF.Exp)
    P.reduce(lbv[:, :, 2], lb3[:], ALU.add, AX.X)
    P.op("vector", lambda e: e.reciprocal(lbv.h[:, :, 3], lbv.h[:, :, 2]), reads=[lbv], writes=[lbv])
    P.tt(lbv[:, :, 0], lb3[:, :, 0], lbv[:, :, 3], ALU.mult)
    P.ts(lbv[:, :, 1], lbv[:, :, 0], -1.0, ALU.mult, 1.0, ALU.add)
    P.ts(rv[:, :, 7], rv[:, :, 3], -1.0, ALU.mult, 1.0, ALU.add)
    P.ts(mu8[:, 1, :], mu8[:, 0, :], -1.0, ALU.mult, 1.0, ALU.add)

    def f32(n, stack):
        return P.sb(n, [128, 512], F32, stack)

    def b16(n, stack):
        return P.sb(n, [128, 512], BF16, stack)

    for t in range(DBG.get('ar_nt', NT)):
        tile = ExitStack()
        QtT = [b16(f"QtT{h}", tile) for h in range(2)]
        KtT = [b16(f"KtT{h}", tile) for h in range(2)]
        QbT = [b16(f"QbT{h}", tile) for h in range(2)]
        KhT = [b16(f"KhT{h}", tile) for h in range(2)]
        VTh = [b16(f"VTh{h}", tile) for h in range(2)]
        SGt = [b16(f"SGt{h}", tile) for h in range(2)]
        E3h = [f32(f"E3h{h}", tile) for h in range(2)]
        OAt = [f32(f"OAt{h}", tile) for h in range(2)]
        AR = [P.sb(f"AR{p}", [128, NC, 2, CH], BF16, tile) for p in range(2)]
        BT = [b16(f"BT{p}", tile) for p in range(2)]
        KT = [b16(f"KT{p}", tile) for p in range(2)]
        VT = [b16(f"VT{p}", tile) for p in range(2)]
        VF = [f32(f"VF{p}", tile) for p in range(2)]
        RKb = [b16(f"RKb{p}", tile) for p in range(2)]
        GT = [b16(f"GT{p}", tile) for p in range(2)]
        E1 = [f32(f"E1{p}", tile) for p in range(2)]
        YT = [f32(f"YT{p}", tile) for p in range(2)]
        with ExitStack() as sp:
            X = P.sb("X", [128, 8, 512], F32, sp)
            H = P.sb("H", [128, 8, 512], BF16, sp)
            sq = P.sb("sq", [128, 8, 512], BF16, sp)
            rstd = f32("rstd", sp)
            WA = b16("WA", sp)
            sgT = b16("sgT", sp)
            tmp = [f32(f"tp{i}", sp) for i in range(6)]
            tb = [b16(f"tb{i}", sp) for i in range(2)]
            pq = [P.ps(f"pq{i}", [128, 512], F32, sp) for i in range(3)]
            pm = [P.ps(f"pm{i}", [128, 512], F32, sp) for i in range(3)]
            P.dma(X[:], tview(xT, 8, t * 512, 512))
            P.act(sq[:], X[:], AF.Square)
            for c in range(8):
                P.mm(pm[0][:], ones[:], sq[:, c, :], start=c == 0, stop=c == 7)
            rms_rstd(P, pm[0][:], rstd[:], 1024.0)
            for c in range(8):
                P.stt(H[:, c, :], X[:, c, :], ga[:, c:c + 1], rstd[:], ALU.mult, ALU.mult)
            pqi = [0]

            def proj(w, ct):
                a = pq[pqi[0] % 3]
                pqi[0] += 1
                for kc in range(8):
                    P.mm(a[:], w[:, kc, ct * 128:(ct + 1) * 128], H[:, kc, :], start=kc == 0, stop=kc == 7)
                return a

            for h in range(2):
                aq = proj(whgs, 0 + h)
                qs = tmp[0]
                P.act(qs[:], aq[:], AF.Silu)
                af = proj(whgs, 2 + h)
                f = tmp[1]
                P.act(f[:], af[:], AF.Sigmoid)
                P.ts(f[:], f[:], lbv[:, h, 1:2], ALU.mult, lbv[:, h, 0:1], ALU.add)
                lf = tmp[2]
                P.act(lf[:], f[:], AF.Ln)
                kq = tmp[3]
                P.ts(kq[:], f[:], -1.0, ALU.mult, 1.0, ALU.add, eng="gpsimd")
                b = tmp[4]
                P.op("vector", (lambda b, lf: lambda e: e.tensor_tensor_scan(b.h[:], rm.h[:], lf.h[:], 0.0, ALU.mult, ALU.add))(b, lf),
                     reads=[rm, lf], writes=[b])
                b3 = b[:].re("p (c j) -> p c j", j=CH)
                d = tmp[5]
                P.tt(d[:].re("p (c j) -> p c j", j=CH), b3, V(b, b.h[:, :].rearrange("p (c j) -> p c j", j=CH)[:, :, 31:32].broadcast_to([128, NC, CH])), ALU.subtract)
                e1 = tmp[2]
                P.act(e1[:], d[:], AF.Exp)
                P.tt(QtT[h][:], qs[:], e1[:], ALU.mult, eng="gpsimd")
                P.act(e1[:], d[:], AF.Exp, scale=-1.0)
                P.tt(KtT[h][:], kq[:], e1[:], ALU.mult)
                P.act(E3h[h][:], b[:], AF.Exp)
                P.tt(QbT[h][:], qs[:], E3h[h][:], ALU.mult, eng="gpsimd")
                P.tt(d[:].re("p (c j) -> p c j", j=CH), b3, V(b, b.h[:, :].rearrange("p (c j) -> p c j", j=CH)[:, :, 63:64].broadcast_to([128, NC, CH])), ALU.subtract)
                P.act(e1[:], d[:], AF.Exp, scale=-1.0)
                P.tt(KhT[h][:], kq[:], e1[:], ALU.mult)
                ai = proj(whgs, 4 + h)
                P.copy(VTh[h][:], ai[:], eng="scalar")
                ag = proj(whgs, 6 + h)
                P.act(SGt[h][:], ag[:], AF.Silu)

            def shifted(ct):
                a = proj(wrws, ct)
                U = Ucar[:, ct, :]
                P.copy(U[:, 3:4], U[:, 515:516], eng="gpsimd")
                P.copy(U[:, 4:516], a[:], eng="scalar")
                return U

            def mix(dst, U, ct, eng="vector"):
                P.ts(dst, U[:, 4:516], mu8[:, 1, ct:ct + 1], ALU.mult, eng=eng)
                P.stt(dst, U[:, 3:515], mu8[:, 0, ct:ct + 1], dst, ALU.mult, ALU.add)

            U6 = shifted(6)
            wm = tmp[0]
            mix(wm[:], U6, 6)
            P.act(WA[0:64, :], wm[0:64, :], AF.Tanh)
            P.copy(WA[64:128, :], wm[64:128, :], eng="scalar")
            U7 = shifted(7)
            mix(wm[:], U7, 7)
            P.act(sgT[:], wm[:], AF.Sigmoid)
            for p in range(2):
                rM, kM, kk, a_, t4, t5 = tmp
                mix(rM[:], shifted(0 + p), 0 + p)
                mix(kM[:], shifted(2 + p), 2 + p)
                mix(VF[p][:], shifted(4 + p), 4 + p)
                P.copy(VT[p][:], VF[p][:], eng="gpsimd")
                P.mm(pm[1][:], w2a2[0:64, p * 128:(p + 1) * 128], WA[0:64, :])
                ld = t4
                P.act(ld[:], pm[1][:], AF.Sigmoid, bias=rv[:, p, 0:1])
                P.ts(ld[:], ld[:], -float(np.exp(-0.5)), ALU.mult, eng="gpsimd")
                P.mm(pm[2][:], w2a2[64:128, p * 128:(p + 1) * 128], WA[64:128, :])
                P.act(a_[:], pm[2][:], AF.Sigmoid, bias=rv[:, p, 1:2])
                P.mm(pm[1][:], g2[:, p * 128:(p + 1) * 128], sgT[:])
                P.copy(GT[p][:], pm[1][:], eng="scalar")
                P.ts(kk[:], kM[:], rv[:, p, 2:3], ALU.mult)
                P.act(tb[0][:], kk[:], AF.Square)
                P.mm(pm[2][:], bones[:], tb[0][:])
                P.act(t5[:], pm[2][:], AF.Ln, bias=1e-12)
                P.act(t5[:], t5[:], AF.Exp, scale=-0.5)
                P.tt(kk[:], kk[:], t5[:], ALU.mult)
                P.ts(t5[:], a_[:], rv[:, p, 3:4], ALU.mult, rv[:, p, 7:8], ALU.add)
                P.tt(kM[:], kM[:], t5[:], ALU.mult)
                P.stt(RKb[p][:], rM[:], rv[:, p, 4:5], kM[:], ALU.mult, ALU.mult)
                cs = t5
                P.op("vector", (lambda cs, ld: lambda e: e.tensor_tensor_scan(cs.h[:], rm.h[:], ld.h[:], 0.0, ALU.mult, ALU.add))(cs, ld),
                     reads=[rm, ld], writes=[cs])
                P.act(E1[p][:], cs[:], AF.Exp)
                AR4 = AR[p]
                P.tt(AR4[:, :, 1, :], rM[:].re("p (c j) -> p c j", j=CH), E1[p][:].re("p (c j) -> p c j", j=CH), ALU.mult)
                P.tt(ld[:], cs[:], ld[:], ALU.subtract)
                P.act(ld[:], ld[:], AF.Exp)
                P.stt(AR4[:, :, 0, :], kk[:].re("p (c j) -> p c j", j=CH), -1.0, ld[:].re("p (c j) -> p c j", j=CH), ALU.mult, ALU.mult)
                P.act(cs[:], cs[:], AF.Exp, scale=-1.0)
                P.tt(kk[:], kk[:], a_[:], ALU.mult)
                P.tt(BT[p][:], kk[:], cs[:], ALU.mult)
                P.tt(KT[p][:], kM[:], cs[:], ALU.mult)
            P.barrier()
        hs = [slice(0, 64), slice(64, 128)]
        NG = NC // 4
        keep = ExitStack()
        TOKg = [[P.sb(f"TOKg{p}{g}", [128, 4, 4, 64], BF16, keep) for g in range(NG)] for p in range(2)]
        SCbg = [[P.sb(f"SCbg{p}{g}", [128, 4, 320], BF16, keep) for g in range(NG)] for p in range(2)]
        WhTg = [[P.sb(f"WhTg{p}{g}", [128, 4, 64], BF16, keep) for g in range(NG)] for p in range(2)]
        UHg = [[P.sb(f"UHg{p}{g}", [128, 4, 64], F32, keep) for g in range(NG)] for p in range(2)]
        with ExitStack() as sc:
            bTb = P.ps("bT", [128, 1024], BF16, sc)
            bT = bTb[:].re("p (q j d) -> p q j d", q=4, j=4)
            bA = [P.ps(f"bA{i}", [128, 512], F32, sc) for i in range(2)]
            bN = P.ps("bN", [128, 512], F32, sc)
            bQ = P.ps("bQ", [128, 512], F32, sc)
            bk7bb = P.ps("bk7b", [128, 1024], BF16, sc)
            bk7b = bk7bb[:, 0:256].re("p (a j) -> p a j", a=2)
            bkH = [P.ps(f"bkH{h}", [128, 512], F32, sc) for h in range(2)]
            Tg = [P.sb(f"Tg{i}", [128, 4, 64], BF16, sc) for i in range(2)]
            PQg = [P.sb(f"PQg{i}", [128, 4, 128], BF16, sc) for i in range(2)]
            Zb = P.sb("Zb", [128, 4, 64], BF16, sc)
            TOKH = P.sb("TOKH", [128, 2, 128], BF16, sc)
            AtH = P.sb("AtH", [128, 64], BF16, sc)

            def hgrn_chunk(c):
                o = c * CH
                gcol = o + CH - 1
                gi = t * NC + c
                Sb0, Sb1 = Sb[gi % 2], Sb[(gi + 1) % 2]
                for h in range(2):
                    P.ts(Sg[:, h, :], Sb0[:, h, :], E3h[h][:, gcol:gcol + 1], ALU.mult, eng="gpsimd")
                    P.transpose(bk7b[hs[h], 0, :], KhT[h][:, o:o + CH], ident[:])
                    P.transpose(bk7b[hs[h], 1, :], VTh[h][:, o:o + CH], ident[:])
                P.copy(TOKH[:], bk7b[:], eng="scalar")
                for h in range(2):
                    P.mm(bkH[0][hs[h], 192:256], KtT[h][:, o:o + CH], QtT[h][:, o:o + CH])
                P.tt(AtH[:], bkH[0][:, 192:256], mask[:, 64:128], ALU.mult)
                for h in range(2):
                    P.mm(bkH[h][:, 0:64], TOKH[hs[h], 1, :], AtH[hs[h], :], start=True, stop=False)
                    P.mm(bkH[h][:, 0:64], Sb0[:, h, :], QbT[h][:, o:o + CH], start=False, stop=True)
                for h in range(2):
                    P.copy(OAt[h][:, o:o + CH], bkH[h][:, 0:64], eng="scalar")
                for h in range(2):
                    P.mm(bkH[h][:, 64:192], TOKH[hs[h], 0, :], TOKH[hs[h], 1, :])
                for h in range(2):
                    P.tt(Sb1[:, h, :], bkH[h][:, 64:192], Sg[:, h, :], ALU.add)

            def bc4(v, n):
                return V(v.tile, v.ap.unsqueeze(1).broadcast_to([128, n, v.ap.shape[-1]]))

            def stage1(p, g):
                tok, scb = TOKg[p][g], SCbg[p][g]
                cs_ = [g * 4 + q for q in range(4)]
                aT = [AR[p][:, c, 0, :] for c in cs_]
                arT = [AR[p][:, c, :, :].re("p a j -> p (a j)") for c in cs_]
                bT_ = [BT[p][:, c * CH:(c + 1) * CH] for c in cs_]
                kT_ = [KT[p][:, c * CH:(c + 1) * CH] for c in cs_]
                vT_ = [VT[p][:, c * CH:(c + 1) * CH] for c in cs_]
                for h in range(2):
                    for q in range(4):
                        for j, xx in enumerate((aT[q], bT_[q], kT_[q], vT_[q])):
                            P.transpose(bT[hs[h], q, j, :], xx[hs[h], :], ident[hs[h], hs[h]])
                    for q in range(4):
                        ba = bA[q // 2]
                        o_ = (q % 2) * 256
                        P.mm(ba[hs[h], o_:o_ + 128], bT_[q][hs[h], :], arT[q][hs[h], :])
                        P.mm(ba[hs[h], o_ + 128:o_ + 256], kT_[q][hs[h], :], arT[q][hs[h], :])
                        P.mm(bN[hs[h], q * 64:(q + 1) * 64], aT[q][hs[h], :], bT_[q][hs[h], :])
                P.copy(tok[:], bT, eng="scalar")
                for k in range(2):
                    P.tt(scb[:, 2 * k:2 * k + 2, 0:256], bA[k][:].re("p (q n) -> p q n", q=2), bc4(mask[:, 0:256], 2), ALU.mult)
                P.tt(scb[:, :, 256:320], bN[:, 0:256].re("p (q n) -> p q n", q=4), bc4(mask[:, 256:320], 4), ALU.mult)
                P.tt(Tg[0][:], scb[:, :, 0:64], bc4(i64[:], 4), ALU.add)
                Pm = [scb[:, q, 256:320] for q in range(4)]
                Qm = [scb[:, q, 0:64] for q in range(4)]
                sqb = [bQ, bA[1]]
                tub = [bN, bA[0]]
                v4 = lambda x: x.re("p (q n) -> p q n", q=4)
                for j in range(5):
                    pq_ = PQg[j % 2]
                    for h in range(2):
                        for q in range(4):
                            P.mm(sqb[h][hs[h], q * 128:q * 128 + 64], Qm[q][hs[h], :], Pm[q][hs[h], :])
                            P.mm(sqb[h][hs[h], q * 128 + 64:q * 128 + 128], Pm[q][hs[h], :], Qm[q][hs[h], :])
                    P.copy(pq_[hs[0]], v4(sqb[0][hs[0], :]), eng="scalar")
                    P.copy(pq_[hs[1]], v4(sqb[1][hs[1], :]), eng="vector")
                    Pm = [pq_[:, q, 0:64] for q in range(4)]
                    Qm = [pq_[:, q, 64:128] for q in range(4)]
                    To, Tn = Tg[j % 2], Tg[(j + 1) % 2]
                    for h in range(2):
                        for q in range(4):
                            P.mm(tub[h][hs[h], 256 + q * 64:256 + (q + 1) * 64], Pm[q][hs[h], :], To[hs[h], q, :])
                    for h in range(2):
                        P.tt(Tn[hs[h]], v4(tub[h][hs[h], 256:512]), To[hs[h]], ALU.add)
                Tf = Tg[5 % 2]
                for h in range(2):
                    for q in range(4):
                        P.mm(tub[h][hs[h], q * 64:(q + 1) * 64], scb[hs[h], q, 128:192], tok[hs[h], q, 3, :])
                P.copy(Zb[hs[0]], v4(tub[0][hs[0], 0:256]), eng="scalar")
                P.copy(Zb[hs[1]], v4(tub[1][hs[1], 0:256]), eng="vector")
                for h in range(2):
                    for q in range(4):
                        P.mm(tub[h][hs[h], 256 + q * 64:256 + (q + 1) * 64], Tf[hs[h], q, :], Zb[hs[h], q, :])
                    for q in range(4):
                        P.mm(sqb[h][hs[h], q * 64:(q + 1) * 64], tok[hs[h], q, 0, :], Tf[hs[h], q, :])
                P.copy(UHg[p][g][hs[0]], v4(tub[0][hs[0], 256:512]), eng="scalar")
                P.copy(UHg[p][g][hs[1]], v4(tub[1][hs[1], 256:512]), eng="vector")
                P.copy(WhTg[p][g][hs[0]], v4(sqb[0][hs[0], 0:256]), eng="scalar")
                P.copy(WhTg[p][g][hs[1]], v4(sqb[1][hs[1], 0:256]), eng="vector")

            hg = DBG.get('ar_hg', True)
            rw = DBG.get('ar_rw', True)
            order = [(p, g) for g in range(NG) for p in range(2)]
            for k, (p, g) in enumerate(order):
                if rw:
                    stage1(p, g)
                if hg:
                    for c in range(k * NC // len(order), (k + 1) * NC // len(order)):
                        hgrn_chunk(c)
            P.barrier()
        with ExitStack() as sc:
            bS = [P.ps(f"bS{p}", [128, 512], F32, sc) for p in range(2)]
            Ub = [P.sb(f"Ub{p}", [128, 64], BF16, sc) for p in range(2)]
            for c in range(NC if DBG.get('ar_rw', True) else 0):
                o = c * CH
                gcol = o + CH - 1
                gi = t * NC + c
                g, q = c // 4, c % 4
                Hb0, Hb1 = Hb[gi % 2], Hb[(gi + 1) % 2]
                for p in range(2):
                    P.ts(Hg[:, p, :], Hb0[:, p, :], E1[p][:, gcol:gcol + 1], ALU.mult, eng="gpsimd")
                UO = [(0, 0), (1, 1), (0, 1), (1, 0)]
                for p, h in UO:
                    P.mm(bS[p][hs[h], 0:64], WhTg[p][g][hs[h], q, :], Hb0[hs[h], p, :])
                for p in range(2):
                    P.tt(Ub[p][:], bS[p][:, 0:64], UHg[p][g][:, q, :], ALU.add)
                for p, h in UO:
                    tok, scb = TOKg[p][g], SCbg[p][g]
                    r_T = AR[p][:, c, 1, :]
                    P.mm(bS[p][hs[h], 64:128], Hb0[hs[h], p, :], r_T[hs[h], :], start=True, stop=False)
                    P.mm(bS[p][hs[h], 64:128], Ub[p][hs[h], :], scb[hs[h], q, 64:128], start=False, stop=False)
                    P.mm(bS[p][hs[h], 64:128], tok[hs[h], q, 3, :], scb[hs[h], q, 192:256], start=False, stop=True)
                for p, h in UO:
                    tok = TOKg[p][g]
                    P.mm(bS[p][hs[h], 128:192], tok[hs[h], q, 1, :], Ub[p][hs[h], :], start=True, stop=False)
                    P.mm(bS[p][hs[h], 128:192], tok[hs[h], q, 2, :], tok[hs[h], q, 3, :], start=False, stop=True)
                for p in range(2):
                    P.stt(Hb1[:, p, :], bS[p][:, 128:192], E1[p][:, gcol:gcol + 1], Hg[:, p, :], ALU.mult, ALU.add)
                    P.copy(YT[p][:, o:o + CH], bS[p][:, 64:128], eng="scalar")
            P.barrier()
        keep.close()
        with ExitStack() as so:
            pz = [P.ps(f"pz{i}", [128, 512], F32, so) for i in range(3)]
            ta = [f32(f"ta{i}", so) for i in range(3)]
            tbb = [b16(f"tbb{i}", so) for i in range(2)]
            ob = [b16(f"ob{i}", so) for i in range(4)]
            for h in range(2):
                P.act(tbb[0][:], OAt[h][:], AF.Square)
                P.mm(pz[0][:], ones[:], tbb[0][:])
                rms_rstd(P, pz[0][:], ta[0][:], 128.0)
                P.stt(ta[1][:], OAt[h][:], hv[:, h:h + 1], ta[0][:], ALU.mult, ALU.mult)
                P.tt(ob[h][:], ta[1][:], SGt[h][:], ALU.mult)
                P.dma(oT(h * 128, (h + 1) * 128, t), ob[h][:])
            for p in range(2):
                y = YT[p]
                P.copy(tbb[0][:], y[:], eng="scalar")
                P.mm(pz[0][:], bones[:], tbb[0][:])
                P.stt(ta[0][:], pz[0][:], -1.0 / 64.0, y[:], ALU.mult, ALU.add)
                P.act(tbb[1][:], ta[0][:], AF.Square)
                P.mm(pz[1][:], bones[:], tbb[1][:])
                rms_rstd(P, pz[1][:], ta[1][:], 64.0, eps=RW_LN_EPS)
                P.tt(ta[0][:], ta[0][:], ta[1][:], ALU.mult)
                P.ts(ta[0][:], ta[0][:], rv[:, p, 5:6], ALU.mult, rv[:, p, 6:7], ALU.add)
                P.mm(pz[2][:], bones[:], RKb[p][:])
                P.tt(ta[2][:], pz[2][:], VF[p][:], ALU.mult)
                P.tt(ta[0][:], ta[0][:], ta[2][:], ALU.add)
                P.tt(ob[2 + p][:], ta[0][:], GT[p][:], ALU.mult)
                P.dma(oT(256 + p * 128, 256 + (p + 1) * 128, t), ob[2 + p][:])
            P.barrier()
        tile.close()
    outer.close()


def build_ar():
    P = Prog()
    S = S_LEN
    d = lambda n, sh, dt=F32: P.dram(n, sh, dt, "ExternalInput")
    xT = d("xT", [1024, S])
    whg = d("whg", [1024, 1024])
    wrw = d("wrw", [1024, 1024])
    w2a2 = d("w2a2", [128, 256])
    g2 = d("g2", [128, 256])
    gaT = d("gaT", [128, 8])
    lb3 = d("lb3", [128, 6])
    hv = d("hv", [128, 2])
    rv = d("rv", [128, 14])
    mu8 = d("mu8", [128, 8])
    mask = d("mask", [128, 320])
    i64 = d("i64", [128, 64])
    bones = d("bones", [128, 128], BF16)
    ident = d("ident", [128, 128], BF16)
    rm = d("rm", [128, 512])
    oT = P.dram("oT", [512, S], BF16, "ExternalOutput")
    ar_phase(P, xT, whg, wrw, w2a2, g2, gaT, lb3, hv, rv, mu8, mask, i64, bones, ident, rm,
             lambda r0, r1, t: V(oT, oT.h[r0:r1, t * 512:(t + 1) * 512]))
    P.wait_ticket("sync", oT.w)
    P.barrier()
    return P.emit()


def ar_inputs(xT_b, hh, d):
    w = d["w_in_ar"][0]
    c = lambda a, b: w[:, a:b]
    h0 = hh * 256
    whg = np.concatenate([c(h0, h0 + 256), c(512 + h0, 512 + h0 + 256), c(1024 + h0, 1024 + h0 + 256),
                          c(1536 + h0, 1536 + h0 + 256)], 1)
    wrw = np.concatenate([c(2048 + h0, 2048 + h0 + 256), c(2560 + h0, 2560 + h0 + 256), c(3072 + h0, 3072 + h0 + 256),
                          c(3584, 3840)], 1)
    mu = d["rwkv_mu"][0]
    mu8 = np.concatenate([mu[h0:h0 + 256], mu[512 + h0:512 + h0 + 256], mu[1024 + h0:1024 + h0 + 256], mu[1536:1792]])
    mu8 = np.ascontiguousarray(mu8.reshape(8, 128).T)
    w2a2 = np.concatenate([d["rwkv_w2"][0][:, h0:h0 + 256], d["rwkv_a2"][0][:, h0:h0 + 256]], 0)
    g2 = d["rwkv_g2"][0][:, h0:h0 + 256]
    lb = d["hgrn_lb"]
    lb3 = np.stack([lb[:, h0 + h * 128:h0 + (h + 1) * 128].T for h in range(2)], 1).reshape(128, 6)
    hv = np.stack([d["hgrn_onorm"][0][h0 + h * 128:h0 + (h + 1) * 128] for h in range(2)], 1)
    names = ["rwkv_w0", "rwkv_a0", "rwkv_kk", "rwkv_ka", "rwkv_rk", "rwkv_ln_w", "rwkv_ln_b"]
    rv = np.stack([np.stack([d[n][0][h0 + p * 128:h0 + (p + 1) * 128] for n in names], 1) for p in range(2)], 1)
    ins = dict(xT=xT_b, whg=np.ascontiguousarray(whg), wrw=np.ascontiguousarray(wrw),
               w2a2=np.ascontiguousarray(w2a2), g2=np.ascontiguousarray(g2), gaT=vecT(d["attn_norm"][0]),
               lb3=np.ascontiguousarray(lb3), hv=np.ascontiguousarray(hv),
               rv=np.ascontiguousarray(rv.reshape(128, 14)), mu8=mu8)
    ins.update(ar_consts())
    return ins


RG = [[0, 1], [2, 3], [4, 5], [6, 7]]
ROWMAP0 = [0, 128, 512, 640, 256, 384, 768, 896]
ROWMAP1 = [0, 128, 256, 384, 512, 640, 768, 896]


def allgather_pairs(P, src, dst):
    P.barrier()
    s_, d_ = src.h, dst.h
    P.op("gpsimd", lambda e: e.collective_compute("AllGather", ALU.bypass, replica_groups=RG,
                                                  ins=[s_.opt()], outs=[d_.opt()]),
         reads=[src], writes=[dst], cc=True)
    P.barrier()


def build_fused():
    P = Prog()
    S = S_LEN
    T = S // 2
    all_in = []

    def i_(n, sh, dt=F32):
        t = P.dram(n, sh, dt, "ExternalInput")
        all_in.append((t, dt))
        return t

    def touch():
        s32 = P.sb("s32", [1, 4], F32)
        s16 = P.sb("s16", [1, 4], BF16)
        for t, dt in all_in:
            P.dma((s32 if dt == F32 else s16)[0:1, 0:2], V(t, t.h[0:1, 0:2]))

    t_ = lambda n, sh, dt: P.dram(n, sh, dt, "Internal")
    a_xT = i_("a_xT", [1024, S])
    a_whg = i_("a_whg", [1024, 1024])
    a_wrw = i_("a_wrw", [1024, 1024])
    a_w2a2 = i_("a_w2a2", [128, 256])
    a_g2 = i_("a_g2", [128, 256])
    a_gaT = i_("a_gaT", [128, 8])
    a_lb3 = i_("a_lb3", [128, 6])
    a_hv = i_("a_hv", [128, 2])
    a_rv = i_("a_rv", [128, 14])
    a_mu8 = i_("a_mu8", [128, 8])
    c_mask = i_("c_mask", [128, 320])
    c_i64 = i_("c_i64", [128, 64])
    c_bones = i_("c_bones", [128, 128], BF16)
    c_ident = i_("c_ident", [128, 128], BF16)
    c_rm = i_("c_rm", [128, 512])
    sel = i_("sel", [128, 2])
    ga1T = i_("ga1T", [128, 8])
    dn = []
    for l in range(2):
        dn.append(dict(pT=i_(f"d{l}_pT", [256, T]), wout=i_(f"d{l}_wout", [1024, 1024]), wup=i_(f"d{l}_wup", [1024, 4096]),
                       wdown=i_(f"d{l}_wdown", [4096, 1024]), wple=i_(f"d{l}_wple", [256, 1024]),
                       wgate=i_(f"d{l}_wgate", [1024, 1024]), gmT=i_(f"d{l}_gmT", [128, 8]), gpT=i_(f"d{l}_gpT", [128, 8])))
    d0_xh = i_("d0_xh", [1024, T])
    m_wq = i_("m_wq", [1024, 512])
    m_wk = i_("m_wk", [1024, 512])
    m_wv = i_("m_wv", [1024, 512])
    m_qn2 = i_("m_qn2", [128, 2])
    m_kn2 = i_("m_kn2", [128, 2])
    c_cosT = i_("c_cosT", [128, S])
    c_sinT = i_("c_sinT", [128, S])
    c_perm = i_("c_perm", [128, 128], BF16)
    c_cb = i_("c_cb", [128, 512], BF16)
    c_esel = i_("c_esel", [16, 2048], BF16)
    c_pb = i_("c_pb", [128, 256])
    yT = P.dram("yT", [1024, T], F32, "ExternalOutput")
    o0_src = [t_(f"o0_src{i}", [512, T], BF16) for i in range(2)]
    o0_all = [t_(f"o0_all{i}", [1024, T], BF16) for i in range(2)]
    x1_loc = t_("x1_loc", [1024, T], F32)
    h1_src = [t_(f"h1_src{i}", [512, T], BF16) for i in range(2)]
    h1_all = [t_(f"h1_all{i}", [1024, T], BF16) for i in range(2)]
    o1_src = [t_(f"o1_src{i}", [512, T], BF16) for i in range(2)]
    o1_all = [t_(f"o1_all{i}", [1024, T], BF16) for i in range(2)]

    ar_phase(P, a_xT, a_whg, a_wrw, a_w2a2, a_g2, a_gaT, a_lb3, a_hv, a_rv, a_mu8, c_mask, c_i64, c_bones, c_ident,
             c_rm, lambda r0, r1, t: V(o0_src[t // 4], o0_src[t // 4].h[r0:r1, (t % 4) * 512:(t % 4 + 1) * 512]))
    for i in range(2):
        allgather_pairs(P, o0_src[i], o0_all[i])
    fs = DBG.get("f_stop", 0)
    if fs:
        dbg = P.dram("dbg", [2048, S], BF16, "ExternalOutput")
    if fs == 1:
        touch()
        for i in range(2):
            P.dma(V(dbg, dbg.h[0:1024, i * T:(i + 1) * T]), o0_all[i][:])
        P.wait_ticket("sync", dbg.w)
        P.barrier()
        return P.emit(), P
    d = dn[0]
    dense_phase(P, T, d0_xh, None, d["pT"], d["wout"], d["wup"], d["wdown"], d["wple"], d["wgate"], d["gmT"], d["gpT"],
                x1_loc, o_gather=(o0_all, ROWMAP0), sel_d=sel, h_out=h1_src, ga_next_d=ga1T)
    for i in range(2):
        allgather_pairs(P, h1_src[i], h1_all[i])
    if fs == 2:
        touch()
        for i in range(2):
            P.dma(V(dbg, dbg.h[i * 1024:(i + 1) * 1024, 0:T]), h1_all[i][:])
        P.dma(yT[:], x1_loc[:])
        P.wait_ticket("sync", dbg.w)
        P.wait_ticket("sync", yT.w)
        P.barrier()
        return P.emit(), P
    moba_phase(P, None, m_wq, m_wk, m_wv, ga1T, m_qn2, m_kn2, c_cosT, c_sinT, c_perm, c_ident, c_cb, c_esel, c_pb,
               lambda r0, r1, s_: V(o1_src[s_], o1_src[s_].h[r0:r1, :]), h_all=h1_all)
    for i in range(2):
        allgather_pairs(P, o1_src[i], o1_all[i])
    if fs == 3:
        touch()
        for i in range(2):
            P.dma(V(dbg, dbg.h[0:1024, i * T:(i + 1) * T]), o1_all[i][:])
        P.wait_ticket("sync", dbg.w)
        P.barrier()
        return P.emit(), P
    d = dn[1]
    dense_phase(P, T, x1_loc, None, d["pT"], d["wout"], d["wup"], d["wdown"], d["wple"], d["wgate"], d["gmT"], d["gpT"],
                yT, o_gather=(o1_all, ROWMAP1), sel_d=sel)
    P.wait_ticket("sync", yT.w)
    P.barrier()
    return P.emit(), P


def fused_inputs(d, c, xT_b):
    b, r = c // 2, c % 2
    T = S_LEN // 2
    sl = slice(r * T, (r + 1) * T)
    a = ar_inputs(xT_b, r, d)
    m = moba_inputs(None, r, d)
    ins = dict(a_xT=xT_b, a_whg=a["whg"], a_wrw=a["wrw"], a_w2a2=a["w2a2"], a_g2=a["g2"], a_gaT=a["gaT"],
               a_lb3=a["lb3"], a_hv=a["hv"], a_rv=a["rv"], a_mu8=a["mu8"], c_mask=a["mask"], c_i64=a["i64"],
               c_bones=a["bones"], c_ident=a["ident"], c_rm=a["rm"],
               sel=np.ascontiguousarray(np.tile(np.eye(2, dtype=np.float32)[r][None, :], (128, 1))),
               ga1T=vecT(d["attn_norm"][1]), d0_xh=np.ascontiguousarray(xT_b[:, sl]),
               m_wq=m["wq"], m_wk=m["wk"], m_wv=m["wv"], m_qn2=m["qn2"], m_kn2=m["kn2"], c_cosT=m["cosT"],
               c_sinT=m["sinT"], c_perm=m["perm"], c_cb=m["cb"], c_esel=m["esel"], c_pb=m["pb"])
    wouts = [d["w_out_ar"][0], d["w_o_attn"][0]]
    for l in range(2):
        ins.update({f"d{l}_pT": np.ascontiguousarray(d["p"][l, b, sl].T), f"d{l}_wout": wouts[l], f"d{l}_wup": d["w_up"][l],
                    f"d{l}_wdown": d["w_down"][l], f"d{l}_wple": d["ple_proj"][l], f"d{l}_wgate": d["ple_gate"][l],
                    f"d{l}_gmT": vecT(d["mlp_norm"][l]), f"d{l}_gpT": vecT(d["ple_norm"][l])})
    return ins


def kernel(**inputs):
    d = {k: np.asarray(v) for k, v in inputs.items()}
    B = d["x"].shape[0]
    cores = list(range(8))
    xT = [np.ascontiguousarray(d["x"][b].T) for b in range(B)]
    nc, _ = build_fused()
    r = run_bass_kernel_spmd(nc, [fused_inputs(d, c, xT[c // 2]) for c in cores], core_ids=cores)
    out = np.empty(d["x"].shape, np.float32)
    T = S_LEN // 2
    for c in cores:
        out[c // 2, (c % 2) * T:(c % 2 + 1) * T, :] = r.results[c]["yT"].T
    return out
```

```python
from contextlib import ExitStack

import numpy as np
import ml_dtypes
import concourse.bass as bass
import concourse.mybir as mybir
from concourse.bass_utils import run_bass_kernel_spmd

F32 = mybir.dt.float32
BF16 = mybir.dt.bfloat16
AF = mybir.ActivationFunctionType
ALU = mybir.AluOpType
AX = mybir.AxisListType

ENGS = ["tensor", "vector", "scalar", "gpsimd", "sync"]
SAME_ENGINE_SYNC = True
N_DMA_SEMS = 6
CC_INC = 1


class Tile:
    def __init__(self, h, name):
        self.h = h
        self.name = name
        self.psum = False
        self.w = None
        self.r = {}

    def __getitem__(self, idx):
        return V(self, self.h[idx])

    def ap(self):
        return V(self, self.h.ap() if hasattr(self.h, "ap") and callable(self.h.ap) else self.h[:])


class V:
    def __init__(self, tile, ap):
        self.tile = tile
        self.ap = ap

    def __getitem__(self, idx):
        return V(self.tile, self.ap[idx])

    def re(self, pat, **kw):
        return V(self.tile, self.ap.rearrange(pat, **kw))

    def bc(self, shape):
        return V(self.tile, self.ap.broadcast_to(shape))


def _ap(x):
    return x.ap if isinstance(x, V) else x


class Prog:
    def __init__(self):
        self.nc = bass.Bass("TRN2", target_bir_lowering=False)
        self.es = ExitStack()
        self.ops = {e: [] for e in ENGS}
        self.cnt = {}
        self.seen = {e: {} for e in ENGS}
        self.sems = {}
        self.dma_rr = {"sync": 0, "gpsimd": 0, "scalar": 0}
        self.dma_last = {}
        self.n_ops = 0

    def sem(self, key):
        if key not in self.sems:
            self.sems[key] = self.es.enter_context(self.nc.semaphore(key))
            self.cnt[key] = 0
        return self.sems[key]

    def _uniq(self, name):
        self.uid = getattr(self, "uid", 0) + 1
        return f"{name}_u{self.uid}"

    def sb(self, name, shape, dt, stack=None):
        name = self._uniq(name)
        h = (stack or self.es).enter_context(self.nc.sbuf_tensor(name, list(shape), dt))
        return Tile(h, name)

    def ps(self, name, shape, dt=F32, stack=None):
        name = self._uniq(name)
        h = (stack or self.es).enter_context(self.nc.psum_tensor(name, list(shape), dt))
        t = Tile(h, name)
        t.psum = True
        return t

    def dram(self, name, shape, dt, kind):
        h = self.nc.dram_tensor(name, list(shape), dt, kind=kind)
        return Tile(h.ap(), name)

    def _need(self, eng, waits, tk):
        if tk is None:
            return
        key, val = tk
        if key == eng and not (SAME_ENGINE_SYNC and eng != "tensor"):
            return
        if self.seen[eng].get(key, 0) >= val:
            return
        waits[key] = max(waits.get(key, 0), val)

    def op(self, eng, fn, reads=(), writes=(), dma=False, inc=True, cc=False):
        waits = {}
        rt = []
        wt = []
        for r in reads:
            t = r.tile if isinstance(r, V) else r
            if t is not None and t not in rt:
                rt.append(t)
        for w in writes:
            t = w.tile if isinstance(w, V) else w
            if t is not None and t not in wt:
                wt.append(t)
        for t in rt:
            self._need(eng, waits, t.w)
            if t.psum:
                for k, v in t.r.items():
                    if k != eng:
                        self._need(eng, waits, (k, v))
        for t in wt:
            self._need(eng, waits, t.w)
            for k, v in t.r.items():
                self._need(eng, waits, (k, v))
        if cc:
            key = "cc"
            self.sem(key)
            incv = CC_INC
        elif dma:
            i = self.dma_rr[eng]
            self.dma_rr[eng] = (i + 1) % N_DMA_SEMS
            key = f"d_{eng}_{i}"
            self.sem(key)
            self._need(eng, waits, (key, self.cnt[key]))
            incv = 16
        else:
            key = eng
            self.sem(key)
            incv = 1
        if inc:
            self.cnt[key] += incv
            tk = (key, self.cnt[key])
        else:
            tk = (key, self.cnt[key] + incv)
        for k, v in waits.items():
            self.seen[eng][k] = v
        if not dma and inc:
            self.seen[eng][key] = max(self.seen[eng].get(key, 0), 0)
        wl = [(self.sems[k], v) for k, v in waits.items()]
        semh = self.sems[key]
        self.ops[eng].append((wl, fn, semh if inc else None, incv))
        for t in rt:
            if t not in wt:
                t.r[key] = max(t.r.get(key, 0), tk[1])
        for t in wt:
            t.w = tk
            t.r = {}
        self.n_ops += 1
        bg = getattr(self, "bg", None)
        if bg is not None and not getattr(self, "_in_bg", False):
            self._fg = getattr(self, "_fg", 0) + 1
            if self._fg % self.bg_every == 0:
                self._in_bg = True
                try:
                    next(bg)
                except StopIteration:
                    self.bg = None
                self._in_bg = False
        return tk

    def set_bg(self, gen, every):
        self.flush_bg()
        self.bg = gen
        self.bg_every = every
        self._fg = 0

    def flush_bg(self):
        bg = getattr(self, "bg", None)
        if bg is not None:
            self._in_bg = True
            for _ in bg:
                pass
            self._in_bg = False
            self.bg = None

    def wait_ticket(self, eng, tk):
        waits = {}
        self._need(eng, waits, tk)
        if waits:
            for k, v in waits.items():
                self.seen[eng][k] = v
            wl = [(self.sems[k], v) for k, v in waits.items()]
            self.ops[eng].append((wl, None, None, 0))

    def barrier(self):
        snap = dict(self.cnt)
        for e in ENGS:
            for k, v in snap.items():
                if v > 0:
                    self.wait_ticket(e, (k, v)) if k != e else None

    def emit(self):
        nc = self.nc
        with nc.Block() as block:
            def mk(eng_name):
                lst = self.ops[eng_name]

                def body(e):
                    for wl, fn, semh, incv in lst:
                        for s, v in wl:
                            e.wait_ge(s, v)
                        if fn is not None:
                            ins = fn(e)
                            if semh is not None:
                                ins.then_inc(semh, incv)
                return body
            block.tensor(mk("tensor"))
            block.vector(mk("vector"))
            block.scalar(mk("scalar"))
            block.gpsimd(mk("gpsimd"))
            block.sync(mk("sync"))
        self.es.close()
        return nc

    def dma(self, out, in_, eng="sync", **kw):
        o, i = _ap(out), _ap(in_)
        return self.op(eng, lambda e: e.dma_start(out=o, in_=i, **kw),
                       reads=[in_], writes=[out], dma=True)

    def mm(self, out, lhsT, rhs, start=True, stop=True, extra_reads=(), **kw):
        o, l, r = _ap(out), _ap(lhsT), _ap(rhs)
        return self.op("tensor", lambda e: e.matmul(o, l, r, start=start, stop=stop, **kw),
                       reads=[lhsT, rhs] + list(extra_reads), writes=[out], inc=stop)

    def transpose(self, out, in_, ident, **kw):
        o, i, d = _ap(out), _ap(in_), _ap(ident)
        return self.op("tensor", lambda e: e.transpose(o, i, d, **kw),
                       reads=[in_, ident], writes=[out])

    def act(self, out, in_, func, bias=None, scale=None, accum_out=None, eng="scalar"):
        o, i = _ap(out), _ap(in_)
        kw = {}
        reads = [in_]
        if bias is not None:
            kw["bias"] = _ap(bias)
            if isinstance(bias, V):
                reads.append(bias)
        if scale is not None:
            kw["scale"] = _ap(scale)
            if isinstance(scale, V):
                reads.append(scale)
        writes = [out]
        if accum_out is not None:
            kw["accum_out"] = _ap(accum_out)
            writes.append(accum_out)
        return self.op(eng, lambda e: e.activation(o, i, func, **kw), reads=reads, writes=writes)

    def tt(self, out, in0, in1, op, eng="vector"):
        o, a, b = _ap(out), _ap(in0), _ap(in1)
        return self.op(eng, lambda e: e.tensor_tensor(o, a, b, op), reads=[in0, in1], writes=[out])

    def ts(self, out, in0, s1, op0, s2=None, op1=None, eng="vector", accum_out=None):
        o, a = _ap(out), _ap(in0)
        reads = [in0]
        for s in (s1, s2):
            if isinstance(s, V):
                reads.append(s)
        x1, x2 = _ap(s1), _ap(s2)
        kw = {}
        writes = [out]
        if op1 is not None:
            kw["op1"] = op1
        if accum_out is not None:
            kw["accum_out"] = _ap(accum_out)
            writes.append(accum_out)
        return self.op(eng, lambda e: e.tensor_scalar(o, a, x1, x2, op0, **kw), reads=reads, writes=writes)

    def stt(self, out, in0, scalar, in1, op0, op1, eng="vector"):
        o, a, b = _ap(out), _ap(in0), _ap(in1)
        reads = [in0, in1]
        if isinstance(scalar, V):
            reads.append(scalar)
        s = _ap(scalar)
        return self.op(eng, lambda e: e.scalar_tensor_tensor(o, a, s, b, op0, op1), reads=reads, writes=[out])

    def copy(self, out, in_, eng="vector"):
        o, i = _ap(out), _ap(in_)
        if eng == "scalar":
            return self.op(eng, lambda e: e.copy(o, i), reads=[in_], writes=[out])
        return self.op(eng, lambda e: e.tensor_copy(o, i), reads=[in_], writes=[out])

    def memset(self, out, val, eng="vector"):
        o = _ap(out)
        return self.op(eng, lambda e: e.memset(o, val), reads=[], writes=[out])

    def reduce(self, out, in_, op, axis, eng="vector"):
        o, i = _ap(out), _ap(in_)
        return self.op(eng, lambda e: e.tensor_reduce(o, i, axis, op), reads=[in_], writes=[out])


EPS = 1e-6


def wview(w, r0, kc, c0, ncols):
    return V(w, w.h[r0:r0 + 128 * kc, c0:c0 + ncols].rearrange("(c p) n -> p c n", p=128))


def tview(a, kc, t0, nt, r0=0):
    return V(a, a.h[r0:r0 + 128 * kc, t0:t0 + nt].rearrange("(c p) t -> p c t", p=128))


def rms_rstd(P, pn, rstd, n_feat, eps=EPS):
    P.act(rstd, pn, AF.Ln, scale=1.0 / n_feat, bias=eps)
    P.act(rstd, rstd, AF.Exp, scale=-0.5)


def dense_phase(P, T, xT, oT, pT, wout, wup, wdown, wple, wgate, gmT, gpT, yT,
                o_gather=None, sel_d=None, h_out=None, ga_next_d=None):
    NT = T // 512
    outer = ExitStack()
    X = [P.sb(f"X{t}", [128, 8, 512], F32, outer) for t in range(NT)]
    HT = [P.sb(f"HT{t}", [128, 8, 512], BF16, outer) for t in range(NT)]
    ones = P.sb("ones", [128, 128], BF16, outer)
    gm = P.sb("gm", [128, 8], F32, outer)
    gp = P.sb("gp", [128, 8], F32, outer)
    gn = P.sb("gn", [128, 8], F32, outer)
    sq = [P.sb(f"sq{i}", [128, 8, 512], BF16, outer) for i in range(1)]
    rstd = [P.sb(f"rstd{i}", [128, 512], F32, outer) for i in range(2)]
    tmp = [P.sb(f"tmp{i}", [128, 512], F32, outer) for i in range(2)]
    wo = P.sb("wo", [128, 8, 1024], BF16, outer)
    pa = [P.ps(f"pa{i}", [128, 512], F32, outer) for i in range(4)]
    pn = [P.ps(f"pn{i}", [128, 512], F32, outer) for i in range(2)]
    P.memset(ones[:], 1.0)
    P.dma(gm[:], gmT[:])
    P.dma(gp[:], gpT[:])
    P.dma(wo[:], wview(wout, 0, 8, 0, 1024), eng="gpsimd")
    for t in range(NT):
        P.dma(X[t][:], tview(xT, 8, t * 512, 512))
        if o_gather is None:
            P.dma(HT[t][:], tview(oT, 8, t * 512, 512))
    if o_gather is not None:
        o_all, rowmap = o_gather
        with ExitStack() as s0:
            sel = P.sb("sel", [128, 2], F32, s0)
            Ab = [[P.sb(f"Ab{s_}{i}", [128, 8, 512], BF16, s0) for i in range(2)] for s_ in range(2)]
            P.dma(sel[:], sel_d[:])
            for t in range(NT):
                for s_ in range(2):
                    a = Ab[s_][t % 2]
                    for j in range(4):
                        r0 = rowmap[2 * j]
                        c0 = t * 512
                        oa = o_all[s_]
                        P.dma(a[:, 2 * j:2 * j + 2, :],
                              V(oa, oa.h[r0:r0 + 256, c0:c0 + 512].rearrange("(c p) t -> p c t", p=128)))
                a0, a1 = Ab[0][t % 2], Ab[1][t % 2]
                P.ts(a0[:], a0[:], sel[:, 0:1], ALU.mult)
                P.stt(HT[t][:], a1[:], sel[:, 1:2], a0[:], ALU.mult, ALU.add)
            P.barrier()
    pi = [0]

    def nextpa():
        pi[0] = (pi[0] + 1) % len(pa)
        return pa[pi[0]]

    for m in range(8):
        for t in range(NT):
            acc = nextpa()
            for kc in range(8):
                P.mm(acc[:], wo[:, kc, m * 128:(m + 1) * 128], HT[t][:, kc, :], start=kc == 0, stop=kc == 7)
            P.tt(X[t][:, m, :], X[t][:, m, :], acc[:], ALU.add)

    def rmsnorm_to(dst, src, g, t):
        s = sq[0]
        P.act(s[:], src[:], AF.Square)
        n = pn[t % 2]
        for c in range(8):
            P.mm(n[:], ones[:], s[:, c, :], start=c == 0, stop=c == 7)
        r = rstd[t % 2]
        rms_rstd(P, r[:], n[:], 1024.0) if False else rms_rstd(P, n[:], r[:], 1024.0)
        return r

    for t in range(NT):
        r = rmsnorm_to(None, X[t], gm, t)
        for c in range(8):
            P.stt(HT[t][:, c, :], X[t][:, c, :], gm[:, c:c + 1], r[:], ALU.mult, ALU.mult)

    with ExitStack() as s1:
        A = [P.sb(f"A{t}", [128, 4, 512], BF16, s1) for t in range(NT)]
        wu = [P.sb(f"wu{i}", [128, 8, 512], BF16, s1) for i in range(2)]
        wd = [P.sb(f"wd{i}", [128, 4, 1024], BF16, s1) for i in range(2)]
        for e in range(8):
            P.dma(wu[e % 2][:], wview(wup, 0, 8, e * 512, 512), eng="gpsimd")
            P.dma(wd[e % 2][:], wview(wdown, e * 512, 4, 0, 1024), eng="gpsimd")
            for f in range(4):
                for t in range(NT):
                    acc = nextpa()
                    for kc in range(8):
                        P.mm(acc[:], wu[e % 2][:, kc, f * 128:(f + 1) * 128], HT[t][:, kc, :],
                             start=kc == 0, stop=kc == 7)
                    tm = tmp[(f * NT + t) % 2]
                    P.act(tm[:], acc[:], AF.Square)
                    P.stt(A[t][:, f, :], acc[:], 0.0, tm[:], ALU.is_gt, ALU.mult)
            for m in range(8):
                for t in range(NT):
                    acc = nextpa()
                    for f in range(4):
                        P.mm(acc[:], wd[e % 2][:, f, m * 128:(m + 1) * 128], A[t][:, f, :],
                             start=f == 0, stop=f == 3)
                    P.tt(X[t][:, m, :], X[t][:, m, :], acc[:], ALU.add)
        P.barrier()

    with ExitStack() as s2:
        PT = P.sb("PT", [128, 2, T], BF16, s2)
        wp = P.sb("wp", [128, 2, 1024], BF16, s2)
        PP = P.sb("PP", [128, 8, 512], F32, s2)
        P.dma(PT[:], tview(pT, 2, 0, T), eng="gpsimd")
        P.dma(wp[:], wview(wple, 0, 2, 0, 1024), eng="gpsimd")
        P.dma(wo[:], wview(wgate, 0, 8, 0, 1024), eng="gpsimd")
        for t in range(NT):
            P.copy(HT[t][:], X[t][:], eng="scalar")
        for t in range(NT):
            for m in range(8):
                acc = nextpa()
                for kc in range(2):
                    P.mm(acc[:], wp[:, kc, m * 128:(m + 1) * 128], PT[:, kc, t * 512:(t + 1) * 512],
                         start=kc == 0, stop=kc == 1)
                P.copy(PP[:, m, :], acc[:], eng="scalar")
            r = rmsnorm_to(None, PP, gp, t)
            for m in range(8):
                acc = nextpa()
                for kc in range(8):
                    P.mm(acc[:], wo[:, kc, m * 128:(m + 1) * 128], HT[t][:, kc, :], start=kc == 0, stop=kc == 7)
                tm = tmp[m % 2]
                P.act(tm[:], acc[:], AF.Sigmoid)
                P.stt(PP[:, m, :], PP[:, m, :], gp[:, m:m + 1], r[:], ALU.mult, ALU.mult)
                P.tt(tm[:], tm[:], PP[:, m, :], ALU.mult)
                P.tt(X[t][:, m, :], X[t][:, m, :], tm[:], ALU.add)
            P.dma(tview(yT, 8, t * 512, 512), X[t][:])
            if h_out is not None:
                if t == 0:
                    P.dma(gn[:], ga_next_d[:])
                r = rmsnorm_to(None, X[t], gn, t)
                for c in range(8):
                    P.stt(HT[t][:, c, :], X[t][:, c, :], gn[:, c:c + 1], r[:], ALU.mult, ALU.mult)
                for f_ in range(2):
                    P.dma(tview(h_out[f_], 4, t * 512, 512), HT[t][:, 4 * f_:4 * f_ + 4, :])
        P.barrier()
    outer.close()


def build_dense(T):
    P = Prog()
    xT = P.dram("xT", [1024, T], F32, "ExternalInput")
    oT = P.dram("oT", [1024, T], BF16, "ExternalInput")
    pT = P.dram("pT", [256, T], F32, "ExternalInput")
    wout = P.dram("wout", [1024, 1024], F32, "ExternalInput")
    wup = P.dram("wup", [1024, 4096], F32, "ExternalInput")
    wdown = P.dram("wdown", [4096, 1024], F32, "ExternalInput")
    wple = P.dram("wple", [256, 1024], F32, "ExternalInput")
    wgate = P.dram("wgate", [1024, 1024], F32, "ExternalInput")
    gmT = P.dram("gmT", [128, 8], F32, "ExternalInput")
    gpT = P.dram("gpT", [128, 8], F32, "ExternalInput")
    yT = P.dram("yT", [1024, T], F32, "ExternalOutput")
    dense_phase(P, T, xT, oT, pT, wout, wup, wdown, wple, wgate, gmT, gpT, yT)
    P.wait_ticket("sync", yT.w)
    P.barrier()
    return P.emit()


def vecT(v):
    return np.ascontiguousarray(v.reshape(-1, 128).T)


S_LEN = 4096
NEG = -100.0
DBG = {}


def moba_consts():
    half = 64
    inv = np.power(10000.0, -np.arange(half, dtype=np.float32) / half).astype(np.float32)
    ang = np.arange(S_LEN, dtype=np.float32)[:, None] * inv[None, :]
    cos = np.cos(ang).astype(np.float32).T
    sin = np.sin(ang).astype(np.float32).T
    cosT = np.concatenate([cos, cos], 0)
    sinT = np.concatenate([-sin, sin], 0)
    perm = np.zeros((128, 128), np.float32)
    for dd in range(128):
        perm[(dd + 64) % 128, dd] = 1.0
    ident = np.eye(128, dtype=np.float32)
    cb = np.zeros((128, 2, 256), np.float32)
    for kt in range(2):
        k = kt * 128 + np.arange(128)[:, None]
        q = np.arange(256)[None, :]
        cb[:, kt, :] = np.where(k > q, NEG, 0.0)
    esel = np.zeros((16, 16, 128), np.float32)
    for n in range(16):
        esel[n, n, :] = 1.0
    pb = np.zeros((128, 16, 16), np.float32)
    for i in range(16):
        pb[:, i, i:] = -1e30
    bf = ml_dtypes.bfloat16
    return dict(cosT=np.ascontiguousarray(cosT), sinT=np.ascontiguousarray(sinT), perm=perm.astype(bf),
                ident=ident.astype(bf), cb=cb.reshape(128, 512).astype(bf),
                esel=esel.reshape(16, 2048).astype(bf), pb=pb.reshape(128, 256))


def moba_phase(P, xT, wq, wk, wv, gaT, qn2, kn2, cosT, sinT, perm_d, ident_d, cb_d, esel_d, pb_d, oT, h_all=None):
    S = S_LEN
    NT = S // 512
    outer = ExitStack()
    QR = [P.sb(f"QR{h}", [128, S], BF16, outer) for h in range(4)]
    KR = [P.sb(f"KR{h}", [128, S], BF16, outer) for h in range(4)]
    VP = P.sb("VP", [128, 32, 4, 130], BF16, outer)
    ones = P.sb("ones", [128, 128], BF16, outer)
    ident = P.sb("ident", [128, 128], BF16, outer)
    kmT = P.sb("kmT", [128, 4, 16], BF16, outer)
    P.memset(ones[:], 1.0)
    P.memset(VP[:], 1.0)
    P.dma(ident[:], ident_d[:])
    with ExitStack() as s1:
        ga = P.sb("ga", [128, 8], F32, s1)
        qk = P.sb("qk", [128, 4], F32, s1)
        perm = P.sb("perm", [128, 128], BF16, s1)
        Xt = [P.sb(f"Xt{i}", [128, 8, 512], F32, s1) for i in range(2)]
        Ht = [P.sb(f"Ht{i}", [128, 8, 512], BF16, s1) for i in range(2)]
        cs = [P.sb(f"cs{i}", [128, 2, 512], F32, s1) for i in range(2)]
        sq = P.sb("sq", [128, 8, 512], BF16, s1)
        rstd = P.sb("rstd", [128, 512], F32, s1)
        w3 = [P.sb(f"w3{i}", [128, 8, 512], BF16, s1) for i in range(3)]
        kb = [P.sb(f"kb{i}", [128, 512], BF16, s1) for i in range(2)]
        sk = [P.sb(f"sk{i}", [128, 512], BF16, s1) for i in range(2)]
        r2 = [P.sb(f"r2{i}", [128, 512], F32, s1) for i in range(2)]
        t1 = [P.sb(f"t1{i}", [128, 512], F32, s1) for i in range(2)]
        t2 = [P.sb(f"t2{i}", [128, 512], F32, s1) for i in range(2)]
        km32 = P.sb("km32", [128, 16], F32, s1)
        pk = [P.ps(f"pk{i}", [128, 512], F32, s1) for i in range(2)]
        pp = [P.ps(f"pp{i}", [128, 512], F32, s1) for i in range(2)]
        pn = [P.ps(f"pn{i}", [128, 512], F32, s1) for i in range(2)]
        pv = [P.ps(f"pv{i}", [128, 512], F32, s1) for i in range(2)]
        P.dma(ga[:], gaT[:])
        P.dma(qk[:, 0:2], qn2[:])
        P.dma(qk[:, 2:4], kn2[:])
        P.dma(perm[:], perm_d[:])
        P.ts(qk[:, 0:2], qk[:, 0:2], float(128 ** -0.5), ALU.mult)
        for i, w in enumerate((wq, wk, wv)):
            P.dma(w3[i][:], wview(w, 0, 8, 0, 512), eng="gpsimd")
        cnt = 0
        for t in range(NT):
            X = Xt[t % 2]
            H = Ht[t % 2]
            C = cs[t % 2]
            P.dma(C[:, 0, :], V(cosT, cosT.h[:, t * 512:(t + 1) * 512]))
            P.dma(C[:, 1, :], V(sinT, sinT.h[:, t * 512:(t + 1) * 512]))
            if h_all is not None:
                rk_, c0_ = t // 4, (t % 4) * 512
                for f_ in range(2):
                    P.dma(H[:, 4 * f_:4 * f_ + 4, :],
                          V(h_all[f_], h_all[f_].h[rk_ * 512:(rk_ + 1) * 512, c0_:c0_ + 512].rearrange("(c p) t -> p c t", p=128)))
            else:
                P.dma(X[:], tview(xT, 8, t * 512, 512))
                P.act(sq[:], X[:], AF.Square)
                n = pn[0]
                for c in range(8):
                    P.mm(n[:], ones[:], sq[:, c, :], start=c == 0, stop=c == 7)
                rms_rstd(P, n[:], rstd[:], 1024.0)
                for c in range(8):
                    P.stt(H[:, c, :], X[:, c, :], ga[:, c:c + 1], rstd[:], ALU.mult, ALU.mult)
            for h in range(4):
                for which in range(2):
                    w = w3[which]
                    dst = (QR if which == 0 else KR)[h]
                    g0 = qk[:, 2 * which:2 * which + 1]
                    g1 = qk[:, 2 * which + 1:2 * which + 2]
                    j = cnt % 2
                    cnt += 1
                    a = pk[j]
                    for kc in range(8):
                        P.mm(a[:], w[:, kc, h * 128:(h + 1) * 128], H[:, kc, :], start=kc == 0, stop=kc == 7)
                    P.copy(kb[j][:], a[:], eng="scalar")
                    P.act(sk[j][:], a[:], AF.Square)
                    P.mm(pp[j][:], perm[:], kb[j][:])
                    P.mm(pn[1][:], ones[:], sk[j][:])
                    rms_rstd(P, pn[1][:], r2[j][:], 128.0)
                    P.stt(t1[j][:], a[:], g0, C[:, 0, :], ALU.mult, ALU.mult)
                    P.stt(t2[j][:], pp[j][:], g1, C[:, 1, :], ALU.mult, ALU.mult)
                    P.tt(t1[j][:], t1[j][:], t2[j][:], ALU.add, eng="gpsimd")
                    P.tt(dst[:, t * 512:(t + 1) * 512], t1[j][:], r2[j][:], ALU.mult, eng="gpsimd")
            for sub in range(4):
                a = pv[sub % 2]
                for kc in range(8):
                    P.mm(a[:], H[:, kc, sub * 128:(sub + 1) * 128], w3[2][:, kc, :], start=kc == 0, stop=kc == 7)
                P.copy(VP[:, t * 4 + sub, :, 0:128], a[:].re("p (h d) -> p h d", h=4), eng="vector")
        for h in range(4):
            P.reduce(km32[:], KR[h][:].re("p (n j) -> p n j", j=256), ALU.add, AX.X)
            P.ts(kmT[:, h, :], km32[:], 1.0 / 256.0, ALU.mult)
        P.barrier()
    if DBG.get("moba_stop") == "A":
        outer.close()
        return
    with ExitStack() as s2:
        cb = P.sb("cb", [128, 2, 256], BF16, s2)
        esel = P.sb("esel", [16, 16, 128], BF16, s2)
        pb = P.sb("pb", [128, 16, 16], F32, s2)
        SBT = P.sb("SBT", [16, 4, S], BF16, s2)
        OT = [P.sb(f"OT{i}", [128, S], BF16, s2) for i in range(2)]
        PT = [P.sb(f"PT{i}", [128, 2, 256], BF16, s2) for i in range(3)]
        gm = P.sb("gm", [128, 32, 16], F32, s2)
        m8 = P.sb("m8", [128, 32, 8], F32, s2)
        sbq = P.sb("sbq", [128, 32, 16], BF16, s2)
        rec = [P.sb(f"rec{i}", [128, 1], F32, s2) for i in range(4)]
        on = [P.sb(f"on{i}", [128, 128], BF16, s2) for i in range(4)]
        pS = [P.ps(f"pS{i}", [128, 2, 256], F32, s2) for i in range(2)]
        pO = [P.ps(f"pO{i}", [128, 512], F32, s2) for i in range(4)]
        pg = P.ps("pg", [128, 32, 16], F32, s2)
        ptr = P.ps("ptr", [128, 1024], BF16, s2)
        P.dma(cb[:], V(cb_d, cb_d.h[:, :].rearrange("p (k q) -> p k q", k=2)))
        P.dma(esel[:], V(esel_d, esel_d.h[:, :].rearrange("p (n j) -> p n j", n=16)))
        P.dma(pb[:], V(pb_d, pb_d.h[:, :].rearrange("p (i n) -> p i n", i=16)))
        for h in range(4):
            for qt in range(32):
                P.mm(pg[:, qt, :], QR[h][:, qt * 128:(qt + 1) * 128], kmT[:, h, :])
            gm4 = gm[:].re("p (i two) n -> p i two n", two=2)
            pg4 = pg[:].re("p (i two) n -> p i two n", two=2)
            for two in range(2):
                P.tt(gm4[:, :, two, :], pg4[:, :, two, :], pb[:], ALU.add)
            for qt in range(32):
                P.op("vector", (lambda qt: lambda e: e.max(out=m8.h[:, qt, :], in_=gm.h[:, qt, :]))(qt),
                     reads=[gm], writes=[m8])
            P.tt(sbq[:], gm[:], V(m8, m8.h[:, :, 2:3].broadcast_to([128, 32, 16])), ALU.is_lt)
            P.ts(sbq[:], sbq[:], NEG, ALU.mult, eng="gpsimd")
            for rnd in range(4):
                for j in range(8):
                    qt = rnd * 8 + j
                    P.transpose(ptr[0:16, j * 128:(j + 1) * 128], sbq[:, qt, :], ident[:])
                P.copy(SBT[:, h, rnd * 1024:(rnd + 1) * 1024], ptr[0:16, :], eng="scalar")
        for h in range(4):
            ot = OT[h % 2]
            its = [(i, n) for i in range(DBG.get("moba_nblk", 16)) for n in range(i + 1)]
            pend = []

            def emit_S(idx):
                i, n = its[idx]
                q0 = i * 256
                ps_ = pS[idx % 2]
                for kt in range(2):
                    k0 = (n * 2 + kt) * 128
                    P.mm(ps_[:, kt, :], KR[h][:, k0:k0 + 128], QR[h][:, q0:q0 + 256], start=True, stop=False)
                    if n < i:
                        P.mm(ps_[:, kt, :], esel[:, n, :], SBT[:, h, q0:q0 + 256], start=False, stop=True)
                    else:
                        P.mm(ps_[:, kt, :], ident[:], cb[:, kt, :], start=False, stop=True)
                P.act(PT[idx % 3][:], ps_[:], AF.Exp)

            def emit_PV(idx):
                i, n = its[idx]
                q0 = i * 256
                pt = PT[idx % 3]
                po = [pO[(i % 2) * 2 + qs] for qs in range(2)]
                for qs in range(2):
                    for kt in range(2):
                        P.mm(po[qs][:, 0:129], pt[:, kt, qs * 128:(qs + 1) * 128], VP[:, n * 2 + kt, h, 0:129],
                             start=(n == 0 and kt == 0), stop=(n == i and kt == 1))
                if n == i:
                    for qs in range(2):
                        k_ = (i % 2) * 2 + qs
                        P.op("vector", (lambda r, p_: lambda e: e.reciprocal(r.h[:], p_.h[:, 128:129]))(rec[k_], po[qs]),
                             reads=[po[qs]], writes=[rec[k_]])
                        P.ts(on[k_][:], po[qs][:, 0:128], rec[k_][:, 0:1], ALU.mult)
                        pend.append((idx + 2, k_, q0 + qs * 128))

            def flush(idx, force=False):
                while pend and (force or pend[0][0] <= idx):
                    _, k_, c0 = pend.pop(0)
                    cc = 256 + (k_ % 2) * 128
                    P.transpose(ptr[:, cc:cc + 128], on[k_][:], ident[:])
                    P.copy(ot[:, c0:c0 + 128], ptr[:, cc:cc + 128], eng="scalar")

            for idx in range(len(its) + 1):
                if idx < len(its):
                    emit_S(idx)
                if idx >= 1:
                    emit_PV(idx - 1)
                flush(idx)
            flush(0, force=True)
            for s_ in range(2):
                P.dma(oT(h * 128, (h + 1) * 128, s_), ot[:, s_ * 2048:(s_ + 1) * 2048])
        P.barrier()
    outer.close()


def build_moba():
    P = Prog()
    S = S_LEN
    xT = P.dram("xT", [1024, S], F32, "ExternalInput")
    wq = P.dram("wq", [1024, 512], F32, "ExternalInput")
    wk = P.dram("wk", [1024, 512], F32, "ExternalInput")
    wv = P.dram("wv", [1024, 512], F32, "ExternalInput")
    gaT = P.dram("gaT", [128, 8], F32, "ExternalInput")
    qn2 = P.dram("qn2", [128, 2], F32, "ExternalInput")
    kn2 = P.dram("kn2", [128, 2], F32, "ExternalInput")
    cosT = P.dram("cosT", [128, S], F32, "ExternalInput")
    sinT = P.dram("sinT", [128, S], F32, "ExternalInput")
    perm = P.dram("perm", [128, 128], BF16, "ExternalInput")
    ident = P.dram("ident", [128, 128], BF16, "ExternalInput")
    cb = P.dram("cb", [128, 512], BF16, "ExternalInput")
    esel = P.dram("esel", [16, 2048], BF16, "ExternalInput")
    pb = P.dram("pb", [128, 256], F32, "ExternalInput")
    oT = P.dram("oT", [512, S], BF16, "ExternalOutput")
    moba_phase(P, xT, wq, wk, wv, gaT, qn2, kn2, cosT, sinT, perm, ident, cb, esel, pb,
               lambda r0, r1, s_: V(oT, oT.h[r0:r1, s_ * 2048:(s_ + 1) * 2048]))
    P.wait_ticket("sync", oT.w)
    P.barrier()
    return P.emit()


def moba_inputs(x1T_b, hh, d):
    c = moba_consts()
    wqkv = d["w_qkv"][0]
    qn = d["q_norm"][0]
    kn = d["k_norm"][0]
    pidx = (np.arange(128) + 64) % 128
    ins = dict(wq=np.ascontiguousarray(wqkv[:, hh * 512:(hh + 1) * 512]),
               wk=np.ascontiguousarray(wqkv[:, 1024 + hh * 512:1024 + (hh + 1) * 512]),
               wv=np.ascontiguousarray(wqkv[:, 2048 + hh * 512:2048 + (hh + 1) * 512]),
               gaT=vecT(d["attn_norm"][1]),
               qn2=np.ascontiguousarray(np.stack([qn, qn[pidx]], 1)),
               kn2=np.ascontiguousarray(np.stack([kn, kn[pidx]], 1)))
    if x1T_b is not None:
        ins["xT"] = x1T_b
    ins.update(c)
    return ins


CH = 64
RW_LN_EPS = 64e-5


def ar_consts():
    bf = ml_dtypes.bfloat16
    s = np.arange(64)[:, None]
    t = np.arange(64)[None, :]
    strictT = (s < t).astype(np.float32)
    inclT = (s <= t).astype(np.float32)
    strict = (t < s).astype(np.float32)
    m = np.concatenate([strictT, inclT, strictT, inclT, strict], 1)
    mask = np.concatenate([m, m], 0)
    i64 = np.concatenate([np.eye(64), np.eye(64)], 0).astype(np.float32)
    bones = np.zeros((128, 128), np.float32)
    bones[:64, :64] = 1
    bones[64:, 64:] = 1
    rm = np.ones((128, 512), np.float32)
    rm[:, ::64] = 0
    return dict(mask=mask.astype(np.float32), i64=i64, bones=bones.astype(bf),
                ident=np.eye(128, dtype=np.float32).astype(bf), rm=rm)


def ar_phase(P, xT, whg, wrw, w2a2_d, g2_d, gaT, lb3_d, hv_d, rv_d, mu8_d, mask_d, i64_d, bones_d, ident_d, rm_d, oT):
    S = S_LEN
    NT = S // 512
    NC = 512 // CH
    outer = ExitStack()
    sb = lambda n, sh, dt: P.sb(n, sh, dt, outer)
    ones = sb("ones", [128, 128], BF16)
    bones = sb("bones", [128, 128], BF16)
    ident = sb("ident", [128, 128], BF16)
    mask = sb("mask", [128, 320], F32)
    i64 = sb("i64", [128, 64], F32)
    rm = sb("rm", [128, 512], F32)
    ga = sb("ga", [128, 8], F32)
    lb3 = sb("lb3", [128, 2, 3], F32)
    lbv = sb("lbv", [128, 2, 4], F32)
    hv = sb("hv", [128, 2], F32)
    rv = sb("rv", [128, 2, 8], F32)
    mu8 = sb("mu8", [128, 2, 8], F32)
    whgs = sb("whgs", [128, 8, 1024], BF16)
    wrws = sb("wrws", [128, 8, 1024], BF16)
    w2a2 = sb("w2a2", [128, 256], BF16)
    g2 = sb("g2", [128, 256], BF16)
    Ucar = sb("Ucar", [128, 8, 516], F32)
    Hb = [sb(f"Hb{i}", [128, 2, 64], BF16) for i in range(2)]
    Hg = sb("Hg", [128, 2, 64], BF16)
    Sb = [sb(f"Sb{i}", [128, 2, 128], BF16) for i in range(2)]
    Sg = sb("Sg", [128, 2, 128], BF16)
    P.memset(ones[:], 1.0)
    P.memset(Ucar[:], 0.0)
    for t_ in Hb + Sb:
        P.memset(t_[:], 0.0)
    for dst, src in ((bones, bones_d), (ident, ident_d), (mask, mask_d), (i64, i64_d), (rm, rm_d), (ga, gaT)):
        P.dma(dst[:], src[:])
    P.dma(lb3[:], V(lb3_d, lb3_d.h[:, :].rearrange("p (h j) -> p h j", h=2)))
    P.dma(hv[:], hv_d[:])
    P.dma(rv[:, :, 0:7], V(rv_d, rv_d.h[:, :].rearrange("p (h j) -> p h j", h=2)))
    P.dma(mu8[:, 0, :], mu8_d[:])
    P.dma(whgs[:], wview(whg, 0, 8, 0, 1024), eng="gpsimd")
    P.dma(wrws[:], wview(wrw, 0, 8, 0, 1024), eng="gpsimd")
    P.dma(w2a2[:], w2a2_d[:], eng="gpsimd")
    P.dma(g2[:], g2_d[:], eng="gpsimd")
    P.act(lb3[:], lb3[:], AF.Exp)
    P.reduce(lbv[:, :, 2], lb3[:], ALU.add, AX.X)
    P.op("vector", lambda e: e.reciprocal(lbv.h[:, :, 3], lbv.h[:, :, 2]), reads=[lbv], writes=[lbv])
    P.tt(lbv[:, :, 0], lb3[:, :, 0], lbv[:, :, 3], ALU.mult)
    P.ts(lbv[:, :, 1], lbv[:, :, 0], -1.0, ALU.mult, 1.0, ALU.add)
    P.ts(rv[:, :, 7], rv[:, :, 3], -1.0, ALU.mult, 1.0, ALU.add)
    P.ts(mu8[:, 1, :], mu8[:, 0, :], -1.0, ALU.mult, 1.0, ALU.add)

    def f32(n, stack):
        return P.sb(n, [128, 512], F32, stack)

    def b16(n, stack):
        return P.sb(n, [128, 512], BF16, stack)

    for t in range(DBG.get('ar_nt', NT)):
        tile = ExitStack()
        QtT = [b16(f"QtT{h}", tile) for h in range(2)]
        KtT = [b16(f"KtT{h}", tile) for h in range(2)]
        QbT = [b16(f"QbT{h}", tile) for h in range(2)]
        KhT = [b16(f"KhT{h}", tile) for h in range(2)]
        VTh = [b16(f"VTh{h}", tile) for h in range(2)]
        SGt = [b16(f"SGt{h}", tile) for h in range(2)]
        E3h = [f32(f"E3h{h}", tile) for h in range(2)]
        OAt = [f32(f"OAt{h}", tile) for h in range(2)]
        AR = [P.sb(f"AR{p}", [128, NC, 2, CH], BF16, tile) for p in range(2)]
        BT = [b16(f"BT{p}", tile) for p in range(2)]
        KT = [b16(f"KT{p}", tile) for p in range(2)]
        VT = [b16(f"VT{p}", tile) for p in range(2)]
        VF = [f32(f"VF{p}", tile) for p in range(2)]
        RKb = [b16(f"RKb{p}", tile) for p in range(2)]
        GT = [b16(f"GT{p}", tile) for p in range(2)]
        E1 = [f32(f"E1{p}", tile) for p in range(2)]
        YT = [f32(f"YT{p}", tile) for p in range(2)]
        with ExitStack() as sp:
            H = P.sb("H", [128, 8, 512], BF16, sp)
            rstd = f32("rstd", sp)
            pq = [P.ps(f"pq{i}", [128, 512], F32, sp) for i in range(2)]
            pm = [P.ps(f"pm{i}", [128, 512], F32, sp) for i in range(3)]
            bk7bb = P.ps("bk7b", [128, 1024], BF16, sp)
            bk7b = bk7bb[:, 0:256].re("p (a j) -> p a j", a=2)
            bkH = [P.ps(f"bkH{h}", [128, 512], F32, sp) for h in range(2)]
            hs = [slice(0, 64), slice(64, 128)]
            with ExitStack() as sn:
                X = P.sb("X", [128, 8, 512], F32, sn)
                sq = P.sb("sq", [128, 8, 512], BF16, sn)
                P.dma(X[:], tview(xT, 8, t * 512, 512))
                P.act(sq[:], X[:], AF.Square)
                for c in range(8):
                    P.mm(pm[0][:], ones[:], sq[:, c, :], start=c == 0, stop=c == 7)
                rms_rstd(P, pm[0][:], rstd[:], 1024.0)
                for c in range(8):
                    P.stt(H[:, c, :], X[:, c, :], ga[:, c:c + 1], rstd[:], ALU.mult, ALU.mult)
                P.barrier()
            WA = b16("WA", sp)
            sgT = b16("sgT", sp)
            tmpH = [[f32(f"th{h}{i}", sp) for i in range(6)] for h in range(2)]
            tmpR = [[f32(f"tr{p}{i}", sp) for i in range(6)] for p in range(2)]
            tbR = [b16(f"tb{p}", sp) for p in range(2)]
            TOKH = P.sb("TOKH", [128, 2, 128], BF16, sp)
            AtH = P.sb("AtH", [128, 64], BF16, sp)
            pqi = [0]

            def proj(w, ct):
                a = pq[pqi[0] % 2]
                pqi[0] += 1
                for kc in range(8):
                    P.mm(a[:], w[:, kc, ct * 128:(ct + 1) * 128], H[:, kc, :], start=kc == 0, stop=kc == 7)
                return a

            done = {}

            def hgrn_prep(h):
                tmp = tmpH[h]
                aq = proj(whgs, 0 + h)
                qs = tmp[0]
                P.act(qs[:], aq[:], AF.Silu)
                yield
                af = proj(whgs, 2 + h)
                f = tmp[1]
                P.act(f[:], af[:], AF.Sigmoid)
                yield
                P.ts(f[:], f[:], lbv[:, h, 1:2], ALU.mult, lbv[:, h, 0:1], ALU.add)
                yield
                lf = tmp[2]
                P.act(lf[:], f[:], AF.Ln)
                kq = tmp[3]
                P.ts(kq[:], f[:], -1.0, ALU.mult, 1.0, ALU.add, eng="gpsimd")
                yield
                b = tmp[4]
                P.op("vector", (lambda b, lf: lambda e: e.tensor_tensor_scan(b.h[:], rm.h[:], lf.h[:], 0.0, ALU.mult, ALU.add))(b, lf),
                     reads=[rm, lf], writes=[b])
                yield
                b3 = b[:].re("p (c j) -> p c j", j=CH)
                d = tmp[5]
                P.tt(d[:].re("p (c j) -> p c j", j=CH), b3, V(b, b.h[:, :].rearrange("p (c j) -> p c j", j=CH)[:, :, 31:32].broadcast_to([128, NC, CH])), ALU.subtract)
                P.act(E3h[h][:], b[:], AF.Exp)
                yield
                e1 = tmp[2]
                e2 = tmp[1]
                P.act(e1[:], d[:], AF.Exp)
                P.act(e2[:], d[:], AF.Exp, scale=-1.0)
                yield
                P.tt(QtT[h][:], qs[:], e1[:], ALU.mult, eng="gpsimd")
                P.tt(KtT[h][:], kq[:], e2[:], ALU.mult)
                yield
                P.tt(QbT[h][:], qs[:], E3h[h][:], ALU.mult)
                yield
                P.tt(d[:].re("p (c j) -> p c j", j=CH), b3, V(b, b.h[:, :].rearrange("p (c j) -> p c j", j=CH)[:, :, 63:64].broadcast_to([128, NC, CH])), ALU.subtract)
                yield
                P.act(e1[:], d[:], AF.Exp, scale=-1.0)
                yield
                P.tt(KhT[h][:], kq[:], e1[:], ALU.mult, eng="gpsimd")
                ai = proj(whgs, 4 + h)
                P.copy(VTh[h][:], ai[:], eng="scalar")
                yield
                ag = proj(whgs, 6 + h)
                P.act(SGt[h][:], ag[:], AF.Silu)
                done[("h", h)] = True
                yield

            def hgrn_gen():
                while not (done.get(("h", 0)) and done.get(("h", 1))):
                    yield
                for c in range(DBG.get('ar_nc', NC) if DBG.get('ar_hg', True) else 0):
                    o = c * CH
                    gcol = o + CH - 1
                    gi = t * NC + c
                    Sb0, Sb1 = Sb[gi % 2], Sb[(gi + 1) % 2]
                    for h in range(2):
                        P.ts(Sg[:, h, :], Sb0[:, h, :], E3h[h][:, gcol:gcol + 1], ALU.mult, eng="gpsimd")
                        P.transpose(bk7b[hs[h], 0, :], KhT[h][:, o:o + CH], ident[:])
                        P.transpose(bk7b[hs[h], 1, :], VTh[h][:, o:o + CH], ident[:])
                    yield
                    P.copy(TOKH[:], bk7b[:], eng="scalar")
                    for h in range(2):
                        P.mm(bkH[0][hs[h], 192:256], KtT[h][:, o:o + CH], QtT[h][:, o:o + CH])
                    yield
                    P.tt(AtH[:], bkH[0][:, 192:256], mask[:, 64:128], ALU.mult)
                    yield
                    for h in range(2):
                        P.mm(bkH[h][:, 0:64], TOKH[hs[h], 1, :], AtH[hs[h], :], start=True, stop=False)
                        P.mm(bkH[h][:, 0:64], Sb0[:, h, :], QbT[h][:, o:o + CH], start=False, stop=True)
                    for h in range(2):
                        P.mm(bkH[h][:, 64:192], TOKH[hs[h], 0, :], TOKH[hs[h], 1, :])
                    yield
                    for h in range(2):
                        P.copy(OAt[h][:, o:o + CH], bkH[h][:, 0:64], eng="scalar")
                    for h in range(2):
                        P.tt(Sb1[:, h, :], bkH[h][:, 64:192], Sg[:, h, :], ALU.add)
                    yield

            def shifted(ct):
                a = proj(wrws, ct)
                U = Ucar[:, ct, :]
                P.copy(U[:, 3:4], U[:, 515:516], eng="gpsimd")
                P.copy(U[:, 4:516], a[:], eng="scalar")
                return U

            def mix(dst, U, ct, eng="vector"):
                P.ts(dst, U[:, 4:516], mu8[:, 1, ct:ct + 1], ALU.mult, eng=eng)
                P.stt(dst, U[:, 3:515], mu8[:, 0, ct:ct + 1], dst, ALU.mult, ALU.add)

            def rwkv_pre():
                U6 = shifted(6)
                wm = tmpR[0][5]
                mix(wm[:], U6, 6)
                yield
                P.act(WA[0:64, :], wm[0:64, :], AF.Tanh)
                P.copy(WA[64:128, :], wm[64:128, :], eng="scalar")
                yield
                U7 = shifted(7)
                wm2 = tmpR[1][5]
                mix(wm2[:], U7, 7)
                yield
                P.act(sgT[:], wm2[:], AF.Sigmoid)
                done["pre"] = True
                yield

            def rwkv_prep(p):
                rM, kM, kk, a_, t4, t5 = tmpR[p]
                tb0 = tbR[p]
                mix(rM[:], shifted(0 + p), 0 + p)
                yield
                mix(kM[:], shifted(2 + p), 2 + p)
                yield
                mix(VF[p][:], shifted(4 + p), 4 + p)
                P.copy(VT[p][:], VF[p][:], eng="gpsimd")
                yield
                P.ts(kk[:], kM[:], rv[:, p, 2:3], ALU.mult)
                P.act(tb0[:], kk[:], AF.Square)
                yield
                while not done.get("pre"):
                    yield
                pz_ = pm[1 + p]
                P.mm(pz_[:], w2a2[0:64, p * 128:(p + 1) * 128], WA[0:64, :])
                ld = t4
                P.act(ld[:], pz_[:], AF.Sigmoid, bias=rv[:, p, 0:1])
                yield
                P.ts(ld[:], ld[:], -float(np.exp(-0.5)), ALU.mult)
                P.mm(pz_[:], w2a2[64:128, p * 128:(p + 1) * 128], WA[64:128, :])
                P.act(a_[:], pz_[:], AF.Sigmoid, bias=rv[:, p, 1:2])
                yield
                P.mm(pz_[:], g2[:, p * 128:(p + 1) * 128], sgT[:])
                P.copy(GT[p][:], pz_[:], eng="scalar")
                yield
                P.mm(pz_[:], bones[:], tb0[:])
                P.act(t5[:], pz_[:], AF.Ln, bias=1e-12)
                yield
                P.act(t5[:], t5[:], AF.Exp, scale=-0.5)
                yield
                P.tt(kk[:], kk[:], t5[:], ALU.mult)
                yield
                P.ts(t5[:], a_[:], rv[:, p, 3:4], ALU.mult, rv[:, p, 7:8], ALU.add)
                yield
                P.tt(kM[:], kM[:], t5[:], ALU.mult)
                yield
                P.stt(RKb[p][:], rM[:], rv[:, p, 4:5], kM[:], ALU.mult, ALU.mult)
                cs = t5
                P.op("vector", (lambda cs, ld: lambda e: e.tensor_tensor_scan(cs.h[:], rm.h[:], ld.h[:], 0.0, ALU.mult, ALU.add))(cs, ld),
                     reads=[rm, ld], writes=[cs])
                yield
                P.act(E1[p][:], cs[:], AF.Exp)
                P.tt(ld[:], cs[:], ld[:], ALU.subtract)
                yield
                AR4 = AR[p]
                P.tt(AR4[:, :, 1, :], rM[:].re("p (c j) -> p c j", j=CH), E1[p][:].re("p (c j) -> p c j", j=CH), ALU.mult)
                P.act(ld[:], ld[:], AF.Exp)
                yield
                P.stt(AR4[:, :, 0, :], kk[:].re("p (c j) -> p c j", j=CH), -1.0, ld[:].re("p (c j) -> p c j", j=CH), ALU.mult, ALU.mult)
                P.act(cs[:], cs[:], AF.Exp, scale=-1.0)
                yield
                P.tt(kk[:], kk[:], a_[:], ALU.mult, eng="gpsimd")
                P.tt(KT[p][:], kM[:], cs[:], ALU.mult)
                yield
                P.tt(BT[p][:], kk[:], cs[:], ALU.mult)
                yield

            gens = [hgrn_prep(0), rwkv_pre(), hgrn_prep(1), rwkv_prep(0), rwkv_prep(1), hgrn_gen()]
            while gens:
                for g_ in list(gens):
                    try:
                        next(g_)
                    except StopIteration:
                        gens.remove(g_)
            P.barrier()
        hs = [slice(0, 64), slice(64, 128)]
        NG = NC // 4
        keep = ExitStack()
        TOKg = [[P.sb(f"TOKg{p}{g}", [128, 4, 4, 64], BF16, keep) for g in range(NG)] for p in range(2)]
        SCbg = [[P.sb(f"SCbg{p}{g}", [128, 4, 320], BF16, keep) for g in range(NG)] for p in range(2)]
        WhTg = [[P.sb(f"WhTg{p}{g}", [128, 4, 64], BF16, keep) for g in range(NG)] for p in range(2)]
        UHg = [[P.sb(f"UHg{p}{g}", [128, 4, 64], F32, keep) for g in range(NG)] for p in range(2)]
        with ExitStack() as sc:
            bTb = P.ps("bT", [128, 1024], BF16, sc)
            bT = bTb[:].re("p (q j d) -> p q j d", q=4, j=4)
            bA = [P.ps(f"bA{i}", [128, 512], F32, sc) for i in range(2)]
            bN = P.ps("bN", [128, 512], F32, sc)
            bQ = P.ps("bQ", [128, 512], F32, sc)
            bS = [P.ps(f"bS{p}", [128, 512], F32, sc) for p in range(2)]
            pzh = P.ps("pzh", [128, 512], F32, sc)
            Ub = [P.sb(f"Ub{p}", [128, 64], BF16, sc) for p in range(2)]
            Tg = [P.sb(f"Tg{i}", [128, 4, 64], BF16, sc) for i in range(2)]
            PQg = [P.sb(f"PQg{i}", [128, 4, 128], BF16, sc) for i in range(2)]
            Zb = P.sb("Zb", [128, 4, 64], BF16, sc)

            def bc4(v, n):
                return V(v.tile, v.ap.unsqueeze(1).broadcast_to([128, n, v.ap.shape[-1]]))

            def stage1(p, g):
                tok, scb = TOKg[p][g], SCbg[p][g]
                cs_ = [g * 4 + q for q in range(4)]
                aT = [AR[p][:, c, 0, :] for c in cs_]
                arT = [AR[p][:, c, :, :].re("p a j -> p (a j)") for c in cs_]
                bT_ = [BT[p][:, c * CH:(c + 1) * CH] for c in cs_]
                kT_ = [KT[p][:, c * CH:(c + 1) * CH] for c in cs_]
                vT_ = [VT[p][:, c * CH:(c + 1) * CH] for c in cs_]
                for h in range(2):
                    for q in range(4):
                        for j, xx in enumerate((aT[q], bT_[q], kT_[q], vT_[q])):
                            P.transpose(bT[hs[h], q, j, :], xx[hs[h], :], ident[hs[h], hs[h]])
                    for q in range(4):
                        ba = bA[q // 2]
                        o_ = (q % 2) * 256
                        P.mm(ba[hs[h], o_:o_ + 128], bT_[q][hs[h], :], arT[q][hs[h], :])
                        P.mm(ba[hs[h], o_ + 128:o_ + 256], kT_[q][hs[h], :], arT[q][hs[h], :])
                        P.mm(bN[hs[h], q * 64:(q + 1) * 64], aT[q][hs[h], :], bT_[q][hs[h], :])
                P.copy(tok[:], bT, eng="scalar")
                for k in range(2):
                    P.tt(scb[:, 2 * k:2 * k + 2, 0:256], bA[k][:].re("p (q n) -> p q n", q=2), bc4(mask[:, 0:256], 2), ALU.mult)
                P.tt(scb[:, :, 256:320], bN[:, 0:256].re("p (q n) -> p q n", q=4), bc4(mask[:, 256:320], 4), ALU.mult)
                P.tt(Tg[0][:], scb[:, :, 0:64], bc4(i64[:], 4), ALU.add)
                Pm = [scb[:, q, 256:320] for q in range(4)]
                Qm = [scb[:, q, 0:64] for q in range(4)]
                sqb = [bQ, bA[1]]
                tub = [bN, bA[0]]
                v4 = lambda x: x.re("p (q n) -> p q n", q=4)
                for j in range(5):
                    pq_ = PQg[j % 2]
                    for h in range(2):
                        for q in range(4):
                            P.mm(sqb[h][hs[h], q * 128:q * 128 + 64], Qm[q][hs[h], :], Pm[q][hs[h], :])
                            P.mm(sqb[h][hs[h], q * 128 + 64:q * 128 + 128], Pm[q][hs[h], :], Qm[q][hs[h], :])
                    P.copy(pq_[hs[0]], v4(sqb[0][hs[0], :]), eng="scalar")
                    P.copy(pq_[hs[1]], v4(sqb[1][hs[1], :]), eng="vector")
                    Pm = [pq_[:, q, 0:64] for q in range(4)]
                    Qm = [pq_[:, q, 64:128] for q in range(4)]
                    To, Tn = Tg[j % 2], Tg[(j + 1) % 2]
                    for h in range(2):
                        for q in range(4):
                            P.mm(tub[h][hs[h], 256 + q * 64:256 + (q + 1) * 64], Pm[q][hs[h], :], To[hs[h], q, :])
                    for h in range(2):
                        P.tt(Tn[hs[h]], v4(tub[h][hs[h], 256:512]), To[hs[h]], ALU.add)
                Tf = Tg[5 % 2]
                for h in range(2):
                    for q in range(4):
                        P.mm(tub[h][hs[h], q * 64:(q + 1) * 64], scb[hs[h], q, 128:192], tok[hs[h], q, 3, :])
                P.copy(Zb[hs[0]], v4(tub[0][hs[0], 0:256]), eng="scalar")
                P.copy(Zb[hs[1]], v4(tub[1][hs[1], 0:256]), eng="vector")
                for h in range(2):
                    for q in range(4):
                        P.mm(tub[h][hs[h], 256 + q * 64:256 + (q + 1) * 64], Tf[hs[h], q, :], Zb[hs[h], q, :])
                    for q in range(4):
                        P.mm(sqb[h][hs[h], q * 64:(q + 1) * 64], tok[hs[h], q, 0, :], Tf[hs[h], q, :])
                P.copy(UHg[p][g][hs[0]], v4(tub[0][hs[0], 256:512]), eng="scalar")
                P.copy(UHg[p][g][hs[1]], v4(tub[1][hs[1], 256:512]), eng="vector")
                P.copy(WhTg[p][g][hs[0]], v4(sqb[0][hs[0], 0:256]), eng="scalar")
                P.copy(WhTg[p][g][hs[1]], v4(sqb[1][hs[1], 0:256]), eng="vector")

            def stage2_gen(g):
                for q in range(4):
                    c = g * 4 + q
                    o = c * CH
                    gcol = o + CH - 1
                    gi = t * NC + c
                    Hb0, Hb1 = Hb[gi % 2], Hb[(gi + 1) % 2]
                    UO = [(0, 0), (1, 1), (0, 1), (1, 0)]
                    for p in range(2):
                        P.ts(Hg[:, p, :], Hb0[:, p, :], E1[p][:, gcol:gcol + 1], ALU.mult, eng="gpsimd")
                    for p, h in UO:
                        P.mm(bS[p][hs[h], 0:64], WhTg[p][g][hs[h], q, :], Hb0[hs[h], p, :])
                    yield
                    for p in range(2):
                        P.tt(Ub[p][:], bS[p][:, 0:64], UHg[p][g][:, q, :], ALU.add)
                    yield
                    for p, h in UO:
                        tok, scb = TOKg[p][g], SCbg[p][g]
                        r_T = AR[p][:, c, 1, :]
                        P.mm(bS[p][hs[h], 64:128], Hb0[hs[h], p, :], r_T[hs[h], :], start=True, stop=False)
                        P.mm(bS[p][hs[h], 64:128], Ub[p][hs[h], :], scb[hs[h], q, 64:128], start=False, stop=False)
                        P.mm(bS[p][hs[h], 64:128], tok[hs[h], q, 3, :], scb[hs[h], q, 192:256], start=False, stop=True)
                    for p, h in UO:
                        tok = TOKg[p][g]
                        P.mm(bS[p][hs[h], 128:192], tok[hs[h], q, 1, :], Ub[p][hs[h], :], start=True, stop=False)
                        P.mm(bS[p][hs[h], 128:192], tok[hs[h], q, 2, :], tok[hs[h], q, 3, :], start=False, stop=True)
                    yield
                    for p in range(2):
                        P.stt(Hb1[:, p, :], bS[p][:, 128:192], E1[p][:, gcol:gcol + 1], Hg[:, p, :], ALU.mult, ALU.add)
                        P.copy(YT[p][:, o:o + CH], bS[p][:, 64:128], eng="scalar")
                    yield

            rw = DBG.get('ar_rw', True)
            for g in range(NG if rw else 0):
                for p in range(2):
                    stage1(p, g)
                P.set_bg(stage2_gen(g), 24)
            tah = [f32(f"tah{i}", sc) for i in range(2)]
            tbh = b16("tbh", sc)
            obh = [b16(f"obh{i}", sc) for i in range(2)]
            for h in range(2):
                P.act(tbh[:], OAt[h][:], AF.Square)
                P.mm(pzh[:], ones[:], tbh[:])
                rms_rstd(P, pzh[:], tah[0][:], 128.0)
                P.stt(tah[1][:], OAt[h][:], hv[:, h:h + 1], tah[0][:], ALU.mult, ALU.mult)
                P.tt(obh[h][:], tah[1][:], SGt[h][:], ALU.mult)
                P.dma(oT(h * 128, (h + 1) * 128, t), obh[h][:])
            P.flush_bg()
            P.barrier()
        keep.close()
        with ExitStack() as so:
            pz = [P.ps(f"pz{i}", [128, 512], F32, so) for i in range(3)]
            ta = [f32(f"ta{i}", so) for i in range(3)]
            tbb = [b16(f"tbb{i}", so) for i in range(2)]
            ob = [b16(f"ob{i}", so) for i in range(4)]
            for p in range(2):
                y = YT[p]
                P.copy(tbb[0][:], y[:], eng="scalar")
                P.mm(pz[0][:], bones[:], tbb[0][:])
                P.stt(ta[0][:], pz[0][:], -1.0 / 64.0, y[:], ALU.mult, ALU.add)
                P.act(tbb[1][:], ta[0][:], AF.Square)
                P.mm(pz[1][:], bones[:], tbb[1][:])
                rms_rstd(P, pz[1][:], ta[1][:], 64.0, eps=RW_LN_EPS)
                P.tt(ta[0][:], ta[0][:], ta[1][:], ALU.mult)
                P.ts(ta[0][:], ta[0][:], rv[:, p, 5:6], ALU.mult, rv[:, p, 6:7], ALU.add)
                P.mm(pz[2][:], bones[:], RKb[p][:])
                P.tt(ta[2][:], pz[2][:], VF[p][:], ALU.mult)
                P.tt(ta[0][:], ta[0][:], ta[2][:], ALU.add)
                P.tt(ob[2 + p][:], ta[0][:], GT[p][:], ALU.mult)
                P.dma(oT(256 + p * 128, 256 + (p + 1) * 128, t), ob[2 + p][:])
            P.barrier()
        tile.close()
    outer.close()


def build_ar():
    P = Prog()
    S = S_LEN
    d = lambda n, sh, dt=F32: P.dram(n, sh, dt, "ExternalInput")
    xT = d("xT", [1024, S])
    whg = d("whg", [1024, 1024])
    wrw = d("wrw", [1024, 1024])
    w2a2 = d("w2a2", [128, 256])
    g2 = d("g2", [128, 256])
    gaT = d("gaT", [128, 8])
    lb3 = d("lb3", [128, 6])
    hv = d("hv", [128, 2])
    rv = d("rv", [128, 14])
    mu8 = d("mu8", [128, 8])
    mask = d("mask", [128, 320])
    i64 = d("i64", [128, 64])
    bones = d("bones", [128, 128], BF16)
    ident = d("ident", [128, 128], BF16)
    rm = d("rm", [128, 512])
    oT = P.dram("oT", [512, S], BF16, "ExternalOutput")
    ar_phase(P, xT, whg, wrw, w2a2, g2, gaT, lb3, hv, rv, mu8, mask, i64, bones, ident, rm,
             lambda r0, r1, t: V(oT, oT.h[r0:r1, t * 512:(t + 1) * 512]))
    P.wait_ticket("sync", oT.w)
    P.barrier()
    return P.emit()


def ar_inputs(xT_b, hh, d):
    w = d["w_in_ar"][0]
    c = lambda a, b: w[:, a:b]
    h0 = hh * 256
    whg = np.concatenate([c(h0, h0 + 256), c(512 + h0, 512 + h0 + 256), c(1024 + h0, 1024 + h0 + 256),
                          c(1536 + h0, 1536 + h0 + 256)], 1)
    wrw = np.concatenate([c(2048 + h0, 2048 + h0 + 256), c(2560 + h0, 2560 + h0 + 256), c(3072 + h0, 3072 + h0 + 256),
                          c(3584, 3840)], 1)
    mu = d["rwkv_mu"][0]
    mu8 = np.concatenate([mu[h0:h0 + 256], mu[512 + h0:512 + h0 + 256], mu[1024 + h0:1024 + h0 + 256], mu[1536:1792]])
    mu8 = np.ascontiguousarray(mu8.reshape(8, 128).T)
    w2a2 = np.concatenate([d["rwkv_w2"][0][:, h0:h0 + 256], d["rwkv_a2"][0][:, h0:h0 + 256]], 0)
    g2 = d["rwkv_g2"][0][:, h0:h0 + 256]
    lb = d["hgrn_lb"]
    lb3 = np.stack([lb[:, h0 + h * 128:h0 + (h + 1) * 128].T for h in range(2)], 1).reshape(128, 6)
    hv = np.stack([d["hgrn_onorm"][0][h0 + h * 128:h0 + (h + 1) * 128] for h in range(2)], 1)
    names = ["rwkv_w0", "rwkv_a0", "rwkv_kk", "rwkv_ka", "rwkv_rk", "rwkv_ln_w", "rwkv_ln_b"]
    rv = np.stack([np.stack([d[n][0][h0 + p * 128:h0 + (p + 1) * 128] for n in names], 1) for p in range(2)], 1)
    ins = dict(xT=xT_b, whg=np.ascontiguousarray(whg), wrw=np.ascontiguousarray(wrw),
               w2a2=np.ascontiguousarray(w2a2), g2=np.ascontiguousarray(g2), gaT=vecT(d["attn_norm"][0]),
               lb3=np.ascontiguousarray(lb3), hv=np.ascontiguousarray(hv),
               rv=np.ascontiguousarray(rv.reshape(128, 14)), mu8=mu8)
    ins.update(ar_consts())
    return ins


RG = [[0, 1], [2, 3], [4, 5], [6, 7]]
ROWMAP0 = [0, 128, 512, 640, 256, 384, 768, 896]
ROWMAP1 = [0, 128, 256, 384, 512, 640, 768, 896]


def allgather_pairs(P, src, dst):
    P.barrier()
    s_, d_ = src.h, dst.h
    P.op("gpsimd", lambda e: e.collective_compute("AllGather", ALU.bypass, replica_groups=RG,
                                                  ins=[s_.opt()], outs=[d_.opt()]),
         reads=[src], writes=[dst], cc=True)
    P.barrier()


def build_fused():
    P = Prog()
    S = S_LEN
    T = S // 2
    all_in = []

    def i_(n, sh, dt=F32):
        t = P.dram(n, sh, dt, "ExternalInput")
        all_in.append((t, dt))
        return t

    def touch():
        s32 = P.sb("s32", [1, 4], F32)
        s16 = P.sb("s16", [1, 4], BF16)
        for t, dt in all_in:
            P.dma((s32 if dt == F32 else s16)[0:1, 0:2], V(t, t.h[0:1, 0:2]))

    t_ = lambda n, sh, dt: P.dram(n, sh, dt, "Internal")
    a_xT = i_("a_xT", [1024, S])
    a_whg = i_("a_whg", [1024, 1024])
    a_wrw = i_("a_wrw", [1024, 1024])
    a_w2a2 = i_("a_w2a2", [128, 256])
    a_g2 = i_("a_g2", [128, 256])
    a_gaT = i_("a_gaT", [128, 8])
    a_lb3 = i_("a_lb3", [128, 6])
    a_hv = i_("a_hv", [128, 2])
    a_rv = i_("a_rv", [128, 14])
    a_mu8 = i_("a_mu8", [128, 8])
    c_mask = i_("c_mask", [128, 320])
    c_i64 = i_("c_i64", [128, 64])
    c_bones = i_("c_bones", [128, 128], BF16)
    c_ident = i_("c_ident", [128, 128], BF16)
    c_rm = i_("c_rm", [128, 512])
    sel = i_("sel", [128, 2])
    ga1T = i_("ga1T", [128, 8])
    dn = []
    for l in range(2):
        dn.append(dict(pT=i_(f"d{l}_pT", [256, T]), wout=i_(f"d{l}_wout", [1024, 1024]), wup=i_(f"d{l}_wup", [1024, 4096]),
                       wdown=i_(f"d{l}_wdown", [4096, 1024]), wple=i_(f"d{l}_wple", [256, 1024]),
                       wgate=i_(f"d{l}_wgate", [1024, 1024]), gmT=i_(f"d{l}_gmT", [128, 8]), gpT=i_(f"d{l}_gpT", [128, 8])))
    d0_xh = i_("d0_xh", [1024, T])
    m_wq = i_("m_wq", [1024, 512])
    m_wk = i_("m_wk", [1024, 512])
    m_wv = i_("m_wv", [1024, 512])
    m_qn2 = i_("m_qn2", [128, 2])
    m_kn2 = i_("m_kn2", [128, 2])
    c_cosT = i_("c_cosT", [128, S])
    c_sinT = i_("c_sinT", [128, S])
    c_perm = i_("c_perm", [128, 128], BF16)
    c_cb = i_("c_cb", [128, 512], BF16)
    c_esel = i_("c_esel", [16, 2048], BF16)
    c_pb = i_("c_pb", [128, 256])
    yT = P.dram("yT", [1024, T], F32, "ExternalOutput")
    o0_src = [t_(f"o0_src{i}", [512, T], BF16) for i in range(2)]
    o0_all = [t_(f"o0_all{i}", [1024, T], BF16) for i in range(2)]
    x1_loc = t_("x1_loc", [1024, T], F32)
    h1_src = [t_(f"h1_src{i}", [512, T], BF16) for i in range(2)]
    h1_all = [t_(f"h1_all{i}", [1024, T], BF16) for i in range(2)]
    o1_src = [t_(f"o1_src{i}", [512, T], BF16) for i in range(2)]
    o1_all = [t_(f"o1_all{i}", [1024, T], BF16) for i in range(2)]

    ar_phase(P, a_xT, a_whg, a_wrw, a_w2a2, a_g2, a_gaT, a_lb3, a_hv, a_rv, a_mu8, c_mask, c_i64, c_bones, c_ident,
             c_rm, lambda r0, r1, t: V(o0_src[t // 4], o0_src[t // 4].h[r0:r1, (t % 4) * 512:(t % 4 + 1) * 512]))
    for i in range(2):
        allgather_pairs(P, o0_src[i], o0_all[i])
    fs = DBG.get("f_stop", 0)
    if fs:
        dbg = P.dram("dbg", [2048, S], BF16, "ExternalOutput")
    if fs == 1:
        touch()
        for i in range(2):
            P.dma(V(dbg, dbg.h[0:1024, i * T:(i + 1) * T]), o0_all[i][:])
        P.wait_ticket("sync", dbg.w)
        P.barrier()
        return P.emit(), P
    d = dn[0]
    dense_phase(P, T, d0_xh, None, d["pT"], d["wout"], d["wup"], d["wdown"], d["wple"], d["wgate"], d["gmT"], d["gpT"],
                x1_loc, o_gather=(o0_all, ROWMAP0), sel_d=sel, h_out=h1_src, ga_next_d=ga1T)
    for i in range(2):
        allgather_pairs(P, h1_src[i], h1_all[i])
    if fs == 2:
        touch()
        for i in range(2):
            P.dma(V(dbg, dbg.h[i * 1024:(i + 1) * 1024, 0:T]), h1_all[i][:])
        P.dma(yT[:], x1_loc[:])
        P.wait_ticket("sync", dbg.w)
        P.wait_ticket("sync", yT.w)
        P.barrier()
        return P.emit(), P
    moba_phase(P, None, m_wq, m_wk, m_wv, ga1T, m_qn2, m_kn2, c_cosT, c_sinT, c_perm, c_ident, c_cb, c_esel, c_pb,
               lambda r0, r1, s_: V(o1_src[s_], o1_src[s_].h[r0:r1, :]), h_all=h1_all)
    for i in range(2):
        allgather_pairs(P, o1_src[i], o1_all[i])
    if fs == 3:
        touch()
        for i in range(2):
            P.dma(V(dbg, dbg.h[0:1024, i * T:(i + 1) * T]), o1_all[i][:])
        P.wait_ticket("sync", dbg.w)
        P.barrier()
        return P.emit(), P
    d = dn[1]
    dense_phase(P, T, x1_loc, None, d["pT"], d["wout"], d["wup"], d["wdown"], d["wple"], d["wgate"], d["gmT"], d["gpT"],
                yT, o_gather=(o1_all, ROWMAP1), sel_d=sel)
    P.wait_ticket("sync", yT.w)
    P.barrier()
    return P.emit(), P


def fused_inputs(d, c, xT_b):
    b, r = c // 2, c % 2
    T = S_LEN // 2
    sl = slice(r * T, (r + 1) * T)
    a = ar_inputs(xT_b, r, d)
    m = moba_inputs(None, r, d)
    ins = dict(a_xT=xT_b, a_whg=a["whg"], a_wrw=a["wrw"], a_w2a2=a["w2a2"], a_g2=a["g2"], a_gaT=a["gaT"],
               a_lb3=a["lb3"], a_hv=a["hv"], a_rv=a["rv"], a_mu8=a["mu8"], c_mask=a["mask"], c_i64=a["i64"],
               c_bones=a["bones"], c_ident=a["ident"], c_rm=a["rm"],
               sel=np.ascontiguousarray(np.tile(np.eye(2, dtype=np.float32)[r][None, :], (128, 1))),
               ga1T=vecT(d["attn_norm"][1]), d0_xh=np.ascontiguousarray(xT_b[:, sl]),
               m_wq=m["wq"], m_wk=m["wk"], m_wv=m["wv"], m_qn2=m["qn2"], m_kn2=m["kn2"], c_cosT=m["cosT"],
               c_sinT=m["sinT"], c_perm=m["perm"], c_cb=m["cb"], c_esel=m["esel"], c_pb=m["pb"])
    wouts = [d["w_out_ar"][0], d["w_o_attn"][0]]
    for l in range(2):
        ins.update({f"d{l}_pT": np.ascontiguousarray(d["p"][l, b, sl].T), f"d{l}_wout": wouts[l], f"d{l}_wup": d["w_up"][l],
                    f"d{l}_wdown": d["w_down"][l], f"d{l}_wple": d["ple_proj"][l], f"d{l}_wgate": d["ple_gate"][l],
                    f"d{l}_gmT": vecT(d["mlp_norm"][l]), f"d{l}_gpT": vecT(d["ple_norm"][l])})
    return ins


def kernel(**inputs):
    d = {k: np.asarray(v) for k, v in inputs.items()}
    B = d["x"].shape[0]
    cores = list(range(8))
    xT = [np.ascontiguousarray(d["x"][b].T) for b in range(B)]
    nc, _ = build_fused()
    r = run_bass_kernel_spmd(nc, [fused_inputs(d, c, xT[c // 2]) for c in cores], core_ids=cores)
    out = np.empty(d["x"].shape, np.float32)
    T = S_LEN // 2
    for c in cores:
        out[c // 2, (c % 2) * T:(c % 2 + 1) * T, :] = r.results[c]["yT"].T
    return out
```

```python
from contextlib import ExitStack

import numpy as np
import ml_dtypes
import concourse.bass as bass
import concourse.mybir as mybir
from concourse.bass_utils import run_bass_kernel_spmd

F32 = mybir.dt.float32
BF16 = mybir.dt.bfloat16
AF = mybir.ActivationFunctionType
ALU = mybir.AluOpType
AX = mybir.AxisListType

ENGS = ["tensor", "vector", "scalar", "gpsimd", "sync"]
SAME_ENGINE_SYNC = True
N_DMA_SEMS = 6
CC_INC = 1


class Tile:
    def __init__(self, h, name):
        self.h = h
        self.name = name
        self.psum = False
        self.w = None
        self.r = {}

    def __getitem__(self, idx):
        return V(self, self.h[idx])

    def ap(self):
        return V(self, self.h.ap() if hasattr(self.h, "ap") and callable(self.h.ap) else self.h[:])


class V:
    def __init__(self, tile, ap):
        self.tile = tile
        self.ap = ap

    def __getitem__(self, idx):
        return V(self.tile, self.ap[idx])

    def re(self, pat, **kw):
        return V(self.tile, self.ap.rearrange(pat, **kw))

    def bc(self, shape):
        return V(self.tile, self.ap.broadcast_to(shape))


def _ap(x):
    return x.ap if isinstance(x, V) else x


class Prog:
    def __init__(self):
        self.nc = bass.Bass("TRN2", target_bir_lowering=False)
        self.es = ExitStack()
        self.ops = {e: [] for e in ENGS}
        self.cnt = {}
        self.seen = {e: {} for e in ENGS}
        self.sems = {}
        self.dma_rr = {"sync": 0, "gpsimd": 0, "scalar": 0}
        self.dma_last = {}
        self.n_ops = 0

    def sem(self, key):
        if key not in self.sems:
            self.sems[key] = self.es.enter_context(self.nc.semaphore(key))
            self.cnt[key] = 0
        return self.sems[key]

    def _uniq(self, name):
        self.uid = getattr(self, "uid", 0) + 1
        return f"{name}_u{self.uid}"

    def sb(self, name, shape, dt, stack=None):
        name = self._uniq(name)
        h = (stack or self.es).enter_context(self.nc.sbuf_tensor(name, list(shape), dt))
        return Tile(h, name)

    def ps(self, name, shape, dt=F32, stack=None):
        name = self._uniq(name)
        h = (stack or self.es).enter_context(self.nc.psum_tensor(name, list(shape), dt))
        t = Tile(h, name)
        t.psum = True
        return t

    def dram(self, name, shape, dt, kind):
        h = self.nc.dram_tensor(name, list(shape), dt, kind=kind)
        return Tile(h.ap(), name)

    def _need(self, eng, waits, tk):
        if tk is None:
            return
        key, val = tk
        if key == eng and not (SAME_ENGINE_SYNC and eng != "tensor"):
            return
        if self.seen[eng].get(key, 0) >= val:
            return
        waits[key] = max(waits.get(key, 0), val)

    def op(self, eng, fn, reads=(), writes=(), dma=False, inc=True, cc=False):
        waits = {}
        rt = []
        wt = []
        for r in reads:
            t = r.tile if isinstance(r, V) else r
            if t is not None and t not in rt:
                rt.append(t)
        for w in writes:
            t = w.tile if isinstance(w, V) else w
            if t is not None and t not in wt:
                wt.append(t)
        for t in rt:
            self._need(eng, waits, t.w)
            if t.psum:
                for k, v in t.r.items():
                    if k != eng:
                        self._need(eng, waits, (k, v))
        for t in wt:
            self._need(eng, waits, t.w)
            for k, v in t.r.items():
                self._need(eng, waits, (k, v))
        if cc:
            key = "cc"
            self.sem(key)
            incv = CC_INC
        elif dma:
            i = self.dma_rr[eng]
            self.dma_rr[eng] = (i + 1) % N_DMA_SEMS
            key = f"d_{eng}_{i}"
            self.sem(key)
            self._need(eng, waits, (key, self.cnt[key]))
            incv = 16
        else:
            key = eng
            self.sem(key)
            incv = 1
        if inc:
            self.cnt[key] += incv
            tk = (key, self.cnt[key])
        else:
            tk = (key, self.cnt[key] + incv)
        for k, v in waits.items():
            self.seen[eng][k] = v
        if not dma and inc:
            self.seen[eng][key] = max(self.seen[eng].get(key, 0), 0)
        wl = [(self.sems[k], v) for k, v in waits.items()]
        semh = self.sems[key]
        self.ops[eng].append((wl, fn, semh if inc else None, incv))
        for t in rt:
            if t not in wt:
                t.r[key] = max(t.r.get(key, 0), tk[1])
        for t in wt:
            t.w = tk
            t.r = {}
        self.n_ops += 1
        bg = getattr(self, "bg", None)
        if bg is not None and not getattr(self, "_in_bg", False):
            self._fg = getattr(self, "_fg", 0) + 1
            if self._fg % self.bg_every == 0:
                self._in_bg = True
                try:
                    next(bg)
                except StopIteration:
                    self.bg = None
                self._in_bg = False
        return tk

    def set_bg(self, gen, every):
        self.flush_bg()
        self.bg = gen
        self.bg_every = every
        self._fg = 0

    def flush_bg(self):
        bg = getattr(self, "bg", None)
        if bg is not None:
            self._in_bg = True
            for _ in bg:
                pass
            self._in_bg = False
            self.bg = None

    def wait_ticket(self, eng, tk):
        waits = {}
        self._need(eng, waits, tk)
        if waits:
            for k, v in waits.items():
                self.seen[eng][k] = v
            wl = [(self.sems[k], v) for k, v in waits.items()]
            self.ops[eng].append((wl, None, None, 0))

    def barrier(self):
        snap = dict(self.cnt)
        for e in ENGS:
            for k, v in snap.items():
                if v > 0:
                    self.wait_ticket(e, (k, v)) if k != e else None

    def emit(self):
        nc = self.nc
        with nc.Block() as block:
            def mk(eng_name):
                lst = self.ops[eng_name]

                def body(e):
                    for wl, fn, semh, incv in lst:
                        for s, v in wl:
                            e.wait_ge(s, v)
                        if fn is not None:
                            ins = fn(e)
                            if semh is not None:
                                ins.then_inc(semh, incv)
                return body
            block.tensor(mk("tensor"))
            block.vector(mk("vector"))
            block.scalar(mk("scalar"))
            block.gpsimd(mk("gpsimd"))
            block.sync(mk("sync"))
        self.es.close()
        return nc

    def dma(self, out, in_, eng="sync", **kw):
        o, i = _ap(out), _ap(in_)
        return self.op(eng, lambda e: e.dma_start(out=o, in_=i, **kw),
                       reads=[in_], writes=[out], dma=True)

    def mm(self, out, lhsT, rhs, start=True, stop=True, extra_reads=(), **kw):
        o, l, r = _ap(out), _ap(lhsT), _ap(rhs)
        return self.op("tensor", lambda e: e.matmul(o, l, r, start=start, stop=stop, **kw),
                       reads=[lhsT, rhs] + list(extra_reads), writes=[out], inc=stop)

    def transpose(self, out, in_, ident, **kw):
        o, i, d = _ap(out), _ap(in_), _ap(ident)
        return self.op("tensor", lambda e: e.transpose(o, i, d, **kw),
                       reads=[in_, ident], writes=[out])

    def act(self, out, in_, func, bias=None, scale=None, accum_out=None, eng="scalar"):
        o, i = _ap(out), _ap(in_)
        kw = {}
        reads = [in_]
        if bias is not None:
            kw["bias"] = _ap(bias)
            if isinstance(bias, V):
                reads.append(bias)
        if scale is not None:
            kw["scale"] = _ap(scale)
            if isinstance(scale, V):
                reads.append(scale)
        writes = [out]
        if accum_out is not None:
            kw["accum_out"] = _ap(accum_out)
            writes.append(accum_out)
        return self.op(eng, lambda e: e.activation(o, i, func, **kw), reads=reads, writes=writes)

    def tt(self, out, in0, in1, op, eng="vector"):
        o, a, b = _ap(out), _ap(in0), _ap(in1)
        return self.op(eng, lambda e: e.tensor_tensor(o, a, b, op), reads=[in0, in1], writes=[out])

    def ts(self, out, in0, s1, op0, s2=None, op1=None, eng="vector", accum_out=None):
        o, a = _ap(out), _ap(in0)
        reads = [in0]
        for s in (s1, s2):
            if isinstance(s, V):
                reads.append(s)
        x1, x2 = _ap(s1), _ap(s2)
        kw = {}
        writes = [out]
        if op1 is not None:
            kw["op1"] = op1
        if accum_out is not None:
            kw["accum_out"] = _ap(accum_out)
            writes.append(accum_out)
        return self.op(eng, lambda e: e.tensor_scalar(o, a, x1, x2, op0, **kw), reads=reads, writes=writes)

    def stt(self, out, in0, scalar, in1, op0, op1, eng="vector"):
        o, a, b = _ap(out), _ap(in0), _ap(in1)
        reads = [in0, in1]
        if isinstance(scalar, V):
            reads.append(scalar)
        s = _ap(scalar)
        return self.op(eng, lambda e: e.scalar_tensor_tensor(o, a, s, b, op0, op1), reads=reads, writes=[out])

    def copy(self, out, in_, eng="vector"):
        o, i = _ap(out), _ap(in_)
        if eng == "scalar":
            return self.op(eng, lambda e: e.copy(o, i), reads=[in_], writes=[out])
        return self.op(eng, lambda e: e.tensor_copy(o, i), reads=[in_], writes=[out])

    def memset(self, out, val, eng="vector"):
        o = _ap(out)
        return self.op(eng, lambda e: e.memset(o, val), reads=[], writes=[out])

    def reduce(self, out, in_, op, axis, eng="vector"):
        o, i = _ap(out), _ap(in_)
        return self.op(eng, lambda e: e.tensor_reduce(o, i, axis, op), reads=[in_], writes=[out])


EPS = 1e-6


def wview(w, r0, kc, c0, ncols):
    return V(w, w.h[r0:r0 + 128 * kc, c0:c0 + ncols].rearrange("(c p) n -> p c n", p=128))


def tview(a, kc, t0, nt, r0=0):
    return V(a, a.h[r0:r0 + 128 * kc, t0:t0 + nt].rearrange("(c p) t -> p c t", p=128))


def rms_rstd(P, pn, rstd, n_feat, eps=EPS):
    P.act(rstd, pn, AF.Ln, scale=1.0 / n_feat, bias=eps)
    P.act(rstd, rstd, AF.Exp, scale=-0.5)


def dense_phase(P, T, xT, oT, pT, wout, wup, wdown, wple, wgate, gmT, gpT, yT,
                o_gather=None, sel_d=None, h_out=None, ga_next_d=None):
    NT = T // 512
    outer = ExitStack()
    X = [P.sb(f"X{t}", [128, 8, 512], F32, outer) for t in range(NT)]
    HT = [P.sb(f"HT{t}", [128, 8, 512], BF16, outer) for t in range(NT)]
    ones = P.sb("ones", [128, 128], BF16, outer)
    gm = P.sb("gm", [128, 8], F32, outer)
    gp = P.sb("gp", [128, 8], F32, outer)
    gn = P.sb("gn", [128, 8], F32, outer)
    sq = [P.sb(f"sq{i}", [128, 8, 512], BF16, outer) for i in range(1)]
    rstd = [P.sb(f"rstd{i}", [128, 512], F32, outer) for i in range(2)]
    tmp = [P.sb(f"tmp{i}", [128, 512], F32, outer) for i in range(2)]
    wo = P.sb("wo", [128, 8, 1024], BF16, outer)
    pa = [P.ps(f"pa{i}", [128, 512], F32, outer) for i in range(4)]
    pn = [P.ps(f"pn{i}", [128, 512], F32, outer) for i in range(2)]
    P.memset(ones[:], 1.0)
    P.dma(gm[:], gmT[:])
    P.dma(gp[:], gpT[:])
    P.dma(wo[:], wview(wout, 0, 8, 0, 1024), eng="gpsimd")
    for t in range(NT):
        P.dma(X[t][:], tview(xT, 8, t * 512, 512))
        if o_gather is None:
            P.dma(HT[t][:], tview(oT, 8, t * 512, 512))
    if o_gather is not None:
        o_all, rowmap = o_gather
        with ExitStack() as s0:
            sel = P.sb("sel", [128, 2], F32, s0)
            Ab = [[P.sb(f"Ab{s_}{i}", [128, 8, 512], BF16, s0) for i in range(2)] for s_ in range(2)]
            P.dma(sel[:], sel_d[:])
            for t in range(NT):
                for s_ in range(2):
                    a = Ab[s_][t % 2]
                    for j in range(4):
                        r0 = rowmap[2 * j]
                        c0 = t * 512
                        oa = o_all[s_]
                        P.dma(a[:, 2 * j:2 * j + 2, :],
                              V(oa, oa.h[r0:r0 + 256, c0:c0 + 512].rearrange("(c p) t -> p c t", p=128)))
                a0, a1 = Ab[0][t % 2], Ab[1][t % 2]
                P.ts(a0[:], a0[:], sel[:, 0:1], ALU.mult)
                P.stt(HT[t][:], a1[:], sel[:, 1:2], a0[:], ALU.mult, ALU.add)
            P.barrier()
    pi = [0]

    def nextpa():
        pi[0] = (pi[0] + 1) % len(pa)
        return pa[pi[0]]

    for m in range(8):
        for t in range(NT):
            acc = nextpa()
            for kc in range(8):
                P.mm(acc[:], wo[:, kc, m * 128:(m + 1) * 128], HT[t][:, kc, :], start=kc == 0, stop=kc == 7)
            P.tt(X[t][:, m, :], X[t][:, m, :], acc[:], ALU.add)

    def rmsnorm_to(dst, src, g, t):
        s = sq[0]
        P.act(s[:], src[:], AF.Square)
        n = pn[t % 2]
        for c in range(8):
            P.mm(n[:], ones[:], s[:, c, :], start=c == 0, stop=c == 7)
        r = rstd[t % 2]
        rms_rstd(P, r[:], n[:], 1024.0) if False else rms_rstd(P, n[:], r[:], 1024.0)
        return r

    for t in range(NT):
        r = rmsnorm_to(None, X[t], gm, t)
        for c in range(8):
            P.stt(HT[t][:, c, :], X[t][:, c, :], gm[:, c:c + 1], r[:], ALU.mult, ALU.mult)

    with ExitStack() as s1:
        A = [P.sb(f"A{t}", [128, 4, 512], BF16, s1) for t in range(NT)]
        wu = [P.sb(f"wu{i}", [128, 8, 512], BF16, s1) for i in range(2)]
        wd = [P.sb(f"wd{i}", [128, 4, 1024], BF16, s1) for i in range(2)]
        for e in range(8):
            P.dma(wu[e % 2][:], wview(wup, 0, 8, e * 512, 512), eng="gpsimd")
            P.dma(wd[e % 2][:], wview(wdown, e * 512, 4, 0, 1024), eng="gpsimd")
            for f in range(4):
                for t in range(NT):
                    acc = nextpa()
                    for kc in range(8):
                        P.mm(acc[:], wu[e % 2][:, kc, f * 128:(f + 1) * 128], HT[t][:, kc, :],
                             start=kc == 0, stop=kc == 7)
                    tm = tmp[(f * NT + t) % 2]
                    P.act(tm[:], acc[:], AF.Square)
                    P.stt(A[t][:, f, :], acc[:], 0.0, tm[:], ALU.is_gt, ALU.mult)
            for m in range(8):
                for t in range(NT):
                    acc = nextpa()
                    for f in range(4):
                        P.mm(acc[:], wd[e % 2][:, f, m * 128:(m + 1) * 128], A[t][:, f, :],
                             start=f == 0, stop=f == 3)
                    P.tt(X[t][:, m, :], X[t][:, m, :], acc[:], ALU.add)
        P.barrier()

    with ExitStack() as s2:
        PT = P.sb("PT", [128, 2, T], BF16, s2)
        wp = P.sb("wp", [128, 2, 1024], BF16, s2)
        PP = P.sb("PP", [128, 8, 512], F32, s2)
        P.dma(PT[:], tview(pT, 2, 0, T), eng="gpsimd")
        P.dma(wp[:], wview(wple, 0, 2, 0, 1024), eng="gpsimd")
        P.dma(wo[:], wview(wgate, 0, 8, 0, 1024), eng="gpsimd")
        for t in range(NT):
            P.copy(HT[t][:], X[t][:], eng="scalar")
        for t in range(NT):
            for m in range(8):
                acc = nextpa()
                for kc in range(2):
                    P.mm(acc[:], wp[:, kc, m * 128:(m + 1) * 128], PT[:, kc, t * 512:(t + 1) * 512],
                         start=kc == 0, stop=kc == 1)
                P.copy(PP[:, m, :], acc[:], eng="scalar")
            r = rmsnorm_to(None, PP, gp, t)
            for m in range(8):
                acc = nextpa()
                for kc in range(8):
                    P.mm(acc[:], wo[:, kc, m * 128:(m + 1) * 128], HT[t][:, kc, :], start=kc == 0, stop=kc == 7)
                tm = tmp[m % 2]
                P.act(tm[:], acc[:], AF.Sigmoid)
                P.stt(PP[:, m, :], PP[:, m, :], gp[:, m:m + 1], r[:], ALU.mult, ALU.mult)
                P.tt(tm[:], tm[:], PP[:, m, :], ALU.mult)
                P.tt(X[t][:, m, :], X[t][:, m, :], tm[:], ALU.add)
            P.dma(tview(yT, 8, t * 512, 512), X[t][:])
            if h_out is not None:
                if t == 0:
                    P.dma(gn[:], ga_next_d[:])
                r = rmsnorm_to(None, X[t], gn, t)
                for c in range(8):
                    P.stt(HT[t][:, c, :], X[t][:, c, :], gn[:, c:c + 1], r[:], ALU.mult, ALU.mult)
                for f_ in range(2):
                    P.dma(tview(h_out[f_], 4, t * 512, 512), HT[t][:, 4 * f_:4 * f_ + 4, :])
        P.barrier()
    outer.close()


def build_dense(T):
    P = Prog()
    xT = P.dram("xT", [1024, T], F32, "ExternalInput")
    oT = P.dram("oT", [1024, T], BF16, "ExternalInput")
    pT = P.dram("pT", [256, T], F32, "ExternalInput")
    wout = P.dram("wout", [1024, 1024], F32, "ExternalInput")
    wup = P.dram("wup", [1024, 4096], F32, "ExternalInput")
    wdown = P.dram("wdown", [4096, 1024], F32, "ExternalInput")
    wple = P.dram("wple", [256, 1024], F32, "ExternalInput")
    wgate = P.dram("wgate", [1024, 1024], F32, "ExternalInput")
    gmT = P.dram("gmT", [128, 8], F32, "ExternalInput")
    gpT = P.dram("gpT", [128, 8], F32, "ExternalInput")
    yT = P.dram("yT", [1024, T], F32, "ExternalOutput")
    dense_phase(P, T, xT, oT, pT, wout, wup, wdown, wple, wgate, gmT, gpT, yT)
    P.wait_ticket("sync", yT.w)
    P.barrier()
    return P.emit()


def vecT(v):
    return np.ascontiguousarray(v.reshape(-1, 128).T)


S_LEN = 4096
NEG = -100.0
DBG = {}


def moba_consts():
    half = 64
    inv = np.power(10000.0, -np.arange(half, dtype=np.float32) / half).astype(np.float32)
    ang = np.arange(S_LEN, dtype=np.float32)[:, None] * inv[None, :]
    cos = np.cos(ang).astype(np.float32).T
    sin = np.sin(ang).astype(np.float32).T
    cosT = np.concatenate([cos, cos], 0)
    sinT = np.concatenate([-sin, sin], 0)
    perm = np.zeros((128, 128), np.float32)
    for dd in range(128):
        perm[(dd + 64) % 128, dd] = 1.0
    ident = np.eye(128, dtype=np.float32)
    cb = np.zeros((128, 2, 256), np.float32)
    for kt in range(2):
        k = kt * 128 + np.arange(128)[:, None]
        q = np.arange(256)[None, :]
        cb[:, kt, :] = np.where(k > q, NEG, 0.0)
    esel = np.zeros((16, 16, 128), np.float32)
    for n in range(16):
        esel[n, n, :] = 1.0
    pb = np.zeros((128, 16, 16), np.float32)
    for i in range(16):
        pb[:, i, i:] = -1e30
    bf = ml_dtypes.bfloat16
    return dict(cosT=np.ascontiguousarray(cosT), sinT=np.ascontiguousarray(sinT), perm=perm.astype(bf),
                ident=ident.astype(bf), cb=cb.reshape(128, 512).astype(bf),
                esel=esel.reshape(16, 2048).astype(bf), pb=pb.reshape(128, 256))


def moba_phase(P, xT, wq, wk, wv, gaT, qn2, kn2, cosT, sinT, perm_d, ident_d, cb_d, esel_d, pb_d, oT, h_all=None):
    S = S_LEN
    NT = S // 512
    outer = ExitStack()
    QR = [P.sb(f"QR{h}", [128, S], BF16, outer) for h in range(4)]
    KR = [P.sb(f"KR{h}", [128, S], BF16, outer) for h in range(4)]
    VP = P.sb("VP", [128, 32, 4, 130], BF16, outer)
    ones = P.sb("ones", [128, 128], BF16, outer)
    ident = P.sb("ident", [128, 128], BF16, outer)
    kmT = P.sb("kmT", [128, 4, 16], BF16, outer)
    P.memset(ones[:], 1.0)
    P.memset(VP[:], 1.0)
    P.dma(ident[:], ident_d[:])
    with ExitStack() as s1:
        ga = P.sb("ga", [128, 8], F32, s1)
        qk = P.sb("qk", [128, 4], F32, s1)
        perm = P.sb("perm", [128, 128], BF16, s1)
        Xt = [P.sb(f"Xt{i}", [128, 8, 512], F32, s1) for i in range(2)]
        Ht = [P.sb(f"Ht{i}", [128, 8, 512], BF16, s1) for i in range(2)]
        cs = [P.sb(f"cs{i}", [128, 2, 512], F32, s1) for i in range(2)]
        sq = P.sb("sq", [128, 8, 512], BF16, s1)
        rstd = P.sb("rstd", [128, 512], F32, s1)
        w3 = [P.sb(f"w3{i}", [128, 8, 512], BF16, s1) for i in range(3)]
        kb = [P.sb(f"kb{i}", [128, 512], BF16, s1) for i in range(2)]
        sk = [P.sb(f"sk{i}", [128, 512], BF16, s1) for i in range(2)]
        r2 = [P.sb(f"r2{i}", [128, 512], F32, s1) for i in range(2)]
        t1 = [P.sb(f"t1{i}", [128, 512], F32, s1) for i in range(2)]
        t2 = [P.sb(f"t2{i}", [128, 512], F32, s1) for i in range(2)]
        km32 = P.sb("km32", [128, 16], F32, s1)
        pk = [P.ps(f"pk{i}", [128, 512], F32, s1) for i in range(2)]
        pp = [P.ps(f"pp{i}", [128, 512], F32, s1) for i in range(2)]
        pn = [P.ps(f"pn{i}", [128, 512], F32, s1) for i in range(2)]
        pv = [P.ps(f"pv{i}", [128, 512], F32, s1) for i in range(2)]
        P.dma(ga[:], gaT[:])
        P.dma(qk[:, 0:2], qn2[:])
        P.dma(qk[:, 2:4], kn2[:])
        P.dma(perm[:], perm_d[:])
        P.ts(qk[:, 0:2], qk[:, 0:2], float(128 ** -0.5), ALU.mult)
        for i, w in enumerate((wq, wk, wv)):
            P.dma(w3[i][:], wview(w, 0, 8, 0, 512), eng="gpsimd")
        cnt = 0
        for t in range(NT):
            X = Xt[t % 2]
            H = Ht[t % 2]
            C = cs[t % 2]
            P.dma(C[:, 0, :], V(cosT, cosT.h[:, t * 512:(t + 1) * 512]))
            P.dma(C[:, 1, :], V(sinT, sinT.h[:, t * 512:(t + 1) * 512]))
            if h_all is not None:
                rk_, c0_ = t // 4, (t % 4) * 512
                for f_ in range(2):
                    P.dma(H[:, 4 * f_:4 * f_ + 4, :],
                          V(h_all[f_], h_all[f_].h[rk_ * 512:(rk_ + 1) * 512, c0_:c0_ + 512].rearrange("(c p) t -> p c t", p=128)))
            else:
                P.dma(X[:], tview(xT, 8, t * 512, 512))
                P.act(sq[:], X[:], AF.Square)
                n = pn[0]
                for c in range(8):
                    P.mm(n[:], ones[:], sq[:, c, :], start=c == 0, stop=c == 7)
                rms_rstd(P, n[:], rstd[:], 1024.0)
                for c in range(8):
                    P.stt(H[:, c, :], X[:, c, :], ga[:, c:c + 1], rstd[:], ALU.mult, ALU.mult)
            for h in range(4):
                for which in range(2):
                    w = w3[which]
                    dst = (QR if which == 0 else KR)[h]
                    g0 = qk[:, 2 * which:2 * which + 1]
                    g1 = qk[:, 2 * which + 1:2 * which + 2]
                    j = cnt % 2
                    cnt += 1
                    a = pk[j]
                    for kc in range(8):
                        P.mm(a[:], w[:, kc, h * 128:(h + 1) * 128], H[:, kc, :], start=kc == 0, stop=kc == 7)
                    P.copy(kb[j][:], a[:], eng="scalar")
                    P.act(sk[j][:], a[:], AF.Square)
                    P.mm(pp[j][:], perm[:], kb[j][:])
                    P.mm(pn[1][:], ones[:], sk[j][:])
                    rms_rstd(P, pn[1][:], r2[j][:], 128.0)
                    P.stt(t1[j][:], a[:], g0, C[:, 0, :], ALU.mult, ALU.mult)
                    P.stt(t2[j][:], pp[j][:], g1, C[:, 1, :], ALU.mult, ALU.mult)
                    P.tt(t1[j][:], t1[j][:], t2[j][:], ALU.add)
                    P.tt(dst[:, t * 512:(t + 1) * 512], t1[j][:], r2[j][:], ALU.mult, eng="gpsimd")
            for sub in range(4):
                a = pv[sub % 2]
                for kc in range(8):
                    P.mm(a[:], H[:, kc, sub * 128:(sub + 1) * 128], w3[2][:, kc, :], start=kc == 0, stop=kc == 7)
                P.copy(VP[:, t * 4 + sub, :, 0:128], a[:].re("p (h d) -> p h d", h=4), eng="vector")
        for h in range(4):
            P.reduce(km32[:], KR[h][:].re("p (n j) -> p n j", j=256), ALU.add, AX.X)
            P.ts(kmT[:, h, :], km32[:], 1.0 / 256.0, ALU.mult)
        P.barrier()
    if DBG.get("moba_stop") == "A":
        outer.close()
        return
    with ExitStack() as s2:
        cb = P.sb("cb", [128, 2, 256], BF16, s2)
        esel = P.sb("esel", [16, 16, 128], BF16, s2)
        pb = P.sb("pb", [128, 16, 16], F32, s2)
        SBT = P.sb("SBT", [16, 4, S], BF16, s2)
        OT = [P.sb(f"OT{i}", [128, S], BF16, s2) for i in range(2)]
        PT = [P.sb(f"PT{i}", [128, 2, 256], BF16, s2) for i in range(3)]
        gm = P.sb("gm", [128, 32, 16], F32, s2)
        m8 = P.sb("m8", [128, 32, 8], F32, s2)
        sbq = P.sb("sbq", [128, 32, 16], BF16, s2)
        rec = [P.sb(f"rec{i}", [128, 1], F32, s2) for i in range(4)]
        on = [P.sb(f"on{i}", [128, 128], BF16, s2) for i in range(4)]
        pS = [P.ps(f"pS{i}", [128, 2, 256], F32, s2) for i in range(2)]
        pO = [P.ps(f"pO{i}", [128, 512], F32, s2) for i in range(4)]
        pg = P.ps("pg", [128, 32, 16], F32, s2)
        ptr = P.ps("ptr", [128, 1024], BF16, s2)
        P.dma(cb[:], V(cb_d, cb_d.h[:, :].rearrange("p (k q) -> p k q", k=2)))
        P.dma(esel[:], V(esel_d, esel_d.h[:, :].rearrange("p (n j) -> p n j", n=16)))
        P.dma(pb[:], V(pb_d, pb_d.h[:, :].rearrange("p (i n) -> p i n", i=16)))
        for h in range(4):
            for qt in range(32):
                P.mm(pg[:, qt, :], QR[h][:, qt * 128:(qt + 1) * 128], kmT[:, h, :])
            gm4 = gm[:].re("p (i two) n -> p i two n", two=2)
            pg4 = pg[:].re("p (i two) n -> p i two n", two=2)
            for two in range(2):
                P.tt(gm4[:, :, two, :], pg4[:, :, two, :], pb[:], ALU.add)
            for qt in range(32):
                P.op("vector", (lambda qt: lambda e: e.max(out=m8.h[:, qt, :], in_=gm.h[:, qt, :]))(qt),
                     reads=[gm], writes=[m8])
            P.tt(sbq[:], gm[:], V(m8, m8.h[:, :, 2:3].broadcast_to([128, 32, 16])), ALU.is_lt)
            P.ts(sbq[:], sbq[:], NEG, ALU.mult, eng="gpsimd")
            for rnd in range(4):
                for j in range(8):
                    qt = rnd * 8 + j
                    P.transpose(ptr[0:16, j * 128:(j + 1) * 128], sbq[:, qt, :], ident[:])
                P.copy(SBT[:, h, rnd * 1024:(rnd + 1) * 1024], ptr[0:16, :], eng="scalar")
        for h in range(4):
            ot = OT[h % 2]
            its = [(i, n) for i in range(DBG.get("moba_nblk", 16)) for n in range(i + 1)]
            pend = []

            def emit_S(idx):
                i, n = its[idx]
                q0 = i * 256
                ps_ = pS[idx % 2]
                for kt in range(2):
                    k0 = (n * 2 + kt) * 128
                    P.mm(ps_[:, kt, :], KR[h][:, k0:k0 + 128], QR[h][:, q0:q0 + 256], start=True, stop=False)
                    if n < i:
                        P.mm(ps_[:, kt, :], esel[:, n, :], SBT[:, h, q0:q0 + 256], start=False, stop=True)
                    else:
                        P.mm(ps_[:, kt, :], ident[:], cb[:, kt, :], start=False, stop=True)
                P.act(PT[idx % 3][:], ps_[:], AF.Exp)

            def emit_PV(idx):
                i, n = its[idx]
                q0 = i * 256
                pt = PT[idx % 3]
                po = [pO[(i % 2) * 2 + qs] for qs in range(2)]
                for qs in range(2):
                    for kt in range(2):
                        P.mm(po[qs][:, 0:129], pt[:, kt, qs * 128:(qs + 1) * 128], VP[:, n * 2 + kt, h, 0:129],
                             start=(n == 0 and kt == 0), stop=(n == i and kt == 1))
                if n == i:
                    for qs in range(2):
                        k_ = (i % 2) * 2 + qs
                        P.op("vector", (lambda r, p_: lambda e: e.reciprocal(r.h[:], p_.h[:, 128:129]))(rec[k_], po[qs]),
                             reads=[po[qs]], writes=[rec[k_]])
                        P.ts(on[k_][:], po[qs][:, 0:128], rec[k_][:, 0:1], ALU.mult)
                        pend.append((idx + 2, k_, q0 + qs * 128))

            def flush(idx, force=False):
                while pend and (force or pend[0][0] <= idx):
                    _, k_, c0 = pend.pop(0)
                    cc = 256 + (k_ % 2) * 128
                    P.transpose(ptr[:, cc:cc + 128], on[k_][:], ident[:])
                    P.copy(ot[:, c0:c0 + 128], ptr[:, cc:cc + 128], eng="scalar")

            for idx in range(len(its) + 1):
                if idx < len(its):
                    emit_S(idx)
                if idx >= 1:
                    emit_PV(idx - 1)
                flush(idx)
            flush(0, force=True)
            for s_ in range(2):
                P.dma(oT(h * 128, (h + 1) * 128, s_), ot[:, s_ * 2048:(s_ + 1) * 2048])
        P.barrier()
    outer.close()


def build_moba():
    P = Prog()
    S = S_LEN
    xT = P.dram("xT", [1024, S], F32, "ExternalInput")
    wq = P.dram("wq", [1024, 512], F32, "ExternalInput")
    wk = P.dram("wk", [1024, 512], F32, "ExternalInput")
    wv = P.dram("wv", [1024, 512], F32, "ExternalInput")
    gaT = P.dram("gaT", [128, 8], F32, "ExternalInput")
    qn2 = P.dram("qn2", [128, 2], F32, "ExternalInput")
    kn2 = P.dram("kn2", [128, 2], F32, "ExternalInput")
    cosT = P.dram("cosT", [128, S], F32, "ExternalInput")
    sinT = P.dram("sinT", [128, S], F32, "ExternalInput")
    perm = P.dram("perm", [128, 128], BF16, "ExternalInput")
    ident = P.dram("ident", [128, 128], BF16, "ExternalInput")
    cb = P.dram("cb", [128, 512], BF16, "ExternalInput")
    esel = P.dram("esel", [16, 2048], BF16, "ExternalInput")
    pb = P.dram("pb", [128, 256], F32, "ExternalInput")
    oT = P.dram("oT", [512, S], BF16, "ExternalOutput")
    moba_phase(P, xT, wq, wk, wv, gaT, qn2, kn2, cosT, sinT, perm, ident, cb, esel, pb,
               lambda r0, r1, s_: V(oT, oT.h[r0:r1, s_ * 2048:(s_ + 1) * 2048]))
    P.wait_ticket("sync", oT.w)
    P.barrier()
    return P.emit()


def moba_inputs(x1T_b, hh, d):
    c = moba_consts()
    wqkv = d["w_qkv"][0]
    qn = d["q_norm"][0]
    kn = d["k_norm"][0]
    pidx = (np.arange(128) + 64) % 128
    ins = dict(wq=np.ascontiguousarray(wqkv[:, hh * 512:(hh + 1) * 512]),
               wk=np.ascontiguousarray(wqkv[:, 1024 + hh * 512:1024 + (hh + 1) * 512]),
               wv=np.ascontiguousarray(wqkv[:, 2048 + hh * 512:2048 + (hh + 1) * 512]),
               gaT=vecT(d["attn_norm"][1]),
               qn2=np.ascontiguousarray(np.stack([qn, qn[pidx]], 1)),
               kn2=np.ascontiguousarray(np.stack([kn, kn[pidx]], 1)))
    if x1T_b is not None:
        ins["xT"] = x1T_b
    ins.update(c)
    return ins


CH = 64
RW_LN_EPS = 64e-5


def ar_consts():
    bf = ml_dtypes.bfloat16
    s = np.arange(64)[:, None]
    t = np.arange(64)[None, :]
    strictT = (s < t).astype(np.float32)
    inclT = (s <= t).astype(np.float32)
    strict = (t < s).astype(np.float32)
    m = np.concatenate([strictT, inclT, strictT, inclT, strict], 1)
    mask = np.concatenate([m, m], 0)
    i64 = np.concatenate([np.eye(64), np.eye(64)], 0).astype(np.float32)
    bones = np.zeros((128, 128), np.float32)
    bones[:64, :64] = 1
    bones[64:, 64:] = 1
    rm = np.ones((128, 512), np.float32)
    rm[:, ::64] = 0
    return dict(mask=mask.astype(np.float32), i64=i64, bones=bones.astype(bf),
                ident=np.eye(128, dtype=np.float32).astype(bf), rm=rm)


def ar_phase(P, xT, whg, wrw, w2a2_d, g2_d, gaT, lb3_d, hv_d, rv_d, mu8_d, mask_d, i64_d, bones_d, ident_d, rm_d, oT,
             after_tile=None):
    S = S_LEN
    NT = S // 512
    NC = 512 // CH
    outer = ExitStack()
    sb = lambda n, sh, dt: P.sb(n, sh, dt, outer)
    ones = sb("ones", [128, 128], BF16)
    bones = sb("bones", [128, 128], BF16)
    ident = sb("ident", [128, 128], BF16)
    mask = sb("mask", [128, 320], F32)
    i64 = sb("i64", [128, 64], F32)
    rm = sb("rm", [128, 512], F32)
    ga = sb("ga", [128, 8], F32)
    lb3 = sb("lb3", [128, 2, 3], F32)
    lbv = sb("lbv", [128, 2, 4], F32)
    hv = sb("hv", [128, 2], F32)
    rv = sb("rv", [128, 2, 8], F32)
    mu8 = sb("mu8", [128, 2, 8], F32)
    whgs = sb("whgs", [128, 8, 1024], BF16)
    wrws = sb("wrws", [128, 8, 1024], BF16)
    w2a2 = sb("w2a2", [128, 256], BF16)
    g2 = sb("g2", [128, 256], BF16)
    Ucar = sb("Ucar", [128, 8, 516], F32)
    Hb = [sb(f"Hb{i}", [128, 2, 64], BF16) for i in range(2)]
    Hg = sb("Hg", [128, 2, 64], BF16)
    Sb = [sb(f"Sb{i}", [128, 2, 128], BF16) for i in range(2)]
    Sg = sb("Sg", [128, 2, 128], BF16)
    P.memset(ones[:], 1.0)
    P.memset(Ucar[:], 0.0)
    for t_ in Hb + Sb:
        P.memset(t_[:], 0.0)
    for dst, src in ((bones, bones_d), (ident, ident_d), (mask, mask_d), (i64, i64_d), (rm, rm_d), (ga, gaT)):
        P.dma(dst[:], src[:])
    P.dma(lb3[:], V(lb3_d, lb3_d.h[:, :].rearrange("p (h j) -> p h j", h=2)))
    P.dma(hv[:], hv_d[:])
    P.dma(rv[:, :, 0:7], V(rv_d, rv_d.h[:, :].rearrange("p (h j) -> p h j", h=2)))
    P.dma(mu8[:, 0, :], mu8_d[:])
    P.dma(whgs[:], wview(whg, 0, 8, 0, 1024), eng="gpsimd")
    P.dma(wrws[:], wview(wrw, 0, 8, 0, 1024), eng="gpsimd")
    P.dma(w2a2[:], w2a2_d[:], eng="gpsimd")
    P.dma(g2[:], g2_d[:], eng="gpsimd")
    P.act(lb3[:], lb3[:], AF.Exp)
    P.reduce(lbv[:, :, 2], lb3[:], ALU.add, AX.X)
    P.op("vector", lambda e: e.reciprocal(lbv.h[:, :, 3], lbv.h[:, :, 2]), reads=[lbv], writes=[lbv])
    P.tt(lbv[:, :, 0], lb3[:, :, 0], lbv[:, :, 3], ALU.mult)
    P.ts(lbv[:, :, 1], lbv[:, :, 0], -1.0, ALU.mult, 1.0, ALU.add)
    P.ts(rv[:, :, 7], rv[:, :, 3], -1.0, ALU.mult, 1.0, ALU.add)
    P.ts(mu8[:, 1, :], mu8[:, 0, :], -1.0, ALU.mult, 1.0, ALU.add)

    def f32(n, stack):
        return P.sb(n, [128, 512], F32, stack)

    def b16(n, stack):
        return P.sb(n, [128, 512], BF16, stack)

    for t in range(DBG.get('ar_nt', NT)):
        tile = ExitStack()
        QtT = [b16(f"QtT{h}", tile) for h in range(2)]
        KtT = [b16(f"KtT{h}", tile) for h in range(2)]
        QbT = [b16(f"QbT{h}", tile) for h in range(2)]
        KhT = [b16(f"KhT{h}", tile) for h in range(2)]
        VTh = [b16(f"VTh{h}", tile) for h in range(2)]
        SGt = [b16(f"SGt{h}", tile) for h in range(2)]
        E3h = [f32(f"E3h{h}", tile) for h in range(2)]
        OAt = [f32(f"OAt{h}", tile) for h in range(2)]
        AR = [P.sb(f"AR{p}", [128, NC, 2, CH], BF16, tile) for p in range(2)]
        BT = [b16(f"BT{p}", tile) for p in range(2)]
        KT = [b16(f"KT{p}", tile) for p in range(2)]
        VT = [b16(f"VT{p}", tile) for p in range(2)]
        VF = [f32(f"VF{p}", tile) for p in range(2)]
        RKb = [b16(f"RKb{p}", tile) for p in range(2)]
        GT = [b16(f"GT{p}", tile) for p in range(2)]
        E1 = [f32(f"E1{p}", tile) for p in range(2)]
        YT = [f32(f"YT{p}", tile) for p in range(2)]
        with ExitStack() as sp:
            H = P.sb("H", [128, 8, 512], BF16, sp)
            rstd = f32("rstd", sp)
            pq = [P.ps(f"pq{i}", [128, 512], F32, sp) for i in range(2)]
            pm = [P.ps(f"pm{i}", [128, 512], F32, sp) for i in range(3)]
            bk7bb = P.ps("bk7b", [128, 1024], BF16, sp)
            bk7b = bk7bb[:, 0:256].re("p (a j) -> p a j", a=2)
            bkH = [P.ps(f"bkH{h}", [128, 512], F32, sp) for h in range(2)]
            hs = [slice(0, 64), slice(64, 128)]
            with ExitStack() as sn:
                X = P.sb("X", [128, 8, 512], F32, sn)
                sq = P.sb("sq", [128, 8, 512], BF16, sn)
                P.dma(X[:], tview(xT, 8, t * 512, 512))
                P.act(sq[:], X[:], AF.Square)
                for c in range(8):
                    P.mm(pm[0][:], ones[:], sq[:, c, :], start=c == 0, stop=c == 7)
                rms_rstd(P, pm[0][:], rstd[:], 1024.0)
                for c in range(8):
                    P.stt(H[:, c, :], X[:, c, :], ga[:, c:c + 1], rstd[:], ALU.mult, ALU.mult)
                P.barrier()
            WA = b16("WA", sp)
            sgT = b16("sgT", sp)
            tmpH = [[f32(f"th{h}{i}", sp) for i in range(6)] for h in range(2)]
            tmpR = [[f32(f"tr{p}{i}", sp) for i in range(6)] for p in range(2)]
            tbR = [b16(f"tb{p}", sp) for p in range(2)]
            TOKH = P.sb("TOKH", [128, 2, 128], BF16, sp)
            AtH = P.sb("AtH", [128, 64], BF16, sp)
            pqi = [0]

            def proj(w, ct):
                a = pq[pqi[0] % 2]
                pqi[0] += 1
                for kc in range(8):
                    P.mm(a[:], w[:, kc, ct * 128:(ct + 1) * 128], H[:, kc, :], start=kc == 0, stop=kc == 7)
                return a

            done = {}

            def hgrn_prep(h):
                tmp = tmpH[h]
                aq = proj(whgs, 0 + h)
                qs = tmp[0]
                P.act(qs[:], aq[:], AF.Silu)
                yield
                af = proj(whgs, 2 + h)
                f = tmp[1]
                P.act(f[:], af[:], AF.Sigmoid)
                yield
                P.ts(f[:], f[:], lbv[:, h, 1:2], ALU.mult, lbv[:, h, 0:1], ALU.add)
                yield
                lf = tmp[2]
                P.act(lf[:], f[:], AF.Ln)
                kq = tmp[3]
                P.act(kq[:], f[:], AF.Identity, scale=-1.0, bias=1.0)
                yield
                b = tmp[4]
                P.op("vector", (lambda b, lf: lambda e: e.tensor_tensor_scan(b.h[:], rm.h[:], lf.h[:], 0.0, ALU.mult, ALU.add))(b, lf),
                     reads=[rm, lf], writes=[b])
                yield
                b3 = b[:].re("p (c j) -> p c j", j=CH)
                d = tmp[5]
                P.tt(d[:].re("p (c j) -> p c j", j=CH), b3, V(b, b.h[:, :].rearrange("p (c j) -> p c j", j=CH)[:, :, 31:32].broadcast_to([128, NC, CH])), ALU.subtract)
                P.act(E3h[h][:], b[:], AF.Exp)
                yield
                e1 = tmp[2]
                e2 = tmp[1]
                P.act(e1[:], d[:], AF.Exp)
                P.act(e2[:], d[:], AF.Exp, scale=-1.0)
                yield
                P.tt(QtT[h][:], qs[:], e1[:], ALU.mult, eng="gpsimd")
                P.tt(KtT[h][:], kq[:], e2[:], ALU.mult)
                yield
                P.tt(QbT[h][:], qs[:], E3h[h][:], ALU.mult)
                yield
                P.tt(d[:].re("p (c j) -> p c j", j=CH), b3, V(b, b.h[:, :].rearrange("p (c j) -> p c j", j=CH)[:, :, 63:64].broadcast_to([128, NC, CH])), ALU.subtract)
                yield
                P.act(e1[:], d[:], AF.Exp, scale=-1.0)
                yield
                P.tt(KhT[h][:], kq[:], e1[:], ALU.mult, eng="gpsimd")
                ai = proj(whgs, 4 + h)
                P.copy(VTh[h][:], ai[:], eng="scalar")
                yield
                ag = proj(whgs, 6 + h)
                P.act(SGt[h][:], ag[:], AF.Silu)
                done[("h", h)] = True
                yield

            def hgrn_gen():
                while not (done.get(("h", 0)) and done.get(("h", 1))):
                    yield
                for c in range(DBG.get('ar_nc', NC) if DBG.get('ar_hg', True) else 0):
                    o = c * CH
                    gcol = o + CH - 1
                    gi = t * NC + c
                    Sb0, Sb1 = Sb[gi % 2], Sb[(gi + 1) % 2]
                    for h in range(2):
                        P.ts(Sg[:, h, :], Sb0[:, h, :], E3h[h][:, gcol:gcol + 1], ALU.mult, eng="gpsimd")
                        P.transpose(bk7b[hs[h], 0, :], KhT[h][:, o:o + CH], ident[:])
                        P.transpose(bk7b[hs[h], 1, :], VTh[h][:, o:o + CH], ident[:])
                    yield
                    P.copy(TOKH[:], bk7b[:], eng="scalar")
                    for h in range(2):
                        P.mm(bkH[0][hs[h], 192:256], KtT[h][:, o:o + CH], QtT[h][:, o:o + CH])
                    yield
                    P.tt(AtH[:], bkH[0][:, 192:256], mask[:, 64:128], ALU.mult)
                    yield
                    for h in range(2):
                        P.mm(bkH[h][:, 0:64], TOKH[hs[h], 1, :], AtH[hs[h], :], start=True, stop=False)
                        P.mm(bkH[h][:, 0:64], Sb0[:, h, :], QbT[h][:, o:o + CH], start=False, stop=True)
                    for h in range(2):
                        P.mm(bkH[h][:, 64:192], TOKH[hs[h], 0, :], TOKH[hs[h], 1, :])
                    yield
                    for h in range(2):
                        P.copy(OAt[h][:, o:o + CH], bkH[h][:, 0:64], eng="scalar")
                    for h in range(2):
                        P.tt(Sb1[:, h, :], bkH[h][:, 64:192], Sg[:, h, :], ALU.add)
                    yield

            def shifted(ct):
                a = proj(wrws, ct)
                U = Ucar[:, ct, :]
                P.copy(U[:, 3:4], U[:, 515:516], eng="gpsimd")
                P.copy(U[:, 4:516], a[:], eng="scalar")
                return U

            def mix(dst, U, ct, eng="vector"):
                P.ts(dst, U[:, 4:516], mu8[:, 1, ct:ct + 1], ALU.mult, eng=eng)
                P.stt(dst, U[:, 3:515], mu8[:, 0, ct:ct + 1], dst, ALU.mult, ALU.add)

            def rwkv_pre():
                U6 = shifted(6)
                wm = tmpR[0][5]
                mix(wm[:], U6, 6)
                yield
                P.act(WA[0:64, :], wm[0:64, :], AF.Tanh)
                P.copy(WA[64:128, :], wm[64:128, :], eng="scalar")
                yield
                U7 = shifted(7)
                wm2 = tmpR[1][5]
                mix(wm2[:], U7, 7)
                yield
                P.act(sgT[:], wm2[:], AF.Sigmoid)
                done["pre"] = True
                yield

            def rwkv_prep(p):
                rM, kM, kk, a_, t4, t5 = tmpR[p]
                tb0 = tbR[p]
                mix(rM[:], shifted(0 + p), 0 + p)
                yield
                mix(kM[:], shifted(2 + p), 2 + p)
                yield
                mix(VF[p][:], shifted(4 + p), 4 + p)
                P.copy(VT[p][:], VF[p][:], eng="scalar")
                yield
                P.ts(kk[:], kM[:], rv[:, p, 2:3], ALU.mult)
                P.act(tb0[:], kk[:], AF.Square)
                yield
                while not done.get("pre"):
                    yield
                pz_ = pm[1 + p]
                P.mm(pz_[:], w2a2[0:64, p * 128:(p + 1) * 128], WA[0:64, :])
                ld = t4
                P.act(ld[:], pz_[:], AF.Sigmoid, bias=rv[:, p, 0:1])
                yield
                P.ts(ld[:], ld[:], -float(np.exp(-0.5)), ALU.mult)
                P.mm(pz_[:], w2a2[64:128, p * 128:(p + 1) * 128], WA[64:128, :])
                P.act(a_[:], pz_[:], AF.Sigmoid, bias=rv[:, p, 1:2])
                yield
                P.mm(pz_[:], g2[:, p * 128:(p + 1) * 128], sgT[:])
                P.copy(GT[p][:], pz_[:], eng="scalar")
                yield
                P.mm(pz_[:], bones[:], tb0[:])
                P.act(t5[:], pz_[:], AF.Ln, bias=1e-12)
                yield
                P.act(t5[:], t5[:], AF.Exp, scale=-0.5)
                yield
                P.tt(kk[:], kk[:], t5[:], ALU.mult)
                yield
                P.ts(t5[:], a_[:], rv[:, p, 3:4], ALU.mult, rv[:, p, 7:8], ALU.add)
                yield
                P.tt(kM[:], kM[:], t5[:], ALU.mult)
                yield
                P.stt(RKb[p][:], rM[:], rv[:, p, 4:5], kM[:], ALU.mult, ALU.mult)
                cs = t5
                P.op("vector", (lambda cs, ld: lambda e: e.tensor_tensor_scan(cs.h[:], rm.h[:], ld.h[:], 0.0, ALU.mult, ALU.add))(cs, ld),
                     reads=[rm, ld], writes=[cs])
                yield
                P.act(E1[p][:], cs[:], AF.Exp)
                P.tt(ld[:], cs[:], ld[:], ALU.subtract)
                yield
                AR4 = AR[p]
                P.tt(AR4[:, :, 1, :], rM[:].re("p (c j) -> p c j", j=CH), E1[p][:].re("p (c j) -> p c j", j=CH), ALU.mult)
                P.act(ld[:], ld[:], AF.Exp)
                yield
                P.stt(AR4[:, :, 0, :], kk[:].re("p (c j) -> p c j", j=CH), -1.0, ld[:].re("p (c j) -> p c j", j=CH), ALU.mult, ALU.mult)
                P.act(cs[:], cs[:], AF.Exp, scale=-1.0)
                yield
                P.tt(kk[:], kk[:], a_[:], ALU.mult)
                P.tt(KT[p][:], kM[:], cs[:], ALU.mult)
                yield
                P.tt(BT[p][:], kk[:], cs[:], ALU.mult)
                yield

            gens = [hgrn_prep(0), rwkv_pre(), hgrn_prep(1), rwkv_prep(0), rwkv_prep(1), hgrn_gen()]
            while gens:
                for g_ in list(gens):
                    try:
                        next(g_)
                    except StopIteration:
                        gens.remove(g_)
            P.barrier()
        hs = [slice(0, 64), slice(64, 128)]
        NG = NC // 4
        keep = ExitStack()
        TOKg = [[P.sb(f"TOKg{p}{g}", [128, 4, 4, 64], BF16, keep) for g in range(NG)] for p in range(2)]
        SCbg = [[P.sb(f"SCbg{p}{g}", [128, 4, 320], BF16, keep) for g in range(NG)] for p in range(2)]
        WhTg = [[P.sb(f"WhTg{p}{g}", [128, 4, 64], BF16, keep) for g in range(NG)] for p in range(2)]
        UHg = [[P.sb(f"UHg{p}{g}", [128, 4, 64], F32, keep) for g in range(NG)] for p in range(2)]
        with ExitStack() as sc:
            bTb = P.ps("bT", [128, 1024], BF16, sc)
            bT = bTb[:].re("p (q j d) -> p q j d", q=4, j=4)
            bA = [P.ps(f"bA{i}", [128, 512], F32, sc) for i in range(2)]
            bN = P.ps("bN", [128, 512], F32, sc)
            bQ = P.ps("bQ", [128, 512], F32, sc)
            bS = [P.ps(f"bS{p}", [128, 512], F32, sc) for p in range(2)]
            pzh = P.ps("pzh", [128, 512], F32, sc)
            Ub = [P.sb(f"Ub{p}", [128, 64], BF16, sc) for p in range(2)]
            Tg = [P.sb(f"Tg{i}", [128, 4, 64], BF16, sc) for i in range(2)]
            PQg = [P.sb(f"PQg{i}", [128, 4, 128], BF16, sc) for i in range(2)]
            Zb = P.sb("Zb", [128, 4, 64], BF16, sc)

            def bc4(v, n):
                return V(v.tile, v.ap.unsqueeze(1).broadcast_to([128, n, v.ap.shape[-1]]))

            def stage1(p, g):
                tok, scb = TOKg[p][g], SCbg[p][g]
                cs_ = [g * 4 + q for q in range(4)]
                aT = [AR[p][:, c, 0, :] for c in cs_]
                arT = [AR[p][:, c, :, :].re("p a j -> p (a j)") for c in cs_]
                bT_ = [BT[p][:, c * CH:(c + 1) * CH] for c in cs_]
                kT_ = [KT[p][:, c * CH:(c + 1) * CH] for c in cs_]
                vT_ = [VT[p][:, c * CH:(c + 1) * CH] for c in cs_]
                for h in range(2):
                    for q in range(4):
                        for j, xx in enumerate((aT[q], bT_[q], kT_[q], vT_[q])):
                            P.transpose(bT[hs[h], q, j, :], xx[hs[h], :], ident[hs[h], hs[h]])
                    for q in range(4):
                        ba = bA[q // 2]
                        o_ = (q % 2) * 256
                        P.mm(ba[hs[h], o_:o_ + 128], bT_[q][hs[h], :], arT[q][hs[h], :])
                        P.mm(ba[hs[h], o_ + 128:o_ + 256], kT_[q][hs[h], :], arT[q][hs[h], :])
                        P.mm(bN[hs[h], q * 64:(q + 1) * 64], aT[q][hs[h], :], bT_[q][hs[h], :])
                P.copy(tok[:], bT, eng="scalar")
                for k in range(2):
                    P.tt(scb[:, 2 * k:2 * k + 2, 0:256], bA[k][:].re("p (q n) -> p q n", q=2), bc4(mask[:, 0:256], 2), ALU.mult)
                P.tt(scb[:, :, 256:320], bN[:, 0:256].re("p (q n) -> p q n", q=4), bc4(mask[:, 256:320], 4), ALU.mult)
                P.tt(Tg[0][:], scb[:, :, 0:64], bc4(i64[:], 4), ALU.add)
                Pm = [scb[:, q, 256:320] for q in range(4)]
                Qm = [scb[:, q, 0:64] for q in range(4)]
                sqb = [bQ, bA[1]]
                tub = [bN, bA[0]]
                v4 = lambda x: x.re("p (q n) -> p q n", q=4)
                for j in range(5):
                    pq_ = PQg[j % 2]
                    for h in range(2):
                        for q in range(4):
                            P.mm(sqb[h][hs[h], q * 128:q * 128 + 64], Qm[q][hs[h], :], Pm[q][hs[h], :])
                            P.mm(sqb[h][hs[h], q * 128 + 64:q * 128 + 128], Pm[q][hs[h], :], Qm[q][hs[h], :])
                    P.copy(pq_[hs[0]], v4(sqb[0][hs[0], :]), eng="scalar")
                    P.copy(pq_[hs[1]], v4(sqb[1][hs[1], :]), eng="vector")
                    Pm = [pq_[:, q, 0:64] for q in range(4)]
                    Qm = [pq_[:, q, 64:128] for q in range(4)]
                    To, Tn = Tg[j % 2], Tg[(j + 1) % 2]
                    for h in range(2):
                        for q in range(4):
                            P.mm(tub[h][hs[h], 256 + q * 64:256 + (q + 1) * 64], Pm[q][hs[h], :], To[hs[h], q, :])
                    for h in range(2):
                        P.tt(Tn[hs[h]], v4(tub[h][hs[h], 256:512]), To[hs[h]], ALU.add)
                Tf = Tg[5 % 2]
                for h in range(2):
                    for q in range(4):
                        P.mm(tub[h][hs[h], q * 64:(q + 1) * 64], scb[hs[h], q, 128:192], tok[hs[h], q, 3, :])
                P.copy(Zb[hs[0]], v4(tub[0][hs[0], 0:256]), eng="scalar")
                P.copy(Zb[hs[1]], v4(tub[1][hs[1], 0:256]), eng="vector")
                for h in range(2):
                    for q in range(4):
                        P.mm(tub[h][hs[h], 256 + q * 64:256 + (q + 1) * 64], Tf[hs[h], q, :], Zb[hs[h], q, :])
                    for q in range(4):
                        P.mm(sqb[h][hs[h], q * 64:(q + 1) * 64], tok[hs[h], q, 0, :], Tf[hs[h], q, :])
                P.copy(UHg[p][g][hs[0]], v4(tub[0][hs[0], 256:512]), eng="scalar")
                P.copy(UHg[p][g][hs[1]], v4(tub[1][hs[1], 256:512]), eng="vector")
                P.copy(WhTg[p][g][hs[0]], v4(sqb[0][hs[0], 0:256]), eng="scalar")
                P.copy(WhTg[p][g][hs[1]], v4(sqb[1][hs[1], 0:256]), eng="vector")

            def stage2_gen(g):
                for q in range(4):
                    c = g * 4 + q
                    o = c * CH
                    gcol = o + CH - 1
                    gi = t * NC + c
                    Hb0, Hb1 = Hb[gi % 2], Hb[(gi + 1) % 2]
                    UO = [(0, 0), (1, 1), (0, 1), (1, 0)]
                    for p in range(2):
                        P.ts(Hg[:, p, :], Hb0[:, p, :], E1[p][:, gcol:gcol + 1], ALU.mult, eng="gpsimd")
                    for p, h in UO:
                        P.mm(bS[p][hs[h], 0:64], WhTg[p][g][hs[h], q, :], Hb0[hs[h], p, :])
                    yield
                    for p in range(2):
                        P.tt(Ub[p][:], bS[p][:, 0:64], UHg[p][g][:, q, :], ALU.add)
                    yield
                    for p, h in UO:
                        tok, scb = TOKg[p][g], SCbg[p][g]
                        r_T = AR[p][:, c, 1, :]
                        P.mm(bS[p][hs[h], 64:128], Hb0[hs[h], p, :], r_T[hs[h], :], start=True, stop=False)
                        P.mm(bS[p][hs[h], 64:128], Ub[p][hs[h], :], scb[hs[h], q, 64:128], start=False, stop=False)
                        P.mm(bS[p][hs[h], 64:128], tok[hs[h], q, 3, :], scb[hs[h], q, 192:256], start=False, stop=True)
                    for p, h in UO:
                        tok = TOKg[p][g]
                        P.mm(bS[p][hs[h], 128:192], tok[hs[h], q, 1, :], Ub[p][hs[h], :], start=True, stop=False)
                        P.mm(bS[p][hs[h], 128:192], tok[hs[h], q, 2, :], tok[hs[h], q, 3, :], start=False, stop=True)
                    yield
                    for p in range(2):
                        P.stt(Hb1[:, p, :], bS[p][:, 128:192], E1[p][:, gcol:gcol + 1], Hg[:, p, :], ALU.mult, ALU.add)
                        P.copy(YT[p][:, o:o + CH], bS[p][:, 64:128], eng="scalar")
                    yield

            rw = DBG.get('ar_rw', True)
            for g in range(NG if rw else 0):
                for p in range(2):
                    stage1(p, g)
                P.set_bg(stage2_gen(g), DBG.get('bg2', 24))
            tah = [f32(f"tah{i}", sc) for i in range(2)]
            tbh = b16("tbh", sc)
            obh = [b16(f"obh{i}", sc) for i in range(2)]
            for h in range(2):
                P.act(tbh[:], OAt[h][:], AF.Square)
                P.mm(pzh[:], ones[:], tbh[:])
                rms_rstd(P, pzh[:], tah[0][:], 128.0)
                P.stt(tah[1][:], OAt[h][:], hv[:, h:h + 1], tah[0][:], ALU.mult, ALU.mult)
                P.tt(obh[h][:], tah[1][:], SGt[h][:], ALU.mult)
                P.dma(oT(h * 128, (h + 1) * 128, t), obh[h][:])
            P.flush_bg()
            P.barrier()
        keep.close()
        with ExitStack() as so:
            pz = [P.ps(f"pz{i}", [128, 512], F32, so) for i in range(3)]
            ta = [f32(f"ta{i}", so) for i in range(3)]
            tbb = [b16(f"tbb{i}", so) for i in range(2)]
            ob = [b16(f"ob{i}", so) for i in range(4)]
            for p in range(2):
                y = YT[p]
                P.copy(tbb[0][:], y[:], eng="scalar")
                P.mm(pz[0][:], bones[:], tbb[0][:])
                P.stt(ta[0][:], pz[0][:], -1.0 / 64.0, y[:], ALU.mult, ALU.add)
                P.act(tbb[1][:], ta[0][:], AF.Square)
                P.mm(pz[1][:], bones[:], tbb[1][:])
                rms_rstd(P, pz[1][:], ta[1][:], 64.0, eps=RW_LN_EPS)
                P.tt(ta[0][:], ta[0][:], ta[1][:], ALU.mult)
                P.ts(ta[0][:], ta[0][:], rv[:, p, 5:6], ALU.mult, rv[:, p, 6:7], ALU.add)
                P.mm(pz[2][:], bones[:], RKb[p][:])
                P.tt(ta[2][:], pz[2][:], VF[p][:], ALU.mult)
                P.tt(ta[0][:], ta[0][:], ta[2][:], ALU.add)
                P.tt(ob[2 + p][:], ta[0][:], GT[p][:], ALU.mult)
                P.dma(oT(256 + p * 128, 256 + (p + 1) * 128, t), ob[2 + p][:])
            P.barrier()
        tile.close()
        if after_tile is not None:
            after_tile(t)
    outer.close()


def build_ar():
    P = Prog()
    S = S_LEN
    d = lambda n, sh, dt=F32: P.dram(n, sh, dt, "ExternalInput")
    xT = d("xT", [1024, S])
    whg = d("whg", [1024, 1024])
    wrw = d("wrw", [1024, 1024])
    w2a2 = d("w2a2", [128, 256])
    g2 = d("g2", [128, 256])
    gaT = d("gaT", [128, 8])
    lb3 = d("lb3", [128, 6])
    hv = d("hv", [128, 2])
    rv = d("rv", [128, 14])
    mu8 = d("mu8", [128, 8])
    mask = d("mask", [128, 320])
    i64 = d("i64", [128, 64])
    bones = d("bones", [128, 128], BF16)
    ident = d("ident", [128, 128], BF16)
    rm = d("rm", [128, 512])
    oT = P.dram("oT", [512, S], BF16, "ExternalOutput")
    ar_phase(P, xT, whg, wrw, w2a2, g2, gaT, lb3, hv, rv, mu8, mask, i64, bones, ident, rm,
             lambda r0, r1, t: V(oT, oT.h[r0:r1, t * 512:(t + 1) * 512]))
    P.wait_ticket("sync", oT.w)
    P.barrier()
    return P.emit()


def ar_inputs(xT_b, hh, d):
    w = d["w_in_ar"][0]
    c = lambda a, b: w[:, a:b]
    h0 = hh * 256
    whg = np.concatenate([c(h0, h0 + 256), c(512 + h0, 512 + h0 + 256), c(1024 + h0, 1024 + h0 + 256),
                          c(1536 + h0, 1536 + h0 + 256)], 1)
    wrw = np.concatenate([c(2048 + h0, 2048 + h0 + 256), c(2560 + h0, 2560 + h0 + 256), c(3072 + h0, 3072 + h0 + 256),
                          c(3584, 3840)], 1)
    mu = d["rwkv_mu"][0]
    mu8 = np.concatenate([mu[h0:h0 + 256], mu[512 + h0:512 + h0 + 256], mu[1024 + h0:1024 + h0 + 256], mu[1536:1792]])
    mu8 = np.ascontiguousarray(mu8.reshape(8, 128).T)
    w2a2 = np.concatenate([d["rwkv_w2"][0][:, h0:h0 + 256], d["rwkv_a2"][0][:, h0:h0 + 256]], 0)
    g2 = d["rwkv_g2"][0][:, h0:h0 + 256]
    lb = d["hgrn_lb"]
    lb3 = np.stack([lb[:, h0 + h * 128:h0 + (h + 1) * 128].T for h in range(2)], 1).reshape(128, 6)
    hv = np.stack([d["hgrn_onorm"][0][h0 + h * 128:h0 + (h + 1) * 128] for h in range(2)], 1)
    names = ["rwkv_w0", "rwkv_a0", "rwkv_kk", "rwkv_ka", "rwkv_rk", "rwkv_ln_w", "rwkv_ln_b"]
    rv = np.stack([np.stack([d[n][0][h0 + p * 128:h0 + (p + 1) * 128] for n in names], 1) for p in range(2)], 1)
    ins = dict(xT=xT_b, whg=np.ascontiguousarray(whg), wrw=np.ascontiguousarray(wrw),
               w2a2=np.ascontiguousarray(w2a2), g2=np.ascontiguousarray(g2), gaT=vecT(d["attn_norm"][0]),
               lb3=np.ascontiguousarray(lb3), hv=np.ascontiguousarray(hv),
               rv=np.ascontiguousarray(rv.reshape(128, 14)), mu8=mu8)
    ins.update(ar_consts())
    return ins


RG = [[0, 1], [2, 3], [4, 5], [6, 7]]
ROWMAP0 = [0, 128, 512, 640, 256, 384, 768, 896]
ROWMAP1 = [0, 128, 256, 384, 512, 640, 768, 896]


def allgather_pairs(P, src, dst):
    s_, d_ = src.h, dst.h
    P.op("gpsimd", lambda e: e.collective_compute("AllGather", ALU.bypass, replica_groups=RG,
                                                  ins=[s_.opt()], outs=[d_.opt()]),
         reads=[src], writes=[dst], cc=True)


def build_fused():
    P = Prog()
    S = S_LEN
    T = S // 2
    all_in = []

    def i_(n, sh, dt=F32):
        t = P.dram(n, sh, dt, "ExternalInput")
        all_in.append((t, dt))
        return t

    def touch():
        s32 = P.sb("s32", [1, 4], F32)
        s16 = P.sb("s16", [1, 4], BF16)
        for t, dt in all_in:
            P.dma((s32 if dt == F32 else s16)[0:1, 0:2], V(t, t.h[0:1, 0:2]))

    t_ = lambda n, sh, dt: P.dram(n, sh, dt, "Internal")
    a_xT = i_("a_xT", [1024, S])
    a_whg = i_("a_whg", [1024, 1024])
    a_wrw = i_("a_wrw", [1024, 1024])
    a_w2a2 = i_("a_w2a2", [128, 256])
    a_g2 = i_("a_g2", [128, 256])
    a_gaT = i_("a_gaT", [128, 8])
    a_lb3 = i_("a_lb3", [128, 6])
    a_hv = i_("a_hv", [128, 2])
    a_rv = i_("a_rv", [128, 14])
    a_mu8 = i_("a_mu8", [128, 8])
    c_mask = i_("c_mask", [128, 320])
    c_i64 = i_("c_i64", [128, 64])
    c_bones = i_("c_bones", [128, 128], BF16)
    c_ident = i_("c_ident", [128, 128], BF16)
    c_rm = i_("c_rm", [128, 512])
    sel = i_("sel", [128, 2])
    ga1T = i_("ga1T", [128, 8])
    dn = []
    for l in range(2):
        dn.append(dict(pT=i_(f"d{l}_pT", [256, T]), wout=i_(f"d{l}_wout", [1024, 1024]), wup=i_(f"d{l}_wup", [1024, 4096]),
                       wdown=i_(f"d{l}_wdown", [4096, 1024]), wple=i_(f"d{l}_wple", [256, 1024]),
                       wgate=i_(f"d{l}_wgate", [1024, 1024]), gmT=i_(f"d{l}_gmT", [128, 8]), gpT=i_(f"d{l}_gpT", [128, 8])))
    d0_xh = i_("d0_xh", [1024, T])
    m_wq = i_("m_wq", [1024, 512])
    m_wk = i_("m_wk", [1024, 512])
    m_wv = i_("m_wv", [1024, 512])
    m_qn2 = i_("m_qn2", [128, 2])
    m_kn2 = i_("m_kn2", [128, 2])
    c_cosT = i_("c_cosT", [128, S])
    c_sinT = i_("c_sinT", [128, S])
    c_perm = i_("c_perm", [128, 128], BF16)
    c_cb = i_("c_cb", [128, 512], BF16)
    c_esel = i_("c_esel", [16, 2048], BF16)
    c_pb = i_("c_pb", [128, 256])
    yT = P.dram("yT", [1024, T], F32, "ExternalOutput")
    o0_src = [t_(f"o0_src{i}", [512, T], BF16) for i in range(2)]
    o0_all = [t_(f"o0_all{i}", [1024, T], BF16) for i in range(2)]
    x1_loc = t_("x1_loc", [1024, T], F32)
    h1_src = [t_(f"h1_src{i}", [512, T], BF16) for i in range(2)]
    h1_all = [t_(f"h1_all{i}", [1024, T], BF16) for i in range(2)]
    o1_src = [t_(f"o1_src{i}", [512, T], BF16) for i in range(2)]
    o1_all = [t_(f"o1_all{i}", [1024, T], BF16) for i in range(2)]

    ar_phase(P, a_xT, a_whg, a_wrw, a_w2a2, a_g2, a_gaT, a_lb3, a_hv, a_rv, a_mu8, c_mask, c_i64, c_bones, c_ident,
             c_rm, lambda r0, r1, t: V(o0_src[t // 4], o0_src[t // 4].h[r0:r1, (t % 4) * 512:(t % 4 + 1) * 512]),
             after_tile=lambda t: allgather_pairs(P, o0_src[t // 4], o0_all[t // 4]) if t % 4 == 3 else None)
    fs = DBG.get("f_stop", 0)
    if fs:
        dbg = P.dram("dbg", [2048, S], BF16, "ExternalOutput")
    if fs == 1:
        touch()
        for i in range(2):
            P.dma(V(dbg, dbg.h[0:1024, i * T:(i + 1) * T]), o0_all[i][:])
        P.wait_ticket("sync", dbg.w)
        P.barrier()
        return P.emit(), P
    d = dn[0]
    dense_phase(P, T, d0_xh, None, d["pT"], d["wout"], d["wup"], d["wdown"], d["wple"], d["wgate"], d["gmT"], d["gpT"],
                x1_loc, o_gather=(o0_all, ROWMAP0), sel_d=sel, h_out=h1_src, ga_next_d=ga1T)
    for i in range(2):
        allgather_pairs(P, h1_src[i], h1_all[i])
    if fs == 2:
        touch()
        for i in range(2):
            P.dma(V(dbg, dbg.h[i * 1024:(i + 1) * 1024, 0:T]), h1_all[i][:])
        P.dma(yT[:], x1_loc[:])
        P.wait_ticket("sync", dbg.w)
        P.wait_ticket("sync", yT.w)
        P.barrier()
        return P.emit(), P
    moba_phase(P, None, m_wq, m_wk, m_wv, ga1T, m_qn2, m_kn2, c_cosT, c_sinT, c_perm, c_ident, c_cb, c_esel, c_pb,
               lambda r0, r1, s_: V(o1_src[s_], o1_src[s_].h[r0:r1, :]), h_all=h1_all)
    for i in range(2):
        allgather_pairs(P, o1_src[i], o1_all[i])
    if fs == 3:
        touch()
        for i in range(2):
            P.dma(V(dbg, dbg.h[0:1024, i * T:(i + 1) * T]), o1_all[i][:])
        P.wait_ticket("sync", dbg.w)
        P.barrier()
        return P.emit(), P
    d = dn[1]
    dense_phase(P, T, x1_loc, None, d["pT"], d["wout"], d["wup"], d["wdown"], d["wple"], d["wgate"], d["gmT"], d["gpT"],
                yT, o_gather=(o1_all, ROWMAP1), sel_d=sel)
    P.wait_ticket("sync", yT.w)
    P.barrier()
    return P.emit(), P


def fused_inputs(d, c, xT_b):
    b, r = c // 2, c % 2
    T = S_LEN // 2
    sl = slice(r * T, (r + 1) * T)
    a = ar_inputs(xT_b, r, d)
    m = moba_inputs(None, r, d)
    ins = dict(a_xT=xT_b, a_whg=a["whg"], a_wrw=a["wrw"], a_w2a2=a["w2a2"], a_g2=a["g2"], a_gaT=a["gaT"],
               a_lb3=a["lb3"], a_hv=a["hv"], a_rv=a["rv"], a_mu8=a["mu8"], c_mask=a["mask"], c_i64=a["i64"],
               c_bones=a["bones"], c_ident=a["ident"], c_rm=a["rm"],
               sel=np.ascontiguousarray(np.tile(np.eye(2, dtype=np.float32)[r][None, :], (128, 1))),
               ga1T=vecT(d["attn_norm"][1]), d0_xh=np.ascontiguousarray(xT_b[:, sl]),
               m_wq=m["wq"], m_wk=m["wk"], m_wv=m["wv"], m_qn2=m["qn2"], m_kn2=m["kn2"], c_cosT=m["cosT"],
               c_sinT=m["sinT"], c_perm=m["perm"], c_cb=m["cb"], c_esel=m["esel"], c_pb=m["pb"])
    wouts = [d["w_out_ar"][0], d["w_o_attn"][0]]
    for l in range(2):
        ins.update({f"d{l}_pT": np.ascontiguousarray(d["p"][l, b, sl].T), f"d{l}_wout": wouts[l], f"d{l}_wup": d["w_up"][l],
                    f"d{l}_wdown": d["w_down"][l], f"d{l}_wple": d["ple_proj"][l], f"d{l}_wgate": d["ple_gate"][l],
                    f"d{l}_gmT": vecT(d["mlp_norm"][l]), f"d{l}_gpT": vecT(d["ple_norm"][l])})
    return ins


def kernel(**inputs):
    d = {k: np.asarray(v) for k, v in inputs.items()}
    B = d["x"].shape[0]
    cores = list(range(8))
    xT = [np.ascontiguousarray(d["x"][b].T) for b in range(B)]
    nc, _ = build_fused()
    r = run_bass_kernel_spmd(nc, [fused_inputs(d, c, xT[c // 2]) for c in cores], core_ids=cores)
    out = np.empty(d["x"].shape, np.float32)
    T = S_LEN // 2
    for c in cores:
        out[c // 2, (c % 2) * T:(c % 2 + 1) * T, :] = r.results[c]["yT"].T
    return out
```
